# Optimizing a Trainium2 kernel written in Bass

```python
import math
import jax
import jax.numpy as jnp
from jax import lax
import numpy as np

D_MODEL = 2048
BATCH = 4
SEQ = 8192
DEPTH = 1

A_HEADS = 8
A_QK_DIM = 64
A_V_DIM = 2 * A_QK_DIM
B_HEADS = 8
B_GROUPS = 2
B_HPG = B_HEADS // B_GROUPS
B_HEAD_DIM = 128
CMP_LEN = 32
CMP_STRIDE = 16
CMP_HIDDEN = 256
SLC_BLOCK = 64
SLC_TOPK = 16
WINDOW = 512
FORCE_SCORE = 1.0e4
ROPE_THETA = 500000.0
ROPE_FRACTION = 4
Q_BLOCK = 128
NSA_Q_BLOCK = 64
EPS = 1e-6
NEG = -1e30

A_Q_W = A_HEADS * 2 * A_QK_DIM
A_V_W = A_HEADS * A_V_DIM
B_Q_W = B_HEADS * B_HEAD_DIM
B_KV_W = B_GROUPS * B_HEAD_DIM
B_GATE_W = B_HEADS * 3
MERGE_W = 2 * D_MODEL
SPLIT_WIDTHS = (A_Q_W, A_Q_W, A_V_W, A_V_W,
                B_Q_W, B_KV_W, B_KV_W, B_KV_W, B_KV_W, B_KV_W, B_KV_W, B_Q_W, B_GATE_W,
                MERGE_W)
IN_WIDTH = sum(SPLIT_WIDTHS)
MIX_WIDTH = A_V_W + B_Q_W

kernel_name = "hybrid_diffattn_nsa_gated_block"


def rms_norm(x, g):
    xf = x.astype(jnp.float32)
    y = xf * lax.rsqrt(jnp.mean(xf * xf, axis=-1, keepdims=True) + EPS)
    return (y * g.astype(jnp.float32)).astype(x.dtype)


def rope_partial(x, pos):
    d = x.shape[-1]
    rd = d // ROPE_FRACTION
    half = rd // 2
    inv = 1.0 / (ROPE_THETA ** (jnp.arange(half, dtype=jnp.float32) * (2.0 / rd)))
    ang = pos.astype(jnp.float32)[:, None] * inv[None, :]
    cos = jnp.cos(ang)[:, None, :]
    sin = jnp.sin(ang)[:, None, :]
    x1 = x[..., :half].astype(jnp.float32)
    x2 = x[..., half:rd].astype(jnp.float32)
    rot = jnp.concatenate([x1 * cos - x2 * sin, x2 * cos + x1 * sin], axis=-1).astype(x.dtype)
    return jnp.concatenate([rot, x[..., rd:]], axis=-1)


def masked_softmax(s, mask):
    s = jnp.where(mask, s.astype(jnp.float32), NEG)
    p = jax.nn.softmax(s, axis=-1)
    return jnp.where(mask, p, 0.0)


def split_cols(p):
    outs = []
    off = 0
    for w in SPLIT_WIDTHS:
        outs.append(p[..., off:off + w])
        off += w
    return outs


def diff_attention(q, k, v, lam, lam_init, sub_g):
    B, S, H, _, dqk = q.shape
    dv = v.shape[-1]
    scale = dqk ** -0.5
    kpos = jnp.arange(S)

    def block(i):
        s0 = i * Q_BLOCK
        qb = lax.dynamic_slice_in_dim(q, s0, Q_BLOCK, axis=1)
        s = jnp.einsum('bqhmd,bkhmd->bhmqk', qb, k).astype(jnp.float32) * scale
        qpos = s0 + jnp.arange(Q_BLOCK)
        mask = kpos[None, :] <= qpos[:, None]
        p = jax.nn.softmax(jnp.where(mask, s, NEG), axis=-1)
        a = p[:, :, 0] - lam * p[:, :, 1]
        return jnp.einsum('bhqk,bkhd->bqhd', a.astype(v.dtype), v)

    o = lax.map(block, jnp.arange(S // Q_BLOCK))
    o = o.transpose(1, 0, 2, 3, 4).reshape(B, S, H, dv)
    return rms_norm(o, sub_g) * (1.0 - lam_init)


def compress_blocks(x, pe, w1, w2):
    B, S, G, d = x.shape
    n_cmp = (S - CMP_LEN) // CMP_STRIDE + 1
    idx = jnp.arange(n_cmp)[:, None] * CMP_STRIDE + jnp.arange(CMP_LEN)[None, :]
    blk = x[:, idx] + pe[None, None, :, None, :]
    flat = blk.transpose(0, 1, 3, 2, 4).reshape(B, n_cmp, G, CMP_LEN * d)
    return jax.nn.silu(flat @ w1) @ w2


def nsa_attention(q, kc_raw, vc_raw, ks, vs, kw, vw, gates, pe_k, pe_v, w1k, w2k, w1v, w2v):
    B, S, H, d = q.shape
    G = B_GROUPS
    scale = d ** -0.5
    dt = q.dtype
    n_cmp = (S - CMP_LEN) // CMP_STRIDE + 1
    cmp_start = jnp.arange(n_cmp) * CMP_STRIDE
    cmp_end = cmp_start + CMP_LEN - 1
    kc = rope_partial(compress_blocks(kc_raw, pe_k, w1k, w2k), cmp_end)
    vc = compress_blocks(vc_raw, pe_v, w1v, w2v)
    n_slc = S // SLC_BLOCK
    n_top = min(SLC_TOPK, n_slc)
    ks_blk = ks.reshape(B, n_slc, SLC_BLOCK, G, d).transpose(0, 3, 1, 2, 4)
    vs_blk = vs.reshape(B, n_slc, SLC_BLOCK, G, d).transpose(0, 3, 1, 2, 4)
    slc_start = jnp.arange(n_slc) * SLC_BLOCK
    overlap = ((cmp_start[:, None] < slc_start[None, :] + SLC_BLOCK) &
               (cmp_start[:, None] + CMP_LEN > slc_start[None, :])).astype(jnp.float32)
    jb = jnp.arange(n_slc)
    b_ix = jnp.arange(B)[:, None, None, None]
    g_ix = jnp.arange(G)[None, :, None, None]
    kw_pad = jnp.pad(kw, ((0, 0), (WINDOW, 0), (0, 0), (0, 0)))
    vw_pad = jnp.pad(vw, ((0, 0), (WINDOW, 0), (0, 0), (0, 0)))
    qg = q.reshape(B, S, G, B_HPG, d)
    gg = gates.reshape(B, S, G, B_HPG, 3)
    QB = NSA_Q_BLOCK

    def block(i):
        s0 = i * QB
        qpos = s0 + jnp.arange(QB)
        qb = lax.dynamic_slice_in_dim(qg, s0, QB, axis=1)
        sc = jnp.einsum('bqghd,bcgd->bghqc', qb, kc) * scale
        pc = masked_softmax(sc, cmp_end[None, :] <= qpos[:, None])
        o_cmp = jnp.einsum('bghqc,bcgd->bqghd', pc.astype(dt), vc)
        imp = jnp.einsum('bghqc,cn->bgqn', pc, overlap)
        cur = qpos // SLC_BLOCK
        forced = (jb[None, :] == 0) | (jb[None, :] == cur[:, None]) | (jb[None, :] == cur[:, None] - 1)
        valid = slc_start[None, :] <= qpos[:, None]
        imp = jnp.where(forced & valid, FORCE_SCORE, imp)
        imp = jnp.where(valid, imp, NEG)
        top_val, top_idx = lax.top_k(imp, n_top)
        sel_ok = top_val > NEG * 0.5
        kg = ks_blk[b_ix, g_ix, top_idx]
        vg = vs_blk[b_ix, g_ix, top_idx].reshape(B, G, QB, n_top * SLC_BLOCK, d)
        ss = jnp.einsum('bqghd,bgqnkd->bghqnk', qb, kg) * scale
        ss = ss.reshape(B, G, B_HPG, QB, n_top * SLC_BLOCK)
        kpos = top_idx[..., None] * SLC_BLOCK + jnp.arange(SLC_BLOCK)
        smask = (kpos <= qpos[None, None, :, None, None]) & sel_ok[..., None]
        ps = masked_softmax(ss, smask.reshape(B, G, 1, QB, n_top * SLC_BLOCK))
        o_slc = jnp.einsum('bghqk,bgqkd->bqghd', ps.astype(dt), vg)
        kwb = lax.dynamic_slice_in_dim(kw_pad, s0, QB + WINDOW, axis=1)
        vwb = lax.dynamic_slice_in_dim(vw_pad, s0, QB + WINDOW, axis=1)
        wpos = s0 - WINDOW + jnp.arange(QB + WINDOW)
        wmask = ((wpos[None, :] <= qpos[:, None]) & (wpos[None, :] > qpos[:, None] - WINDOW)
                 & (wpos[None, :] >= 0))
        sw = jnp.einsum('bqghd,bkgd->bghqk', qb, kwb) * scale
        pw = masked_softmax(sw, wmask)
        o_win = jnp.einsum('bghqk,bkgd->bqghd', pw.astype(dt), vwb)
        gb = lax.dynamic_slice_in_dim(gg, s0, QB, axis=1)
        return gb[..., 0:1] * o_cmp + gb[..., 1:2] * o_slc + gb[..., 2:3] * o_win

    o = lax.map(block, jnp.arange(S // QB))
    return o.transpose(1, 0, 2, 3, 4, 5).reshape(B, S, H, d)


def setup_inputs(seed: int = 0) -> dict:
    key = jax.random.key(seed)
    ks = jax.random.split(key, 22)
    L, D = DEPTH, D_MODEL

    def nrm(k, shape, scale):
        return jax.random.normal(k, shape, jnp.float32) * scale

    return {
        "x": nrm(ks[0], (BATCH, SEQ, D), 1.0),
        "c": nrm(ks[1], (BATCH, D), 1.0),
        "w_ada": nrm(ks[2], (L, D, 3 * D), 0.3 * D ** -0.5),
        "b_ada": nrm(ks[3], (L, 3 * D), 0.01),
        "norm_g": 1.0 + nrm(ks[4], (L, D), 0.02),
        "w_in": nrm(ks[5], (L, D, IN_WIDTH), D ** -0.5),
        "lambda_q1": nrm(ks[6], (L, A_QK_DIM), 0.1),
        "lambda_k1": nrm(ks[7], (L, A_QK_DIM), 0.1),
        "lambda_q2": nrm(ks[8], (L, A_QK_DIM), 0.1),
        "lambda_k2": nrm(ks[9], (L, A_QK_DIM), 0.1),
        "diff_norm_g": 1.0 + nrm(ks[10], (L, A_V_DIM), 0.02),
        "cmp_pe_k": nrm(ks[11], (L, CMP_LEN, B_HEAD_DIM), 0.1),
        "cmp_pe_v": nrm(ks[12], (L, CMP_LEN, B_HEAD_DIM), 0.1),
        "cmp_w1_k": nrm(ks[13], (L, CMP_LEN * B_HEAD_DIM, CMP_HIDDEN), (CMP_LEN * B_HEAD_DIM) ** -0.5),
        "cmp_w2_k": nrm(ks[14], (L, CMP_HIDDEN, B_HEAD_DIM), CMP_HIDDEN ** -0.5),
        "cmp_w1_v": nrm(ks[15], (L, CMP_LEN * B_HEAD_DIM, CMP_HIDDEN), (CMP_LEN * B_HEAD_DIM) ** -0.5),
        "cmp_w2_v": nrm(ks[16], (L, CMP_HIDDEN, B_HEAD_DIM), CMP_HIDDEN ** -0.5),
        "w_branch": nrm(ks[17], (L, MIX_WIDTH, D), A_V_W ** -0.5),
        "w_out": nrm(ks[18], (L, D, D), D ** -0.5),
        "final_norm_g": 1.0 + nrm(ks[19], (D,), 0.02),
    }


def reference(x, c, w_ada, b_ada, norm_g, w_in, lambda_q1, lambda_k1, lambda_q2, lambda_k2,
              diff_norm_g, cmp_pe_k, cmp_pe_v, cmp_w1_k, cmp_w2_k, cmp_w1_v, cmp_w2_v,
              w_branch, w_out, final_norm_g):
    B, S, D = x.shape
    pos = jnp.arange(S)
    for l in range(DEPTH):
        mod = jax.nn.silu(c) @ w_ada[l] + b_ada[l]
        shift, scale, gate = jnp.split(mod, 3, axis=-1)
        h = rms_norm(x, norm_g[l]) * (1.0 + scale[:, None, :]) + shift[:, None, :]
        (aq, ak, av, az, bq, bkc, bvc, bks, bvs, bkw, bvw, bz, bgate, mgate) = split_cols(h @ w_in[l])
        aq = rope_partial(aq.reshape(B, S, A_HEADS * 2, A_QK_DIM), pos).reshape(B, S, A_HEADS, 2, A_QK_DIM)
        ak = rope_partial(ak.reshape(B, S, A_HEADS * 2, A_QK_DIM), pos).reshape(B, S, A_HEADS, 2, A_QK_DIM)
        av = av.reshape(B, S, A_HEADS, A_V_DIM)
        lam_init = 0.8 - 0.6 * math.exp(-0.3 * l)
        lam = (jnp.exp(jnp.sum(lambda_q1[l].astype(jnp.float32) * lambda_k1[l].astype(jnp.float32)))
               - jnp.exp(jnp.sum(lambda_q2[l].astype(jnp.float32) * lambda_k2[l].astype(jnp.float32)))
               + lam_init)
        oa = diff_attention(aq, ak, av, lam, lam_init, diff_norm_g[l])
        ya = (oa.reshape(B, S, A_V_W) * jax.nn.silu(az)) @ w_branch[l, :A_V_W]
        bq = rope_partial(bq.reshape(B, S, B_HEADS, B_HEAD_DIM), pos)
        kv_shape = (B, S, B_GROUPS, B_HEAD_DIM)
        bks = rope_partial(bks.reshape(kv_shape), pos)
        bkw = rope_partial(bkw.reshape(kv_shape), pos)
        bg = jax.nn.sigmoid(bgate).reshape(B, S, B_HEADS, 3)
        ob = nsa_attention(bq, bkc.reshape(kv_shape), bvc.reshape(kv_shape), bks, bvs.reshape(kv_shape),
                           bkw, bvw.reshape(kv_shape), bg, cmp_pe_k[l], cmp_pe_v[l],
                           cmp_w1_k[l], cmp_w2_k[l], cmp_w1_v[l], cmp_w2_v[l])
        yb = (ob.reshape(B, S, B_Q_W) * jax.nn.silu(bz)) @ w_branch[l, A_V_W:]
        ga, gb = jnp.split(jax.nn.sigmoid(mgate), 2, axis=-1)
        y = (ga * ya + gb * yb) @ w_out[l]
        x = x + gate[:, None, :] * y
    return rms_norm(x, final_norm_g)
```

```python
import math
from contextlib import ExitStack

import numpy as np
import concourse.bass as bass
import concourse.mybir as mybir
from concourse.bass_utils import run_bass_kernel_spmd

F32 = mybir.dt.float32
BF16 = mybir.dt.bfloat16
AF = mybir.ActivationFunctionType
ALU = mybir.AluOpType

D = 2048
KC = 16
IN_W = 11800
EPS = 1e-6
THETA = 500000.0
NEGB = -30000.0
PI = math.pi

OFF = dict(aq=0, ak=1024, av=2048, az=3072, bq=4096, bkc=5120, bvc=5376, bks=5632, bvs=5888,
           bkw=6144, bvw=6400, bz=6656, bgate=7680, mgate=7704)


class Sem:
    __slots__ = ("h", "issued", "bg", "kind")

    def __init__(self, h):
        self.h = h
        self.issued = 0
        self.bg = False
        self.kind = None


class Res:
    __slots__ = ("name", "w", "r", "dsem")
    registry = []

    def __init__(self, name):
        self.name = name
        self.w = None
        self.r = []
        self.dsem = None
        Res.registry.append(self)


class Sched:
    ENG = ("pe", "act", "dve", "pool", "sp")

    def __init__(self, nc, sem_handles):
        self.nc = nc
        self.free = [Sem(h) for h in sem_handles]
        self.esem = {e: self.free.pop() for e in self.ENG}
        self.pend = {e: False for e in self.ENG}
        self.q = {e: [] for e in self.ENG}
        self.waited = {e: {} for e in self.ENG}
        self.dsems = []
        self.free_kind = {"sw": [], "hw": []}
        self.nops = {e: 0 for e in self.ENG}

    def new_sem(self, kind):
        pool = self.free_kind[kind]
        if pool:
            s = pool.pop()
        else:
            s = self.free.pop()
            s.kind = kind
        if s not in self.dsems:
            self.dsems.append(s)
        return s

    def recycle(self, res_list):
        for r in res_list:
            if r.dsem is not None:
                self.free_kind[r.dsem.kind].append(r.dsem)
                r.dsem = None

    def _plan_waits(self, eng, deps):
        best = {}
        for sem, val in deps:
            if sem is self.esem[eng] and eng == "pe":
                continue
            if best.get(sem, 0) < val:
                best[sem] = val
        waits = []
        wd = self.waited[eng]
        for sem, val in best.items():
            if wd.get(sem, 0) >= val:
                continue
            assert val <= sem.issued, f"wait on unsignaled token eng={eng}"
            wd[sem] = val
            waits.append((sem.h, val))
        return waits

    def _deps(self, reads, writes):
        deps = []
        for r in reads:
            if r.w is not None:
                deps.append(r.w)
        for w in writes:
            if w.w is not None:
                deps.append(w.w)
            deps.extend(w.r)
        return deps

    def op(self, eng, fn, reads=(), writes=(), signal=True):
        waits = self._plan_waits(eng, self._deps(reads, writes))
        sem = self.esem[eng]
        tok = (sem, sem.issued + 1)
        if signal:
            sem.issued += 1
            self.pend[eng] = False
        else:
            assert eng == "pe"
            self.pend[eng] = True
        for r in reads:
            r.r.append(tok)
        for w in writes:
            w.w = tok
            w.r = []
        self.q[eng].append((waits, fn, sem.h if signal else None, 1))
        self.nops[eng] += 1

    def dma(self, queue, fn, reads=(), writes=(), owner=None, serialize=True, bg=False):
        kind = "sw" if queue == "pool" else "hw"
        if owner.dsem is None:
            owner.dsem = self.new_sem(kind)
            owner.dsem.bg = bg
        assert owner.dsem.kind == kind, f"semaphore of {owner.name} used from both DMA queue kinds"
        sem = owner.dsem
        deps = self._deps(reads, writes)
        if serialize and sem.issued > 0:
            deps.append((sem, sem.issued))
        waits = self._plan_waits(queue, deps)
        sem.issued += 16
        tok = (sem, sem.issued)
        for r in reads:
            r.r.append(tok)
        for w in writes:
            w.w = tok
            w.r = []
        self.q[queue].append((waits, fn, sem.h, 16))

    def barrier(self):
        for e in self.ENG:
            assert not self.pend[e], f"barrier with unsignaled ops on {e}"
        toks = [(self.esem[e], self.esem[e].issued) for e in self.ENG if self.esem[e].issued > 0]
        toks += [(s, s.issued) for s in self.dsems if s.issued > 0 and not s.bg]
        for e in self.ENG:
            waits = self._plan_waits(e, [t for t in toks if t[0] is not self.esem[e]])
            if waits:
                self.q[e].append((waits, None, None, 0))

    def final_wait(self, queue):
        toks = [(s, s.issued) for s in self.dsems if s.issued > 0]
        toks += [(self.esem[e], self.esem[e].issued) for e in self.ENG
                 if e != queue and self.esem[e].issued > 0]
        waits = self._plan_waits(queue, toks)
        self.q[queue].append((waits, None, None, 0))

    def replay(self, eng):
        items = self.q[eng]

        def f(e):
            for waits, fn, sem, inc in items:
                for (h, v) in waits:
                    e.wait_ge(h, v)
                if fn is None:
                    continue
                ins = fn(e)
                if sem is not None:
                    ins.then_inc(sem, inc)
        return f


class Arena:
    def __init__(self, tensor, nbytes):
        self.t = tensor
        self.n = nbytes
        self.top = 0

    def mark(self):
        return self.top

    def release(self, m):
        self.top = m

    def alloc(self, dtype, *shape):
        esz = 4 if dtype == F32 else 2
        n = 1
        for s in shape:
            n *= s
        nb = (n * esz + 31) // 32 * 32
        off = self.top
        self.top += nb
        assert self.top <= self.n, f"SBUF arena overflow {self.top} > {self.n}"
        ap = self.t[:, off // 2: off // 2 + (n * esz) // 2]
        if dtype == F32:
            ap = ap.bitcast(F32)
        if len(shape) == 2:
            ap = ap.rearrange("p (a b) -> p a b", b=shape[1])
        elif len(shape) == 3:
            ap = ap.rearrange("p (a b c) -> p a b c", b=shape[1], c=shape[2])
        elif len(shape) == 4:
            ap = ap.rearrange("p (a b c d) -> p a b c d", b=shape[1], c=shape[2], d=shape[3])
        return ap


def strided(ap2d, start, step, count):
    pst = ap2d.ap[0]
    est = ap2d.ap[-1][0]
    return bass.AP(ap2d.tensor, ap2d.offset + start * est, (tuple(pst), (step * est, count)))


def build_program(S, debug=None):
    NT = S // 128
    NO = NT // 2
    NCMP = (S - 32) // 16 + 1
    NCT = (NCMP + 127) // 128
    KW = 4 * NO - 4 + 128

    nc = bass.Bass("TRN2", target_bir_lowering=False)

    def din(name, shape, dt=F32):
        return nc.dram_tensor(name, list(shape), dt, kind="ExternalInput").ap()

    def dscr(name, shape, dt):
        kind = "ExternalOutput" if (debug and name in debug) else "Internal"
        return nc.dram_tensor(name, list(shape), dt, kind=kind).ap()

    x_all = din("x_all", [S, D])
    x_own = din("x_own", [S // 2, D])
    cT = din("cT", [128, KC])
    w_ada = din("w_ada", [D, 3 * D])
    b_ada = din("b_ada", [1, 3 * D])
    norm_g = din("norm_g", [1, D])
    w_in = din("w_in", [D, IN_W])
    lam4 = din("lam4", [4, 64])
    dng = din("diff_norm_g", [1, 128])
    peT = din("peT", [128, 2, 32])
    w1 = din("cmp_w1", [2, 4096, 256])
    w2 = din("cmp_w2", [2, 256, 128])
    w_br = din("w_branch", [D, D])
    w_out = din("w_out", [D, D])
    fng = din("final_norm_g", [1, D])
    c_ident = din("c_ident", [128, 128])
    c_E = din("c_E", [128, S])
    c_ovl = din("c_ovl", [128, NCT, 128])
    c_L = din("c_L", [128, 128])
    c_keep = din("c_keep", [128, KW])
    c_add = din("c_add", [128, KW])
    c_bias = din("c_bias", [128, 8, 512])
    c_pos_all = din("c_pos_all", [128, NT])
    c_pos_own = din("c_pos_own", [128, NO])
    c_pos_cmp = din("c_pos_cmp", [128, NCT])
    c_p128 = din("c_p128", [128, 1])

    out = nc.dram_tensor("out", [S // 2, D], F32, kind="ExternalOutput").ap()

    hT_d = dscr("hT_d", [NT, 128, D], BF16)
    hTo_d = dscr("hTo_d", [NO, 128, D], BF16)
    win_bf = dscr("win_bf", [D, IN_W], BF16)
    wbr_bf = dscr("wbr_bf", [D, D], BF16)
    wout_bf = dscr("wout_bf", [D, D], BF16)
    w1_bf = dscr("w1_bf", [2, 4096, 256], BF16)
    w2_bf = dscr("w2_bf", [2, 256, 128], BF16)
    oa_d = dscr("oa_d", [NO, 128, 1024], F32)
    ob_d = dscr("ob_d", [NO, 128, 1024], F32)
    G_d = dscr("G_d", [128, D], F32)

    ARENA_BYTES = 204800
    es = ExitStack()
    arena_t = es.enter_context(nc.sbuf_tensor("arena", [128, ARENA_BYTES // 2], BF16))
    banks = [es.enter_context(nc.psum_tensor(f"bank{k}", [128, 512], F32)) for k in range(8)]
    sem_handles = [es.enter_context(nc.semaphore(f"s{k}")) for k in range(100)]
    sc = Sched(nc, sem_handles)
    ar = Arena(arena_t, ARENA_BYTES)

    ps = [b[:] for b in banks]
    psb = [b[:].bitcast(BF16) for b in banks]
    PB = [Res(f"bank{k}") for k in range(8)]

    def mm(out_, lhsT, rhs, start, stop, reads, writes, signal=True):
        sc.op("pe", lambda e: e.matmul(out_, lhsT, rhs, start=start, stop=stop,
                                        skip_group_check=True), reads, writes, signal)

    def tr(out_, in_, ident, reads, writes, signal=True):
        sc.op("pe", lambda e: e.transpose(out_, in_, ident), reads, writes, signal)

    def act(out_, in_, func, reads, writes, bias=None, scale=None, accum_out=None):
        kw = {}
        if bias is not None:
            kw["bias"] = bias
        if scale is not None:
            kw["scale"] = scale
        if accum_out is not None:
            kw["accum_out"] = accum_out
        sc.op("act", lambda e: e.activation(out_, in_, func, **kw), reads, writes)

    def ts(eng, out_, in0, s1, s2, op0, op1, reads, writes):
        if op1 is None:
            sc.op(eng, lambda e: e.tensor_scalar(out_, in0, s1, None, op0), reads, writes)
        else:
            sc.op(eng, lambda e: e.tensor_scalar(out_, in0, s1, s2, op0, op1), reads, writes)

    def tt(eng, out_, in0, in1, op, reads, writes):
        sc.op(eng, lambda e: e.tensor_tensor(out_, in0, in1, op), reads, writes)

    def stt(out_, in0, scalar, in1, op0, op1, reads, writes):
        sc.op("dve", lambda e: e.scalar_tensor_tensor(out_, in0, scalar, in1, op0, op1), reads, writes)

    def cp(eng, out_, in_, reads, writes):
        if eng == "act":
            sc.op("act", lambda e: e.copy(out_, in_), reads, writes)
        else:
            sc.op(eng, lambda e: e.tensor_copy(out_, in_), reads, writes)

    def rsum(out_, in_, reads, writes):
        sc.op("dve", lambda e: e.reduce_sum(out_, in_, axis=mybir.AxisListType.X), reads, writes)

    def memset(eng, ap, val, writes):
        sc.op(eng, lambda e: e.memset(ap, val), (), writes)

    def dma(queue, out_, in_, reads, writes, owner, serialize=True, bg=False):
        sc.dma(queue, lambda e: e.dma_start(out=out_, in_=in_), reads, writes, owner, serialize, bg)

    R_const = Res("const")
    ident = ar.alloc(BF16, 128)
    biasT = ar.alloc(BF16, 8, 512)
    Lt = ar.alloc(F32, 128)
    keepT = ar.alloc(F32, KW)
    addT = ar.alloc(F32, KW)
    ovl = ar.alloc(BF16, NCT, 128)
    p128 = ar.alloc(F32, 1)
    pos_all = ar.alloc(F32, NT)
    pos_own = ar.alloc(F32, NO)
    pos_cmp = ar.alloc(F32, NCT)
    gsub = ar.alloc(F32, 128)
    lamv = ar.alloc(F32, 4, 64)
    ones_f = ar.alloc(F32, 128)
    small = ar.alloc(F32, 16)

    R_constp = Res("constp")

    rope_lo = ar.mark()
    cosA = ar.alloc(F32, NT, 8)
    sinA = ar.alloc(F32, NT, 8)
    cosAo = ar.alloc(F32, NO, 8)
    sinAo = ar.alloc(F32, NO, 8)
    cosB = ar.alloc(F32, NT, 16)
    sinB = ar.alloc(F32, NT, 16)
    cosBo = ar.alloc(F32, NO, 16)
    sinBo = ar.alloc(F32, NO, 16)
    cosC = ar.alloc(F32, NCT, 16)
    sinC = ar.alloc(F32, NCT, 16)
    rope_hi = ar.mark()

    def cdma(queue, out_, in_):
        rr = R_const if queue == "sp" else R_constp
        dma(queue, out_, in_, (), (rr,), rr, serialize=False)

    cdma("pool", ident, c_ident)
    cdma("pool", biasT, c_bias)
    cdma("pool", ovl, c_ovl)
    cdma("sp", Lt, c_L)
    cdma("sp", keepT, c_keep)
    cdma("sp", addT, c_add)
    cdma("sp", p128, c_p128)
    cdma("sp", pos_all, c_pos_all)
    cdma("sp", pos_own, c_pos_own)
    cdma("sp", pos_cmp, c_pos_cmp)
    cdma("sp", gsub, dng.broadcast_to([128, 128]))
    for k in range(4):
        cdma("sp", lamv[:, k, :], lam4[k:k + 1, :].broadcast_to([128, 64]))
    sc.barrier()

    cast_q = []

    def wcast_cols(res, c0, c1, grp):
        for r in range(4):
            cast_q.append((grp, (lambda r=r, c0=c0, c1=c1, res=res: dma(
                "pool", win_bf[r * 512:(r + 1) * 512, c0:c1], w_in[r * 512:(r + 1) * 512, c0:c1],
                (), (res,), res, serialize=False, bg=True))))

    R_wcA = [Res(f"wcA{u}") for u in range(4)]
    R_wcB = Res("wcB")
    R_wcC = Res("wcC")
    R_wcT = Res("wcT")
    for u in range(4):
        for nm in ("ak", "av", "aq"):
            wcast_cols(R_wcA[u], OFF[nm] + 256 * u, OFF[nm] + 256 * u + 256, f"A{u}")
    wcast_cols(R_wcB, OFF["bq"], OFF["bz"], "B")
    wcast_cols(R_wcB, OFF["bgate"], OFF["mgate"], "B")
    for k in range(2):
        for r in range(4):
            cast_q.append(("B", (lambda k=k, r=r: dma(
                "pool", w1_bf[k, r * 1024:(r + 1) * 1024, :], w1[k, r * 1024:(r + 1) * 1024, :],
                (), (R_wcC,), R_wcC, serialize=False, bg=True))))
        cast_q.append(("B", (lambda k=k: dma("pool", w2_bf[k], w2[k], (), (R_wcC,), R_wcC,
                                               serialize=False, bg=True))))
    wcast_cols(R_wcT, OFF["az"], OFF["bq"], "T")
    wcast_cols(R_wcT, OFF["bz"], OFF["bgate"], "T")
    for q4 in range(4):
        wcast_cols(R_wcT, OFF["mgate"] + 1024 * q4, OFF["mgate"] + 1024 * q4 + 1024, "T")
    for r in range(4):
        cast_q.append(("T", (lambda r=r: dma("pool", wbr_bf[r * 512:(r + 1) * 512, :],
                                               w_br[r * 512:(r + 1) * 512, :], (), (R_wcT,), R_wcT,
                                               serialize=False, bg=True))))
        cast_q.append(("T", (lambda r=r: dma("pool", wout_bf[r * 512:(r + 1) * 512, :],
                                               w_out[r * 512:(r + 1) * 512, :], (), (R_wcT,), R_wcT,
                                               serialize=False, bg=True))))

    def drip(n):
        for _ in range(n):
            if cast_q:
                cast_q.pop(0)[1]()

    def flush_casts(grp):
        last = -1
        for k, (g_, _) in enumerate(cast_q):
            if g_ == grp:
                last = k
        drip(last + 1)

    flush_casts("A0")

    R_misc = Res("misc")
    memset("dve", ones_f, 1.0, (R_misc,))
    ts("dve", gsub, gsub, 0.8, None, ALU.mult, None, (R_const,), (R_misc,))
    junk64 = ar.alloc(F32, 64)
    for k in range(2):
        tt("dve", junk64, lamv[:, 2 * k, :], lamv[:, 2 * k + 1, :], ALU.mult, (R_const,), (R_misc,))
        rsum(small[:, k:k + 1], junk64, (R_misc,), (R_misc,))
    act(small[:, 2:4], small[:, 0:2], AF.Exp, (R_misc,), (R_misc,))
    tt("dve", small[:, 4:5], small[:, 2:3], small[:, 3:4], ALU.subtract, (R_misc,), (R_misc,))
    ts("dve", small[:, 5:6], small[:, 4:5], 0.2, None, ALU.add, None, (R_misc,), (R_misc,))
    ts("dve", small[:, 6:7], small[:, 5:6], -1.0, None, ALU.mult, None, (R_misc,), (R_misc,))
    neglam = small[:, 6:7]

    C1 = 6.28125
    C2 = 2 * PI - C1

    R_rope = Res("rope")

    def rope_gen(tmpl):
        rw = (R_rope,)
        tables = ((pos_all, NT, 8, 16, cosA, sinA), (pos_own, NO, 8, 16, cosAo, sinAo),
                  (pos_all, NT, 16, 32, cosB, sinB), (pos_own, NO, 16, 32, cosBo, sinBo),
                  (pos_cmp, NCT, 16, 32, cosC, sinC))
        for (pos, n, half, rd, cosT, sinT) in tables:
            ang, a, kf, r, msk = [t[:, 0:n * half].rearrange("p (n h) -> p n h", h=half) for t in tmpl]
            ki = a.bitcast(mybir.dt.int32)
            for j in range(half):
                inv = float(np.float32(1.0) / np.float32(THETA) ** (np.float32(j) * np.float32(2.0 / rd)))
                ts("dve", ang[:, :, j], pos, inv, None, ALU.mult, None, (R_const,) + rw, rw)
                yield
            for (dst, shift) in ((sinT, 0.0), (cosT, PI / 2)):
                ts("dve", r, ang, shift, None, ALU.add, None, rw, rw)
                yield
                ts("dve", kf, r, 1.0 / (2 * PI), None, ALU.mult, None, rw, rw)
                yield
                cp("dve", ki, kf, rw, rw)
                yield
                cp("dve", kf, ki, rw, rw)
                yield
                stt(r, kf, -C1, r, ALU.mult, ALU.add, rw, rw)
                yield
                stt(r, kf, -C2, r, ALU.mult, ALU.add, rw, rw)
                yield
                ts("dve", msk, r, PI, None, ALU.is_gt, None, rw, rw)
                yield
                stt(r, msk, -2 * PI, r, ALU.mult, ALU.add, rw, rw)
                yield
                ts("dve", msk, r, -PI, None, ALU.is_lt, None, rw, rw)
                yield
                stt(r, msk, 2 * PI, r, ALU.mult, ALU.add, rw, rw)
                yield
                ts("dve", r, r, PI, -PI, ALU.min, ALU.max, rw, rw)
                yield
                act(dst, r, AF.Sin, rw, rw)
                yield

    C1 = 6.28125
    C2 = 2 * PI - C1

    epsc = ar.alloc(F32, 1)
    memset("dve", epsc, EPS, (R_misc,))

    pm = ar.mark()
    A_bc = ar.alloc(F32, D)
    B_bc = ar.alloc(F32, D)
    pm2 = ar.mark()
    G_bc = ar.alloc(F32, D)
    R_G = Res("G")
    ng_bc = ar.alloc(F32, D)
    cts = ar.alloc(F32, KC)
    scv = ar.alloc(F32, KC)
    sc_rep = ar.alloc(F32, KC, 128)
    brow = ar.alloc(F32, 3 * D)
    wada = [ar.alloc(F32, KC, 512) for _ in range(2)]
    R_wada = [Res("wada0"), Res("wada1")]
    R_ada = Res("ada")
    dma("sp", cts, cT, (), (R_ada,), R_ada)
    dma("sp", ng_bc, norm_g.broadcast_to([128, D]), (), (R_ada,), R_ada)
    dma("sp", brow[0:1, :], b_ada, (), (R_ada,), R_ada)
    act(scv, cts, AF.Silu, (R_ada,), (R_ada,))
    cp("dve", sc_rep, scv.unsqueeze(2).broadcast_to([128, KC, 128]), (R_ada,), (R_ada,))
    w_ada_v = w_ada.rearrange("(kc p) c -> p kc c", p=128)
    for cb in range(12):
        sl = cb % 2
        dma("sp", wada[sl], w_ada_v[:, :, cb * 512:(cb + 1) * 512], (), (R_wada[sl],), R_wada[sl])
        bk = cb % 2
        for kc in range(KC):
            mm(ps[bk], sc_rep[:, kc, :], wada[sl][:, kc, :], kc == 0, False,
               (R_ada, R_wada[sl]), (PB[bk],), signal=False)
        mm(ps[bk], ones_f[0:1, :], brow[0:1, cb * 512:(cb + 1) * 512], False, True,
           (R_ada, R_misc), (PB[bk],))
        c0 = (cb % 4) * 512
        if cb < 4:
            cp("act", B_bc[:, c0:c0 + 512], ps[bk], (PB[bk],), (PB[bk], R_ada))
        elif cb < 8:
            stt(A_bc[:, c0:c0 + 512], ps[bk], 1.0, ng_bc[:, c0:c0 + 512], ALU.add, ALU.mult,
                (PB[bk], R_ada), (PB[bk], R_ada))
        else:
            cp("act", G_bc[:, c0:c0 + 512], ps[bk], (PB[bk],), (PB[bk], R_G))
    dma("sp", G_d, G_bc, (R_G,), (R_G,), R_G)

    sc.barrier()
    ar.release(pm2)
    NX = 3
    xbuf = [ar.alloc(F32, D) for _ in range(NX)]
    R_x = [Res(f"x{k}") for k in range(NX)]
    junkb = ar.alloc(BF16, D)
    R_junk = Res("junk")
    tmpf = [ar.alloc(F32, D) for _ in range(2)]
    R_tmp = [Res("tmp0"), Res("tmp1")]
    hb = [ar.alloc(BF16, D) for _ in range(2)]
    R_hb = [Res("hb0"), Res("hb1")]
    R_hb2 = [Res("hb0b"), Res("hb1b")]
    hTs = [ar.alloc(BF16, D) for _ in range(2)]
    R_hTs = [Res("hTs0"), Res("hTs1")]
    ssv = [ar.alloc(F32, 2) for _ in range(NX)]
    R_hTd = Res("hT_d")

    hitems = [(x_all, t, hT_d) for t in range(NT)] + [(x_own, t, hTo_d) for t in range(NO)]

    def h_load(n):
        xsrc, t, dst = hitems[n]
        k3 = n % NX
        dma("sp", xbuf[k3], xsrc[t * 128:(t + 1) * 128, :], (), (R_x[k3],), R_x[k3])

    def h_front(n):
        k3, k2 = n % NX, n % 2
        act(junkb, xbuf[k3], AF.Square, (R_x[k3],), (R_junk, R_x[k3]), accum_out=ssv[k3][:, 0:1])
        act(ssv[k3][:, 1:2], ssv[k3][:, 0:1], AF.Sqrt, (R_x[k3], R_misc), (R_x[k3],), bias=epsc, scale=1.0 / D)
        sc.op("dve", (lambda v=ssv[k3]: lambda e: e.reciprocal(v[:, 1:2], v[:, 1:2]))(), (R_x[k3],), (R_x[k3],))
        stt(tmpf[k2], xbuf[k3], ssv[k3][:, 1:2], A_bc, ALU.mult, ALU.mult,
            (R_x[k3], R_ada), (R_tmp[k2],))
        tt("pool", hb[k2][:, 0:1152], tmpf[k2][:, 0:1152], B_bc[:, 0:1152], ALU.add,
           (R_tmp[k2], R_ada), (R_hb[k2],))
        tt("dve", hb[k2][:, 1152:2048], tmpf[k2][:, 1152:2048], B_bc[:, 1152:2048], ALU.add,
           (R_tmp[k2], R_ada), (R_hb2[k2],))

    def h_back(n):
        xsrc, t, dst = hitems[n]
        k2 = n % 2
        b0, b1 = 2 + 2 * k2, 3 + 2 * k2
        for kc in range(KC):
            bk = b0 if kc < 8 else b1
            tr(psb[bk][:, (kc % 8) * 128:(kc % 8 + 1) * 128], hb[k2][:, kc * 128:(kc + 1) * 128],
               ident, (R_hb[k2], R_hb2[k2], R_const), (PB[bk],), signal=(kc % 8 == 7))
        cp("act", hTs[k2][:, 0:1024], psb[b0], (PB[b0],), (PB[b0], R_hTs[k2]))
        cp("act", hTs[k2][:, 1024:2048], psb[b1], (PB[b1],), (PB[b1], R_hTs[k2]))
        dma("act", dst[t], hTs[k2], (R_hTs[k2],), (R_hTd,), R_hTs[k2])

    rope_tmp = [ar.alloc(F32, NT * 16) for _ in range(5)]
    rgen = rope_gen(rope_tmp)
    h_load(0)
    h_load(1)
    h_front(0)
    for n in range(len(hitems)):
        if n + 2 < len(hitems):
            h_load(n + 2)
        if n + 1 < len(hitems):
            h_front(n + 1)
        for _ in range(2):
            next(rgen, None)
        h_back(n)
    for _ in rgen:
        pass
    ar.release(pm)
    sc.barrier()
    sc.recycle([R_wada[0], R_wada[1], R_ada] + R_x + R_hTs)

    if debug and debug.get("stop") == "prelude":
        return finish(nc, sc, es, out)

    win_v = win_bf.rearrange("(kc p) c -> p kc c", p=128)
    hT_v = hT_d.rearrange("t p (kc k) -> t p kc k", k=128)
    hTo_v = hTo_d.rearrange("t p (kc k) -> t p kc k", k=128)

    def rope_evac(psrc, dstb, nh, dh, half, cosT, sinT, reads, writes, tmp):
        pv = psrc.rearrange("p (h d) -> p h d", d=dh)
        dv = dstb.rearrange("p (h d) -> p h d", d=dh)
        x1 = pv[:, :, 0:half]
        x2 = pv[:, :, half:2 * half]
        cb_ = cosT.unsqueeze(1).broadcast_to([128, nh, half])
        sb_ = sinT.unsqueeze(1).broadcast_to([128, nh, half])
        t1 = tmp[:, 0, :].rearrange("p (h d) -> p h d", d=half)
        t2 = tmp[:, 1, :].rearrange("p (h d) -> p h d", d=half)
        t3 = tmp[:, 2, :].rearrange("p (h d) -> p h d", d=half)
        t4 = tmp[:, 3, :].rearrange("p (h d) -> p h d", d=half)
        cp("dve", dstb, psrc, reads, writes)
        tt("dve", t1, x1, cb_, ALU.mult, reads, writes)
        tt("dve", t2, x2, sb_, ALU.mult, reads, writes)
        tt("dve", t3, x2, cb_, ALU.mult, reads, writes)
        tt("dve", t4, x1, sb_, ALU.mult, reads, writes)
        tt("dve", dv[:, :, 0:half], t1, t2, ALU.subtract, reads, writes)
        tt("dve", dv[:, :, half:2 * half], t3, t4, ALU.add, reads, writes)

    R_oa = Res("oa_d")
    am = ar.mark()
    for u in range(4 if not (debug and debug.get("skipA")) else 0):
        um = ar.mark()
        flush_casts(f"A{u}")
        kT = ar.alloc(BF16, 2, S)
        V = ar.alloc(BF16, NT, 2, 130)
        qTp = ar.alloc(BF16, 2, NO, 2, 128)
        R_kT = [Res(f"kT{t}") for t in range(NT)]
        R_V = [Res(f"V{t}") for t in range(NT)]
        R_q = [Res(f"q{t}") for t in range(NO)]
        R_unit = Res("unit")
        memset("pool", V[:, :, :, 128:130], 1.0, R_V)
        memset("pool", qTp, 0.0, R_q)
        pm = ar.mark()
        W = ar.alloc(BF16, KC, 768)
        R_W = Res("W")
        for (c0, dst0) in ((OFF["ak"] + 256 * u, 0), (OFF["av"] + 256 * u, 256), (OFF["aq"] + 256 * u, 512)):
            dma("sp", W[:, :, dst0:dst0 + 256], win_v[:, :, c0:c0 + 256], (R_wcA[u],), (R_W,), R_W)
        NH = 3
        hTb = [ar.alloc(BF16, KC, 128) for _ in range(NH)]
        R_hTb = [Res(f"hTb{k}") for k in range(NH)]
        kb = [ar.alloc(BF16, 256) for _ in range(2)]
        R_kb = [Res("kb0"), Res("kb1")]
        rtmp = [ar.alloc(F32, 4, 32) for _ in range(2)]
        items = [("kv", t) for t in range(NT)] + [("q", i) for i in range(NO)]

        def ld(n):
            kind, idx = items[n]
            k3 = n % NH
            dma("sp", hTb[k3], (hT_v if kind == "kv" else hTo_v)[idx], (R_hTd,), (R_hTb[k3],), R_hTb[k3])

        def st_a(n):
            kind, idx = items[n]
            k3, bk = n % NH, n % 2
            if kind == "kv":
                for kc in range(KC):
                    mm(ps[bk], hTb[k3][:, kc, :], W[:, kc, 0:512], kc == 0, kc == KC - 1,
                       (R_hTb[k3], R_W), (PB[bk],), signal=(kc == KC - 1))
            else:
                for kc in range(KC):
                    mm(ps[bk][:, 0:256], hTb[k3][:, kc, :], W[:, kc, 512:768], kc == 0, kc == KC - 1,
                       (R_hTb[k3], R_W), (PB[bk],), signal=(kc == KC - 1))

        def st_b(n):
            kind, idx = items[n]
            bk, k2 = n % 2, n % 2
            if kind == "kv":
                t = idx
                cp("act", V[:, t, :, 0:128], ps[bk][:, 256:512].rearrange("p (h d) -> p h d", d=128),
                   (PB[bk],), (PB[bk], R_V[t]))
                rope_evac(ps[bk][:, 0:256], kb[k2], 4, 64, 8, cosA[:, t, :], sinA[:, t, :],
                          (PB[bk], R_misc), (PB[bk], R_kb[k2]), rtmp[k2])
            else:
                i = idx
                rope_evac(ps[bk][:, 0:256], kb[k2], 4, 64, 8, cosAo[:, i, :], sinAo[:, i, :],
                          (PB[bk], R_misc), (PB[bk], R_kb[k2]), rtmp[k2])

        def st_c(n):
            kind, idx = items[n]
            k2 = n % 2
            tb_ = 2 + k2
            for h in range(2):
                tr(psb[tb_][:, h * 128:(h + 1) * 128], kb[k2][:, h * 128:(h + 1) * 128], ident,
                   (R_kb[k2], R_const), (PB[tb_],), signal=(h == 1))
            if kind == "kv":
                t = idx
                cp("act", kT[:, :, t * 128:(t + 1) * 128],
                   psb[tb_][:, 0:256].rearrange("p (h k) -> p h k", k=128), (PB[tb_],), (PB[tb_], R_kT[t]))
            else:
                i = idx
                for h in range(2):
                    cp("act", qTp[0:64, h, i, 0, :], psb[tb_][0:64, h * 128:(h + 1) * 128],
                       (PB[tb_],), (PB[tb_], R_q[i]))
                    cp("act", qTp[64:128, h, i, 1, :], psb[tb_][64:128, h * 128:(h + 1) * 128],
                       (PB[tb_],), (PB[tb_], R_q[i]))

        ld(0)
        ld(1)
        st_a(0)
        for n in range(len(items)):
            if n + 2 < len(items):
                ld(n + 2)
            if n + 1 < len(items):
                st_a(n + 1)
            st_b(n)
            st_c(n)
        ar.release(pm)
        pm = ar.mark()
        NP = 4
        Pt = [ar.alloc(BF16, 512) for _ in range(NP)]
        R_P = [Res(f"P{k}") for k in range(NP)]
        fin = [ar.alloc(F32, 8) for _ in range(2)]
        o1 = [ar.alloc(F32, 128) for _ in range(2)]
        o2 = [ar.alloc(F32, 128) for _ in range(2)]
        ost = [ar.alloc(F32, 2, 128) for _ in range(2)]
        R_fin = [Res("fin0"), Res("fin1")]
        R_ost = [Res("ost0"), Res("ost1")]
        steps = [(i, kt) for i in range(NO) for kt in range(2 * i + 2)]
        pending = []
        fin2 = [[ar.alloc(F32, 8) for _ in range(2)] for _ in range(2)]
        o1b = [[ar.alloc(F32, 128) for _ in range(2)] for _ in range(2)]
        o2b = [[ar.alloc(F32, 128) for _ in range(2)] for _ in range(2)]
        R_fin2 = [[Res("f00"), Res("f01")], [Res("f10"), Res("f11")]]

        def acc_of(i):
            a2 = i % 2
            abk = (3 + 2 * a2, 4 + 2 * a2)
            return abk, [ps[abk[h]][:, 0:260].rearrange("p (m d) -> p m d", d=130) for h in range(2)]

        def emit_s(n):
            i, kt = steps[n]
            sb_ = n % 3
            last2 = kt >= 2 * i
            for h in range(2):
                mm(ps[sb_][:, h * 256:(h + 1) * 256], kT[:, h, kt * 128:(kt + 1) * 128],
                   qTp[:, h, i, :, :].rearrange("p m k -> p (m k)"), h == 0, (h == 1 and not last2),
                   (R_kT[kt], R_q[i]), (PB[sb_],), signal=(h == 1 and not last2))
            if last2:
                bt = 0 if kt == 2 * i else 1
                mm(ps[sb_], ident, biasT[:, bt, :], False, True, (R_const,), (PB[sb_],))

        emit_s(0)
        emit_s(1)
        for n, (i, kt) in enumerate(steps):
            nkt = 2 * i + 2
            os_ = i % 2
            if kt == 0:
                drip(2)
            if n + 2 < len(steps):
                emit_s(n + 2)
            abk, accs = acc_of(i)
            sb_ = n % 3
            pk = n % NP
            act(Pt[pk], ps[sb_], AF.Exp, (PB[sb_],), (PB[sb_], R_P[pk]), scale=0.125)
            for h in range(2):
                for m in range(2):
                    c0 = h * 256 + m * 128
                    mm(accs[h][:, m, 0:129], Pt[pk][:, c0:c0 + 128], V[:, kt, h, 0:129],
                       (kt == 0 and m == 0), kt == nkt - 1, (R_P[pk], R_V[kt]), (PB[abk[0]], PB[abk[1]]),
                       signal=(h == 1 and m == 1 and kt == nkt - 1))
            for (due, fn_) in list(pending):
                if due <= n:
                    pending.remove((due, fn_))
                    fn_()
            if kt != nkt - 1:
                continue
            for h in range(2):
                acc = accs[h]
                ab = abk[h]
                f = fin2[os_][h]
                oo1, oo2 = o1b[os_][h], o2b[os_][h]
                rd_, wr_ = (PB[ab], R_fin2[os_][h], R_misc), (PB[ab], R_fin2[os_][h])
                sc.op("dve", (lambda f=f, acc=acc: lambda e: e.reciprocal(f[:, 0:2], acc[:, :, 128]))(), rd_, wr_)
                tt("dve", f[:, 2:3], f[:, 1:2], neglam, ALU.mult, rd_, wr_)
                ts("dve", oo2, acc[:, 1, 0:128], f[:, 2:3], None, ALU.mult, None, rd_, wr_)
                stt(oo1, acc[:, 0, 0:128], f[:, 0:1], oo2, ALU.mult, ALU.add, rd_, wr_)
                tt("dve", oo2, oo1, oo1, ALU.mult, rd_[1:], wr_[1:])
                rsum(f[:, 3:4], oo2, rd_[1:], wr_[1:])
                ts("dve", f[:, 4:5], f[:, 3:4], 1.0 / 128, EPS, ALU.mult, ALU.add, rd_[1:], wr_[1:])

            def fin_part2(i=i, os_=os_):
                for h in range(2):
                    f = fin2[os_][h]
                    rr = (R_fin2[os_][h], R_misc)
                    act(f[:, 5:6], f[:, 4:5], AF.Ln, rr, (R_fin2[os_][h],))
                    act(f[:, 6:7], f[:, 5:6], AF.Exp, rr, (R_fin2[os_][h],), scale=-0.5)
                    stt(ost[os_][:, h, :], o1b[os_][h], f[:, 6:7], gsub, ALU.mult, ALU.mult,
                        rr + (R_ost[os_],), (R_fin2[os_][h], R_ost[os_]))
                dma("pool", oa_d[i][:, u * 256:(u + 1) * 256], ost[os_].rearrange("p h d -> p (h d)"),
                    (R_ost[os_],), (R_oa,), R_ost[os_])

            pending.append((n + 4, fin_part2))
        for (_, fn_) in pending:
            fn_()
        ar.release(pm)
        ar.release(um)
        sc.barrier()
        sc.recycle([R_W] + R_hTb + R_ost)

    ar.release(am)
    if debug and debug.get("stop") == "A":
        return finish(nc, sc, es, out)

    sB = 128.0 ** -0.5
    flush_casts("B")
    R_ob = Res("ob_d")
    bm = ar.mark()
    ones_ff = ar.alloc(F32, 128)
    memset("dve", ones_ff, 1.0, (R_misc,))
    hT_blk = hT_d.rearrange("t p (kc k) -> p t kc k", k=128)
    for g in range(2):
        um = ar.mark()
        kcT = ar.alloc(BF16, NCT * 128)
        vcx = ar.alloc(BF16, NCT, 128)
        R_kc = Res("kcT")
        R_vc = Res("vcx")
        pm = ar.mark()
        rawT = [ar.alloc(BF16, S) for _ in range(2)]
        R_raw = [Res("rawk"), Res("rawv")]
        Wc = ar.alloc(BF16, KC, 256)
        R_Wc = Res("Wc")
        dma("sp", Wc[:, :, 0:128], win_v[:, :, OFF["bkc"] + 128 * g:OFF["bkc"] + 128 * g + 128],
            (R_wcB,), (R_Wc,), R_Wc)
        dma("sp", Wc[:, :, 128:256], win_v[:, :, OFF["bvc"] + 128 * g:OFF["bvc"] + 128 * g + 128],
            (R_wcB,), (R_Wc,), R_Wc)
        w1s = [ar.alloc(BF16, 32, 256) for _ in range(2)]
        w2s = [ar.alloc(BF16, 2, 128) for _ in range(2)]
        peb = ar.alloc(BF16, 2, 32)
        R_cw = Res("cmpw")
        for k in range(2):
            dma("sp", w1s[k], w1_bf[k].rearrange("(t d) h -> d t h", d=128), (R_wcC,), (R_cw,), R_cw)
            dma("sp", w2s[k], w2_bf[k].rearrange("(j p) d -> p j d", p=128), (R_wcC,), (R_cw,), R_cw)
        R_cwp = Res("cmpwp")
        dma("pool", peb, peT, (), (R_cwp,), R_cwp)
        hblk = [ar.alloc(BF16, 4, KC, 128) for _ in range(2)]
        R_hblk = [Res("hblk0"), Res("hblk1")]
        nev = 0
        for tb in range(NT // 4):
            k2 = tb % 2
            dma("sp", hblk[k2], hT_blk[:, 4 * tb:4 * tb + 4], (R_hTd,), (R_hblk[k2],), R_hblk[k2])
            for which in range(2):
                bk = nev % 2
                nev += 1
                for kc in range(KC):
                    mm(ps[bk], Wc[:, kc, which * 128:(which + 1) * 128], hblk[k2][:, :, kc, :],
                       kc == 0, kc == KC - 1, (R_Wc, R_hblk[k2]), (PB[bk],), signal=(kc == KC - 1))
                cp("act" if which == 0 else "dve", rawT[which][:, tb * 512:(tb + 1) * 512], ps[bk],
                   (PB[bk],), (PB[bk], R_raw[which]))
        bias_h = ar.alloc(F32, 2)
        hidT = [ar.alloc(BF16, 512) for _ in range(2)]
        R_hid = Res("hid")
        kcb = ar.alloc(BF16, 128)
        rtc = ar.alloc(F32, 4, 16)
        R_kcb = Res("kcb")
        for which in range(2):
            for j in range(2):
                for t in range(32):
                    mm(ps[2][:, j:j + 1], w1s[which][:, t, j * 128:(j + 1) * 128], peb[:, which, t:t + 1],
                       t == 0, t == 31, (R_cw, R_cwp), (PB[2],), signal=(t == 31))
            cp("dve", bias_h, ps[2][:, 0:2], (PB[2],), (PB[2], R_hid))
            memset("dve", hidT[0], 0.0, (R_hid,))
            memset("dve", hidT[1], 0.0, (R_hid,))
            for j in range(2):
                bk = 3 + j
                for t in range(32):
                    mm(ps[bk][:, 0:NCMP], w1s[which][:, t, j * 128:(j + 1) * 128],
                       strided(rawT[which], t, 16, NCMP), t == 0, t == 31,
                       (R_cw, R_raw[which]), (PB[bk],), signal=(t == 31))
                act(hidT[j][:, 0:NCMP], ps[bk][:, 0:NCMP], AF.Silu, (PB[bk], R_hid), (PB[bk], R_hid),
                    bias=bias_h[:, j:j + 1])
            for ct in range(NCT):
                bk = 5 + ct % 2
                for j in range(2):
                    mm(ps[bk][:, 0:128], hidT[j][:, ct * 128:(ct + 1) * 128], w2s[which][:, j, :],
                       j == 0, j == 1, (R_hid, R_cw), (PB[bk],), signal=(j == 1))
                if which == 0:
                    rope_evac(ps[bk][:, 0:128], kcb, 1, 128, 16, cosC[:, ct, :], sinC[:, ct, :],
                              (PB[bk], R_misc), (PB[bk], R_kcb), rtc)
                    tr(psb[7][:, 0:128], kcb, ident, (R_kcb, R_const), (PB[7],))
                    cp("act", kcT[:, ct * 128:(ct + 1) * 128], psb[7][:, 0:128], (PB[7],), (PB[7], R_kc))
                else:
                    cp("act", vcx[:, ct, :], ps[bk][:, 0:128], (PB[bk],), (PB[bk], R_vc))
        ar.release(pm)
        sc.barrier()
        sc.recycle([R_Wc, R_cw, R_cwp] + R_hblk)
        qT = ar.alloc(BF16, 4, NO * 128)
        ksT = ar.alloc(BF16, S)
        kwT = ar.alloc(BF16, S)
        Vs = ar.alloc(BF16, NT, 130)
        Vw = ar.alloc(BF16, NT, 130)
        gts = ar.alloc(F32, NO, 12)
        Et = ar.alloc(BF16, S)
        R_K = [Res(f"K{t}") for t in range(NT)]
        R_Vt = [Res(f"Vt{t}") for t in range(NT)]
        R_q = [Res(f"q{t}") for t in range(NO)]
        R_E = Res("E")
        dma("pool", Et, c_E, (), (R_E,), R_E)
        memset("pool", Vs[:, :, 128:130], 1.0, R_Vt)
        memset("pool", Vw[:, :, 128:130], 1.0, R_Vt)
        pm = ar.mark()
        W2 = ar.alloc(BF16, KC, 1040)
        R_W = Res("W2")
        for (nm, d0, wdt) in (("bks", 0, 128), ("bvs", 128, 128), ("bkw", 256, 128), ("bvw", 384, 128)):
            c0 = OFF[nm] + 128 * g
            dma("sp", W2[:, :, d0:d0 + wdt], win_v[:, :, c0:c0 + wdt], (R_wcB,), (R_W,), R_W)
        c0 = OFF["bq"] + 512 * g
        dma("sp", W2[:, :, 512:1024], win_v[:, :, c0:c0 + 512], (R_wcB,), (R_W,), R_W)
        c0 = OFF["bgate"] + 12 * g
        dma("sp", W2[:, :, 1024:1036], win_v[:, :, c0:c0 + 12], (R_wcB,), (R_W,), R_W)
        NH = 2
        hTb = [ar.alloc(BF16, KC, 128) for _ in range(NH)]
        R_hTb = [Res(f"hTb{k}") for k in range(NH)]
        kb = [ar.alloc(BF16, 512) for _ in range(2)]
        R_kb = [Res("kb0"), Res("kb1")]
        rtmp = [ar.alloc(F32, 4, 64) for _ in range(2)]
        items = [("kv", t) for t in range(NT)] + [("q", i) for i in range(NO)]

        def ld(n):
            kind, idx = items[n]
            k3 = n % NH
            dma("sp", hTb[k3], (hT_v if kind == "kv" else hTo_v)[idx], (R_hTd,), (R_hTb[k3],), R_hTb[k3])

        def st_a(n):
            kind, idx = items[n]
            k3, bk = n % NH, n % 2
            c0 = 0 if kind == "kv" else 512
            for kc in range(KC):
                mm(ps[bk], hTb[k3][:, kc, :], W2[:, kc, c0:c0 + 512], kc == 0, kc == KC - 1,
                   (R_hTb[k3], R_W), (PB[bk],), signal=(kc == KC - 1))
            if kind == "q":
                gb_ = 4 + n % 2
                for kc in range(KC):
                    mm(ps[gb_][:, 0:12], hTb[k3][:, kc, :], W2[:, kc, 1024:1036], kc == 0, kc == KC - 1,
                       (R_hTb[k3], R_W), (PB[gb_],), signal=(kc == KC - 1))

        def st_b(n):
            kind, idx = items[n]
            bk, k2 = n % 2, n % 2
            if kind == "kv":
                t = idx
                rope_evac(ps[bk], kb[k2], 2, 256, 16, cosB[:, t, :], sinB[:, t, :],
                          (PB[bk], R_misc), (PB[bk], R_kb[k2]), rtmp[k2][:, :, 0:32])
                cp("pool", Vs[:, t, 0:128], kb[k2][:, 128:256], (R_kb[k2],), (R_Vt[t],))
                cp("pool", Vw[:, t, 0:128], kb[k2][:, 384:512], (R_kb[k2],), (R_Vt[t],))
            else:
                i = idx
                gb_ = 4 + n % 2
                act(gts[:, i, :], ps[gb_][:, 0:12], AF.Sigmoid, (PB[gb_],), (PB[gb_], R_q[i]))
                rope_evac(ps[bk], kb[k2], 4, 128, 16, cosBo[:, i, :], sinBo[:, i, :],
                          (PB[bk], R_misc), (PB[bk], R_kb[k2]), rtmp[k2])

        def st_c(n):
            kind, idx = items[n]
            k2 = n % 2
            tb_ = 2 + k2
            if kind == "kv":
                t = idx
                tr(psb[tb_][:, 0:128], kb[k2][:, 0:128], ident, (R_kb[k2], R_const), (PB[tb_],), signal=False)
                tr(psb[tb_][:, 128:256], kb[k2][:, 256:384], ident, (R_kb[k2], R_const), (PB[tb_],))
                cp("act", ksT[:, t * 128:(t + 1) * 128], psb[tb_][:, 0:128], (PB[tb_],), (PB[tb_], R_K[t]))
                cp("act", kwT[:, t * 128:(t + 1) * 128], psb[tb_][:, 128:256], (PB[tb_],), (PB[tb_], R_K[t]))
            else:
                i = idx
                for h in range(4):
                    tr(psb[tb_][:, h * 128:(h + 1) * 128], kb[k2][:, h * 128:(h + 1) * 128], ident,
                       (R_kb[k2], R_const), (PB[tb_],), signal=(h == 3))
                cp("act", qT[:, :, i * 128:(i + 1) * 128],
                   psb[tb_][:, 0:512].rearrange("p (h k) -> p h k", k=128), (PB[tb_],), (PB[tb_], R_q[i]))

        ld(0)
        st_a(0)
        for n in range(len(items)):
            if n + 1 < len(items):
                ld(n + 1)
                st_a(n + 1)
            st_b(n)
            st_c(n)
        ar.release(pm)
        sc.barrier()
        sc.recycle([R_W] + R_hTb)
        pm = ar.mark()
        NP = 4
        Pt = [ar.alloc(BF16, 512) for _ in range(NP)]
        R_P = [Res(f"P{k}") for k in range(NP)]
        e_t = [ar.alloc(F32, 512) for _ in range(NCT)]
        R_e = [Res(f"e{k}") for k in range(NCT)]
        ef = ar.alloc(F32, 512)
        cm = ar.alloc(F32, 128)
        zr = ar.alloc(F32, 512)
        R_z = Res("z")
        pcn = [ar.alloc(BF16, 512) for _ in range(2)]
        R_pcn = [Res("pcn0"), Res("pcn1")]
        obuf = [ar.alloc(F32, 4, 128) for _ in range(2)]
        R_obuf = [Res("obuf0"), Res("obuf1")]
        impa = ar.alloc(F32, 128)
        wk = ar.alloc(F32, 128)
        m8 = ar.alloc(F32, 16)
        thr = ar.alloc(F32, 1)
        negsel = ar.alloc(BF16, 128)
        nsT = ar.alloc(BF16, 4, 128)
        R_top = Res("top")
        R_ns = Res("nsT")
        fin = ar.alloc(F32, 8)
        R_fin = Res("finB")
        gts4 = gts.rearrange("p i (h c) -> p i h c", c=3)
        npt = 0
        npc = 0

        def branch(i, os_, kts, Kmat, Vmat, accb, gidx, bias_of, with_sel):
            nonlocal npt
            qi = qT[:, :, i * 128:(i + 1) * 128]
            accs = [ps[accb[0]][:, 0:260].rearrange("p (m d) -> p m d", d=130),
                    ps[accb[1]][:, 0:260].rearrange("p (m d) -> p m d", d=130)]
            npt0 = npt
            npt += len(kts)

            def emit_s(n_):
                kt = kts[n_]
                sb_ = (npt0 + n_) % 3
                bt = bias_of(kt)
                mm(ps[sb_], Kmat[:, kt * 128:(kt + 1) * 128], qi, True, False,
                   (R_K[kt], R_q[i]), (PB[sb_],), signal=False)
                if with_sel:
                    mm(ps[sb_], Et[:, kt * 128:(kt + 1) * 128], nsT.rearrange("p h k -> p (h k)"), False,
                       bt is None, (R_E, R_ns), (PB[sb_],), signal=(bt is None))
                if bt is not None:
                    mm(ps[sb_], ident, biasT[:, bt, :], False, True, (R_const,), (PB[sb_],))

            emit_s(0)
            if len(kts) > 1:
                emit_s(1)
            for n_, kt in enumerate(kts):
                if n_ + 2 < len(kts):
                    emit_s(n_ + 2)
                sb_ = (npt0 + n_) % 3
                pk = (npt0 + n_) % NP
                act(Pt[pk], ps[sb_], AF.Exp, (PB[sb_],), (PB[sb_], R_P[pk]), scale=sB)
                for h in range(4):
                    mm(accs[h // 2][:, h % 2, 0:129], Pt[pk][:, h * 128:(h + 1) * 128], Vmat[:, kt, 0:129],
                       (n_ == 0 and h % 2 == 0), n_ == len(kts) - 1, (R_P[pk], R_Vt[kt]),
                       (PB[accb[0]], PB[accb[1]]), signal=(h == 3 and n_ == len(kts) - 1))
            for b2 in range(2):
                rd_ = (PB[accb[b2]], R_fin, R_q[i], R_obuf[os_])
                wr_ = (PB[accb[b2]], R_fin, R_obuf[os_])
                acc = accs[b2]
                sc.op("dve", (lambda acc=acc: lambda e: e.reciprocal(fin[:, 0:2], acc[:, :, 128]))(), rd_, wr_)
                tt("dve", fin[:, 2:4], fin[:, 0:2], gts4[:, i, 2 * b2:2 * b2 + 2, gidx], ALU.mult, rd_, wr_)
                for hh in range(2):
                    h = 2 * b2 + hh
                    stt(obuf[os_][:, h, :], acc[:, hh, 0:128], fin[:, 2 + hh:3 + hh], obuf[os_][:, h, :],
                        ALU.mult, ALU.add, rd_, wr_)

        nsTs = [nsT, ar.alloc(BF16, 4, 128)]
        R_nss = [R_ns, Res("nsT1")]
        cstate = {}

        def cmp_a(i):
            nonlocal npt
            qi = qT[:, :, i * 128:(i + 1) * 128]
            nct = min(NCT, (16 * i + 14) // 128 + 1)
            cstate[i] = nct
            for ct in range(nct):
                sb_ = npt % 3
                npt += 1
                mm(ps[sb_], kcT[:, ct * 128:(ct + 1) * 128], qi, True, True, (R_kc, R_q[i]), (PB[sb_],))
                if 2048 * ct + 2063 <= 256 * i:
                    act(e_t[ct], ps[sb_], AF.Exp, (PB[sb_],), (PB[sb_], R_e[ct]), scale=sB)
                else:
                    act(ef, ps[sb_], AF.Exp, (PB[sb_],), (PB[sb_], R_e[ct]), scale=sB)
                    ts("dve", cm, Lt, p128[:, 0:1], float(2048 * ct - 256 * i), ALU.add, ALU.is_ge,
                       (R_const,), (R_e[ct],))
                    tt("dve", e_t[ct].rearrange("p (h k) -> p h k", k=128),
                       ef.rearrange("p (h k) -> p h k", k=128),
                       cm.unsqueeze(1).broadcast_to([128, 4, 128]), ALU.mult, (R_e[ct],), (R_e[ct],))
                mm(ps[7], ones_ff, e_t[ct], ct == 0, ct == nct - 1, (R_misc, R_e[ct]), (PB[7],),
                   signal=(ct == nct - 1))
            ts("dve", zr, ps[7], 1e-30, None, ALU.max, None, (PB[7],), (PB[7], R_z))
            sc.op("dve", lambda e: e.reciprocal(zr, zr), (R_z,), (R_z,))

        def cmp_b(i):
            nonlocal npc
            os_ = i % 2
            nct = cstate[i]
            for ct in range(nct):
                pc_ = npc % 2
                npc += 1
                tt("dve", pcn[pc_], e_t[ct], zr, ALU.mult, (R_e[ct], R_z), (R_pcn[pc_],))
                for h in range(4):
                    mm(ps[5][:, h * 128:(h + 1) * 128], pcn[pc_][:, h * 128:(h + 1) * 128], vcx[:, ct, :],
                       (ct == 0 and h == 0), ct == nct - 1, (R_pcn[pc_], R_vc), (PB[5],), signal=False)
                for h in range(4):
                    mm(ps[6][:, 0:128], pcn[pc_][:, h * 128:(h + 1) * 128], ovl[:, ct, :],
                       (ct == 0 and h == 0), (ct == nct - 1 and h == 3), (R_pcn[pc_], R_const), (PB[6], PB[5]),
                       signal=(h == 3))
            for h in range(4):
                ts("dve", obuf[os_][:, h, :], ps[5][:, h * 128:(h + 1) * 128], gts[:, i, 3 * h:3 * h + 1], None,
                   ALU.mult, None, (PB[5], R_q[i]), (PB[5], R_obuf[os_]))
            s0 = 4 * (NO - 1 - i)
            rt, wt = (PB[6], R_top, R_const), (PB[6], R_top)
            tt("dve", impa, ps[6][:, 0:128], keepT[:, s0:s0 + 128], ALU.mult, rt, wt)
            tt("dve", impa, impa, addT[:, s0:s0 + 128], ALU.add, rt, wt)
            memset("dve", impa[:, 0:1], 1.0e4, wt)
            sc.op("dve", lambda e: e.max(out=m8[:, 0:8], in_=impa), rt, wt)
            sc.op("dve", lambda e: e.match_replace(out=wk, in_to_replace=m8[:, 0:8], in_values=impa,
                                                   imm_value=-3.0e38), rt, wt)
            sc.op("dve", lambda e: e.max(out=m8[:, 8:16], in_=wk), rt, wt)
            ts("dve", thr, m8[:, 15:16], -5.0e29, None, ALU.max, None, rt, wt)
            ts("dve", negsel, impa, thr[:, 0:1], NEGB, ALU.is_lt, ALU.mult, rt, wt)

        def cmp_c(i):
            k = i % 2
            tr(psb[7][:, 0:128], negsel, ident, (R_top, R_const), (PB[7],))
            cp("act", nsTs[k], psb[7][:, 0:128].unsqueeze(1).broadcast_to([128, 4, 128]),
               (PB[7],), (PB[7], R_nss[k]))

        def tile_stream(i):
            nonlocal npt
            os_ = i % 2
            nsT_i = nsTs[i % 2]
            R_ns_i = R_nss[i % 2]
            qi = qT[:, :, i * 128:(i + 1) * 128]
            slc_k = list(range(2 * i + 2))
            win_k = [2 * i - 4 + r for r in range(6) if 2 * i - 4 + r >= 0]
            steps = [(0, kt) for kt in slc_k] + [(1, kt) for kt in win_k]
            cfg = [dict(K=ksT, V=Vs, accb=(3, 4), gidx=1, n=len(slc_k)),
                   dict(K=kwT, V=Vw, accb=(5, 6), gidx=2, n=len(win_k))]
            for c in cfg:
                c["accs"] = [ps[c["accb"][0]][:, 0:260].rearrange("p (m d) -> p m d", d=130),
                             ps[c["accb"][1]][:, 0:260].rearrange("p (m d) -> p m d", d=130)]
            npt0 = npt
            npt += len(steps)

            def bias_of(br, kt):
                if br == 0:
                    return (0 if kt == 2 * i else 1) if kt >= 2 * i else None
                return 2 + (kt - (2 * i - 4))

            def emit_s(n):
                br, kt = steps[n]
                c = cfg[br]
                sb_ = (npt0 + n) % 3
                bt = bias_of(br, kt)
                mm(ps[sb_], c["K"][:, kt * 128:(kt + 1) * 128], qi, True, False,
                   (R_K[kt], R_q[i]), (PB[sb_],), signal=False)
                if br == 0:
                    mm(ps[sb_], Et[:, kt * 128:(kt + 1) * 128], nsT_i.rearrange("p h k -> p (h k)"), False,
                       bt is None, (R_E, R_ns_i), (PB[sb_],), signal=(bt is None))
                if bt is not None:
                    mm(ps[sb_], ident, biasT[:, bt, :], False, True, (R_const,), (PB[sb_],))

            inject_at = 0
            emit_s(0)
            emit_s(1)
            for n, (br, kt) in enumerate(steps):
                c = cfg[br]
                first = (n == 0) if br == 0 else (n == len(slc_k))
                last = (n == len(slc_k) - 1) if br == 0 else (n == len(steps) - 1)
                if n == inject_at and i + 1 < NO:
                    cmp_b(i + 1)
                if n + 2 < len(steps):
                    emit_s(n + 2)
                sb_ = (npt0 + n) % 3
                pk = (npt0 + n) % NP
                act(Pt[pk], ps[sb_], AF.Exp, (PB[sb_],), (PB[sb_], R_P[pk]), scale=sB)
                accb, accs = c["accb"], c["accs"]
                for h in range(4):
                    mm(accs[h // 2][:, h % 2, 0:129], Pt[pk][:, h * 128:(h + 1) * 128], c["V"][:, kt, 0:129],
                       (first and h % 2 == 0), last, (R_P[pk], R_Vt[kt]),
                       (PB[accb[0]], PB[accb[1]]), signal=(h == 3 and last))
                if not last:
                    continue
                for b2 in range(2):
                    rd_ = (PB[accb[b2]], R_fin, R_q[i], R_obuf[os_])
                    wr_ = (PB[accb[b2]], R_fin, R_obuf[os_])
                    acc = accs[b2]
                    sc.op("dve", (lambda acc=acc: lambda e: e.reciprocal(fin[:, 0:2], acc[:, :, 128]))(), rd_, wr_)
                    tt("dve", fin[:, 2:4], fin[:, 0:2], gts4[:, i, 2 * b2:2 * b2 + 2, c["gidx"]], ALU.mult, rd_, wr_)
                    for hh in range(2):
                        h = 2 * b2 + hh
                        stt(obuf[os_][:, h, :], acc[:, hh, 0:128], fin[:, 2 + hh:3 + hh], obuf[os_][:, h, :],
                            ALU.mult, ALU.add, rd_, wr_)

        cmp_a(0)
        cmp_b(0)
        cmp_c(0)
        if NO > 1:
            cmp_a(1)
        for i in range(NO):
            os_ = i % 2
            tile_stream(i)
            if i + 1 < NO:
                cmp_c(i + 1)
            if i + 2 < NO:
                cmp_a(i + 2)
            dma("pool", ob_d[i][:, g * 512:(g + 1) * 512], obuf[os_].rearrange("p h d -> p (h d)"),
                (R_obuf[os_],), (R_ob,), R_obuf[os_])
        ar.release(pm)
        ar.release(um)
        sc.barrier()
        sc.recycle([R_E] + R_obuf)
    ar.release(bm)
    if debug and debug.get("stop") == "B":
        return finish(nc, sc, es, out)

    flush_casts("T")
    TBK = min(4, NO)
    NB = NO // TBK
    G_bc = ar.alloc(F32, D)
    fg_bc = ar.alloc(F32, D)
    R_tc = Res("tailconst")
    dma("sp", G_bc, G_d, (R_G,), (R_tc,), R_tc)
    dma("sp", fg_bc, fng.broadcast_to([128, D]), (), (R_tc,), R_tc)
    wbr_v = wbr_bf.rearrange("(kc p) c -> p kc c", p=128)
    wout_v = wout_bf.rearrange("(kc p) c -> p kc c", p=128)
    NWS = 6
    wsl = [ar.alloc(BF16, KC, 512) for _ in range(NWS)]
    R_ws = [Res(f"ws{k}") for k in range(NWS)]
    hTt, uT, xo = [], [], []
    for j in range(TBK):
        blk8 = ar.alloc(BF16, 2, KC, 128)
        hTt.append(blk8[:, 0])
        uT.append(blk8[:, 1])
        xo.append(blk8.rearrange("p a k c -> p (a k c)").bitcast(F32))
    ymT = [ar.alloc(BF16, KC, 128) for _ in range(TBK)]
    R_hTt = [Res(f"hTt{k}") for k in range(TBK)]
    R_uT = [Res(f"uT{k}") for k in range(TBK)]
    R_ymT = [Res(f"ymT{k}") for k in range(TBK)]
    R_xost = [Res(f"xost{k}") for k in range(TBK)]
    ar2 = Arena(arena_t, ARENA_BYTES)
    ar2.top = rope_lo
    NTS = 8
    n_in_rope = max(0, min(NTS, (rope_hi - rope_lo) // 2048))
    utmp = [ar.alloc(BF16, 512) for _ in range(2)]
    R_ut = [Res("ut0"), Res("ut1")]
    tsl = [ar2.alloc(F32, 512) for _ in range(n_in_rope)] + [ar.alloc(F32, 512) for _ in range(NTS - n_in_rope)]
    assert ar2.top <= rope_hi
    R_ts = [Res(f"ts{k}") for k in range(NTS)]
    ssf = ar.alloc(F32, 8)
    R_ss = Res("ssf")
    R_out = Res("out")
    nts = 0

    def tslot():
        nonlocal nts
        k = nts % NTS
        nts += 1
        return k

    mg0 = OFF["mgate"]
    wneeds = []
    for blk in range(NB):
        for cb in range(4):
            c0 = (OFF["az"] + 512 * cb) if cb < 2 else (OFF["bz"] + 512 * (cb - 2))
            wneeds.append([win_v[:, :, c0:c0 + 512]])
        for cb in range(4):
            wneeds.append([wbr_v[:, :, cb * 512:(cb + 1) * 512],
                           win_v[:, :, mg0 + cb * 512:mg0 + cb * 512 + 512],
                           win_v[:, :, mg0 + 2048 + cb * 512:mg0 + 2048 + cb * 512 + 512]])
        for cb in range(4):
            wneeds.append([wout_v[:, :, cb * 512:(cb + 1) * 512]])
    wslots = {}
    nws = 0

    def prefetch(idx):
        nonlocal nws
        if idx >= len(wneeds) or idx in wslots:
            return
        sl = []
        for src_ap in wneeds[idx]:
            k = nws % NWS
            nws += 1
            dma("sp", wsl[k], src_ap, (R_wcT,), (R_ws[k],), R_ws[k])
            sl.append(k)
        wslots[idx] = sl

    prefetch(0)
    widx = 0
    for blk in range(NB):
        for j in range(TBK):
            i = blk * TBK + j
            dma("sp", hTt[j], hTo_v[i], (R_hTd,), (R_hTt[j],), R_hTt[j])
        items = [(cb, j) for cb in range(4) for j in range(TBK)]
        info = {}

        def s1_a(n, blk=blk, items=items, info=info, widx=widx):
            cb, j = items[n]
            i = blk * TBK + j
            if j == 0:
                prefetch(widx + cb + 1)
            k = wslots[widx + cb][0]
            ko = tslot()
            info[n] = ko
            src = (oa_d if cb < 2 else ob_d)[i][:, (cb % 2) * 512:(cb % 2) * 512 + 512]
            dma("sp", tsl[ko], src, (R_oa, R_ob), (R_ts[ko],), R_ts[ko])
            bk = n % 2
            for kc in range(KC):
                mm(ps[bk], hTt[j][:, kc, :], wsl[k][:, kc, :], kc == 0, kc == KC - 1,
                   (R_hTt[j], R_ws[k]), (PB[bk],), signal=(kc == KC - 1))

        def s1_b(n, items=items, info=info):
            cb, j = items[n]
            bk, n2 = n % 2, n % 2
            ko = info[n]
            kz = tslot()
            act(tsl[kz], ps[bk], AF.Silu, (PB[bk],), (PB[bk], R_ts[kz]))
            tt("dve", utmp[n2], tsl[kz], tsl[ko], ALU.mult, (R_ts[kz], R_ts[ko]), (R_ut[n2],))

        def s1_c(n, items=items):
            cb, j = items[n]
            n2 = n % 2
            tb_ = 2 + n2
            for q in range(4):
                tr(psb[tb_][:, q * 128:(q + 1) * 128], utmp[n2][:, q * 128:(q + 1) * 128], ident,
                   (R_ut[n2], R_const), (PB[tb_],), signal=(q == 3))
            cp("act", uT[j][:, 4 * cb:4 * cb + 4, :],
               psb[tb_][:, 0:512].rearrange("p (q k) -> p q k", k=128), (PB[tb_],), (PB[tb_], R_uT[j]))

        s1_a(0)
        for n in range(len(items)):
            if n + 1 < len(items):
                s1_a(n + 1)
            s1_b(n)
            s1_c(n)
        widx += 4
        def s3_a(n, items=items, widx=widx):
            cb, j = items[n]
            if j == 0:
                prefetch(widx + cb + 1)
            kbr, kga, kgb = wslots[widx + cb]
            b4 = 4 * (n % 2)
            ba, bb, bc, bd = b4, b4 + 1, b4 + 2, b4 + 3
            for kc in range(8):
                mm(ps[ba], uT[j][:, kc, :], wsl[kbr][:, kc, :], kc == 0, kc == 7,
                   (R_uT[j], R_ws[kbr]), (PB[ba],), signal=(kc == 7))
            for kc in range(8, 16):
                mm(ps[bb], uT[j][:, kc, :], wsl[kbr][:, kc, :], kc == 8, kc == 15,
                   (R_uT[j], R_ws[kbr]), (PB[bb],), signal=(kc == 15))
            for kc in range(KC):
                mm(ps[bc], hTt[j][:, kc, :], wsl[kga][:, kc, :], kc == 0, kc == KC - 1,
                   (R_hTt[j], R_ws[kga]), (PB[bc],), signal=(kc == KC - 1))
            for kc in range(KC):
                mm(ps[bd], hTt[j][:, kc, :], wsl[kgb][:, kc, :], kc == 0, kc == KC - 1,
                   (R_hTt[j], R_ws[kgb]), (PB[bd],), signal=(kc == KC - 1))

        def s3_b(n, items=items):
            cb, j = items[n]
            b4 = 4 * (n % 2)
            n2 = n % 2
            ba, bb, bc, bd = b4, b4 + 1, b4 + 2, b4 + 3
            k1, k2_, k3_, k4_ = tslot(), tslot(), tslot(), tslot()
            act(tsl[k1], ps[bc], AF.Sigmoid, (PB[bc],), (PB[bc], R_ts[k1]))
            act(tsl[k2_], ps[bd], AF.Sigmoid, (PB[bd],), (PB[bd], R_ts[k2_]))
            tt("dve", tsl[k3_], tsl[k1], ps[ba], ALU.mult, (R_ts[k1], PB[ba]), (R_ts[k3_], PB[ba]))
            tt("dve", tsl[k4_], tsl[k2_], ps[bb], ALU.mult, (R_ts[k2_], PB[bb]), (R_ts[k4_], PB[bb]))
            tt("dve", utmp[n2], tsl[k3_], tsl[k4_], ALU.add, (R_ts[k3_], R_ts[k4_]), (R_ut[n2],))

        def s3_c(n, items=items):
            cb, j = items[n]
            n2 = n % 2
            ba = 4 * (n % 2)
            for q in range(4):
                tr(psb[ba][:, q * 128:(q + 1) * 128], utmp[n2][:, q * 128:(q + 1) * 128], ident,
                   (R_ut[n2], R_const), (PB[ba],), signal=(q == 3))
            cp("act", ymT[j][:, 4 * cb:4 * cb + 4, :],
               psb[ba][:, 0:512].rearrange("p (q k) -> p q k", k=128), (PB[ba],), (PB[ba], R_ymT[j]))

        s3_a(0)
        for n in range(len(items)):
            s3_b(n)
            if n + 1 < len(items):
                s3_a(n + 1)
            s3_c(n)
        widx += 4
        for j in range(TBK):
            i = blk * TBK + j
            dma("sp", xo[j], x_own[i * 128:(i + 1) * 128, :], (), (R_hTt[j], R_uT[j]), R_hTt[j])
        for cb in range(4):
            prefetch(widx + cb + 1)
            k = wslots[widx + cb][0]
            for j in range(TBK):
                bk = (cb * TBK + j) % 2
                for kc in range(KC):
                    mm(ps[bk], ymT[j][:, kc, :], wsl[k][:, kc, :], kc == 0, kc == KC - 1,
                       (R_ymT[j], R_ws[k]), (PB[bk],), signal=(kc == KC - 1))
                k1 = tslot()
                tt("dve", tsl[k1], ps[bk], G_bc[:, cb * 512:(cb + 1) * 512], ALU.mult,
                   (PB[bk], R_tc), (PB[bk], R_ts[k1]))
                tt("dve", xo[j][:, cb * 512:(cb + 1) * 512], tsl[k1], xo[j][:, cb * 512:(cb + 1) * 512],
                   ALU.add, (R_ts[k1], R_hTt[j], R_uT[j]), (R_hTt[j], R_uT[j]))
        widx += 4
        for j in range(TBK):
            i = blk * TBK + j
            rw_ = (R_hTt[j], R_uT[j])
            for q in range(4):
                k1 = tslot()
                act(tsl[k1], xo[j][:, q * 512:(q + 1) * 512], AF.Square, rw_, (R_ts[k1], R_ss),
                    accum_out=ssf[:, q:q + 1])
            rsum(ssf[:, 4:5], ssf[:, 0:4], (R_ss,), (R_ss,))
            act(ssf[:, 5:6], ssf[:, 4:5], AF.Sqrt, (R_ss, R_misc), (R_ss,), bias=epsc, scale=1.0 / D)
            sc.op("dve", lambda e: e.reciprocal(ssf[:, 6:7], ssf[:, 5:6]), (R_ss,), (R_ss,))
            stt(xo[j], xo[j], ssf[:, 6:7], fg_bc, ALU.mult, ALU.mult, rw_ + (R_ss, R_tc), rw_)
            dma("pool", out[i * 128:(i + 1) * 128, :], xo[j], rw_, (R_out,), R_xost[j])
    return finish(nc, sc, es, out)


def finish(nc, sc, es, out):
    sc.final_wait("pool")
    build_program.stats = {e: len(sc.q[e]) for e in sc.ENG}
    with nc.Block() as block:
        block.sync(sc.replay("sp"))
        block.tensor(sc.replay("pe"))
        block.scalar(sc.replay("act"))
        block.vector(sc.replay("dve"))
        block.gpsimd(sc.replay("pool"))
    es.close()
    return nc


def host_consts(S, p):
    NT = S // 128
    NO = NT // 2
    NCMP = (S - 32) // 16 + 1
    NCT = (NCMP + 127) // 128
    KW = 4 * NO - 4 + 128
    f = np.float32
    k = np.arange(128)
    cs = {}
    cs["c_ident"] = np.eye(128, dtype=f)
    cs["c_E"] = (k[:, None] == (np.arange(S)[None, :] // 64)).astype(f)
    c = (np.arange(NCT)[None, :, None] * 128 + k[:, None, None])
    n = k[None, None, :]
    cs["c_ovl"] = ((c >= 4 * n - 1) & (c <= 4 * n + 3) & (c < NCMP)).astype(f)
    cs["c_L"] = (k[None, :] - 16 * k[:, None] - 31).astype(f)
    m = np.arange(KW)[None, :]
    r = m - 4 * (NO - 1) - 2 * p
    hq = (k[:, None] >= 64).astype(np.int64)
    keep = (r < hq - 1).astype(f)
    add = np.where((r == hq - 1) | (r == hq), 1.0e4, np.where(r > hq, -1.0e30, 0.0)).astype(f)
    cs["c_keep"] = keep
    cs["c_add"] = add
    kk, qq = k[:, None], k[None, :]
    causal = np.where(kk <= qq, 0.0, NEGB).astype(f)
    lo = np.where(kk > qq, 0.0, NEGB).astype(f)
    allm = np.full((128, 128), NEGB, f)
    zero = np.zeros((128, 128), f)
    if p == 0:
        tabs = [causal, allm, lo, zero, zero, zero, causal, allm]
    else:
        tabs = [zero, causal, allm, lo, zero, zero, zero, causal]
    cs["c_bias"] = np.ascontiguousarray(
        np.stack([np.tile(t, (1, 4)) for t in tabs], axis=1)).astype(f)
    cs["c_pos_all"] = (128 * np.arange(NT)[None, :] + k[:, None]).astype(f)
    cs["c_pos_own"] = (128 * (2 * np.arange(NO)[None, :] + p) + k[:, None]).astype(f)
    cs["c_pos_cmp"] = (16 * (128 * np.arange(NCT)[None, :] + k[:, None]) + 31).astype(f)
    cs["c_p128"] = np.full((128, 1), 128.0 * p, f)
    return cs


def make_in_maps(inputs, S, batches):
    f = np.float32
    NT = S // 128
    g = {k_: np.asarray(v) for k_, v in inputs.items()}
    shared = {
        "w_ada": np.ascontiguousarray(g["w_ada"][0], f),
        "b_ada": np.ascontiguousarray(g["b_ada"][0][None, :], f),
        "norm_g": np.ascontiguousarray(g["norm_g"][0][None, :], f),
        "w_in": np.ascontiguousarray(g["w_in"][0], f),
        "lam4": np.ascontiguousarray(np.stack([g["lambda_q1"][0], g["lambda_k1"][0],
                                               g["lambda_q2"][0], g["lambda_k2"][0]]), f),
        "diff_norm_g": np.ascontiguousarray(g["diff_norm_g"][0][None, :], f),
        "peT": np.ascontiguousarray(np.stack([g["cmp_pe_k"][0].T, g["cmp_pe_v"][0].T], axis=1), f),
        "cmp_w1": np.ascontiguousarray(np.stack([g["cmp_w1_k"][0], g["cmp_w1_v"][0]]), f),
        "cmp_w2": np.ascontiguousarray(np.stack([g["cmp_w2_k"][0], g["cmp_w2_v"][0]]), f),
        "w_branch": np.ascontiguousarray(g["w_branch"][0], f),
        "w_out": np.ascontiguousarray(g["w_out"][0], f),
        "final_norm_g": np.ascontiguousarray(g["final_norm_g"][None, :], f),
    }
    consts = [host_consts(S, 0), host_consts(S, 1)]
    maps = []
    for b in batches:
        xb = np.ascontiguousarray(g["x"][b], f)
        for p in range(2):
            m = dict(shared)
            m.update(consts[p])
            m["x_all"] = xb
            m["x_own"] = np.ascontiguousarray(xb.reshape(NT, 128, D)[p::2].reshape(-1, D))
            m["cT"] = np.ascontiguousarray(g["c"][b].reshape(KC, 128).T, f)
            maps.append(m)
    return maps


def kernel(**inputs):
    S = 8192
    B = 4
    NT = S // 128
    nc = build_program(S)
    maps = make_in_maps(inputs, S, list(range(B)))
    res = run_bass_kernel_spmd(nc, maps, core_ids=list(range(8)))
    outp = np.empty((B, S, D), np.float32)
    for b in range(B):
        v = outp[b].reshape(NT, 128, D)
        for p in range(2):
            v[p::2] = np.asarray(res.results[2 * b + p]["out"], np.float32).reshape(NT // 2, 128, D)
    return outp
```

```python
import math
from contextlib import ExitStack

import numpy as np
import concourse.bass as bass
import concourse.mybir as mybir
from concourse.bass_utils import run_bass_kernel_spmd

F32 = mybir.dt.float32
BF16 = mybir.dt.bfloat16
AF = mybir.ActivationFunctionType
ALU = mybir.AluOpType

D = 2048
KC = 16
IN_W = 11800
EPS = 1e-6
THETA = 500000.0
NEGB = -30000.0
PI = math.pi

OFF = dict(aq=0, ak=1024, av=2048, az=3072, bq=4096, bkc=5120, bvc=5376, bks=5632, bvs=5888,
           bkw=6144, bvw=6400, bz=6656, bgate=7680, mgate=7704)


class Sem:
    __slots__ = ("h", "issued", "bg", "kind")

    def __init__(self, h):
        self.h = h
        self.issued = 0
        self.bg = False
        self.kind = None


class Res:
    __slots__ = ("name", "w", "r", "dsem")
    registry = []

    def __init__(self, name):
        self.name = name
        self.w = None
        self.r = []
        self.dsem = None
        Res.registry.append(self)


class Sched:
    ENG = ("pe", "act", "dve", "pool", "sp")

    def __init__(self, nc, sem_handles):
        self.nc = nc
        self.free = [Sem(h) for h in sem_handles]
        self.esem = {e: self.free.pop() for e in self.ENG}
        self.pend = {e: False for e in self.ENG}
        self.q = {e: [] for e in self.ENG}
        self.waited = {e: {} for e in self.ENG}
        self.dsems = []
        self.free_kind = {"sw": [], "hw": []}
        self.nops = {e: 0 for e in self.ENG}

    def new_sem(self, kind):
        pool = self.free_kind[kind]
        if pool:
            s = pool.pop()
        else:
            s = self.free.pop()
            s.kind = kind
        if s not in self.dsems:
            self.dsems.append(s)
        return s

    def recycle(self, res_list):
        for r in res_list:
            if r.dsem is not None:
                self.free_kind[r.dsem.kind].append(r.dsem)
                r.dsem = None

    def _plan_waits(self, eng, deps):
        best = {}
        for sem, val in deps:
            if sem is self.esem[eng] and eng == "pe":
                continue
            if best.get(sem, 0) < val:
                best[sem] = val
        waits = []
        wd = self.waited[eng]
        for sem, val in best.items():
            if wd.get(sem, 0) >= val:
                continue
            assert val <= sem.issued, f"wait on unsignaled token eng={eng}"
            wd[sem] = val
            waits.append((sem.h, val))
        return waits

    def _deps(self, reads, writes):
        deps = []
        for r in reads:
            if r.w is not None:
                deps.append(r.w)
        for w in writes:
            if w.w is not None:
                deps.append(w.w)
            deps.extend(w.r)
        return deps

    def op(self, eng, fn, reads=(), writes=(), signal=True):
        waits = self._plan_waits(eng, self._deps(reads, writes))
        sem = self.esem[eng]
        tok = (sem, sem.issued + 1)
        if signal:
            sem.issued += 1
            self.pend[eng] = False
        else:
            assert eng == "pe"
            self.pend[eng] = True
        for r in reads:
            r.r.append(tok)
        for w in writes:
            w.w = tok
            w.r = []
        self.q[eng].append((waits, fn, sem.h if signal else None, 1))
        self.nops[eng] += 1

    def dma(self, queue, fn, reads=(), writes=(), owner=None, serialize=True, bg=False):
        kind = "sw" if queue == "pool" else "hw"
        if owner.dsem is None:
            owner.dsem = self.new_sem(kind)
            owner.dsem.bg = bg
        assert owner.dsem.kind == kind, f"semaphore of {owner.name} used from both DMA queue kinds"
        sem = owner.dsem
        deps = self._deps(reads, writes)
        if serialize and sem.issued > 0:
            deps.append((sem, sem.issued))
        waits = self._plan_waits(queue, deps)
        sem.issued += 16
        tok = (sem, sem.issued)
        for r in reads:
            r.r.append(tok)
        for w in writes:
            w.w = tok
            w.r = []
        self.q[queue].append((waits, fn, sem.h, 16))

    def barrier(self):
        for e in self.ENG:
            assert not self.pend[e], f"barrier with unsignaled ops on {e}"
        toks = [(self.esem[e], self.esem[e].issued) for e in self.ENG if self.esem[e].issued > 0]
        toks += [(s, s.issued) for s in self.dsems if s.issued > 0 and not s.bg]
        for e in self.ENG:
            waits = self._plan_waits(e, [t for t in toks if t[0] is not self.esem[e]])
            if waits:
                self.q[e].append((waits, None, None, 0))

    def final_wait(self, queue):
        toks = [(s, s.issued) for s in self.dsems if s.issued > 0]
        toks += [(self.esem[e], self.esem[e].issued) for e in self.ENG
                 if e != queue and self.esem[e].issued > 0]
        waits = self._plan_waits(queue, toks)
        self.q[queue].append((waits, None, None, 0))

    def replay(self, eng):
        items = self.q[eng]

        def f(e):
            for waits, fn, sem, inc in items:
                for (h, v) in waits:
                    e.wait_ge(h, v)
                if fn is None:
                    continue
                ins = fn(e)
                if sem is not None:
                    ins.then_inc(sem, inc)
        return f


class Arena:
    def __init__(self, tensor, nbytes):
        self.t = tensor
        self.n = nbytes
        self.top = 0

    def mark(self):
        return self.top

    def release(self, m):
        self.top = m

    def alloc(self, dtype, *shape):
        esz = 4 if dtype == F32 else 2
        n = 1
        for s in shape:
            n *= s
        nb = (n * esz + 31) // 32 * 32
        off = self.top
        self.top += nb
        assert self.top <= self.n, f"SBUF arena overflow {self.top} > {self.n}"
        ap = self.t[:, off // 2: off // 2 + (n * esz) // 2]
        if dtype == F32:
            ap = ap.bitcast(F32)
        if len(shape) == 2:
            ap = ap.rearrange("p (a b) -> p a b", b=shape[1])
        elif len(shape) == 3:
            ap = ap.rearrange("p (a b c) -> p a b c", b=shape[1], c=shape[2])
        elif len(shape) == 4:
            ap = ap.rearrange("p (a b c d) -> p a b c d", b=shape[1], c=shape[2], d=shape[3])
        return ap


def strided(ap2d, start, step, count):
    pst = ap2d.ap[0]
    est = ap2d.ap[-1][0]
    return bass.AP(ap2d.tensor, ap2d.offset + start * est, (tuple(pst), (step * est, count)))


def build_program(S, debug=None):
    NT = S // 128
    NO = NT // 2
    NCMP = (S - 32) // 16 + 1
    NCT = (NCMP + 127) // 128
    KW = 4 * NO - 4 + 128

    nc = bass.Bass("TRN2", target_bir_lowering=False)

    def din(name, shape, dt=F32):
        return nc.dram_tensor(name, list(shape), dt, kind="ExternalInput").ap()

    def dscr(name, shape, dt):
        kind = "ExternalOutput" if (debug and name in debug) else "Internal"
        return nc.dram_tensor(name, list(shape), dt, kind=kind).ap()

    x_all = din("x_all", [S, D])
    x_own = din("x_own", [S // 2, D])
    cT = din("cT", [128, KC])
    w_ada = din("w_ada", [D, 3 * D])
    b_ada = din("b_ada", [1, 3 * D])
    norm_g = din("norm_g", [1, D])
    w_in = din("w_in", [D, IN_W])
    lam4 = din("lam4", [4, 64])
    dng = din("diff_norm_g", [1, 128])
    peT = din("peT", [128, 2, 32])
    w1 = din("cmp_w1", [2, 4096, 256])
    w2 = din("cmp_w2", [2, 256, 128])
    w_br = din("w_branch", [D, D])
    w_out = din("w_out", [D, D])
    fng = din("final_norm_g", [1, D])
    c_ident = din("c_ident", [128, 128])
    c_E = din("c_E", [128, S])
    c_ovl = din("c_ovl", [128, NCT, 128])
    c_L = din("c_L", [128, 128])
    c_keep = din("c_keep", [128, KW])
    c_add = din("c_add", [128, KW])
    c_bias = din("c_bias", [128, 8, 512])
    c_pos_all = din("c_pos_all", [128, NT])
    c_pos_own = din("c_pos_own", [128, NO])
    c_pos_cmp = din("c_pos_cmp", [128, NCT])
    c_p128 = din("c_p128", [128, 1])

    out = nc.dram_tensor("out", [S // 2, D], F32, kind="ExternalOutput").ap()

    hT_d = dscr("hT_d", [NT, 128, D], BF16)
    hTo_d = dscr("hTo_d", [NO, 128, D], BF16)
    win_bf = dscr("win_bf", [D, IN_W], BF16)
    wbr_bf = dscr("wbr_bf", [D, D], BF16)
    wout_bf = dscr("wout_bf", [D, D], BF16)
    w1_bf = dscr("w1_bf", [2, 4096, 256], BF16)
    w2_bf = dscr("w2_bf", [2, 256, 128], BF16)
    oa_d = dscr("oa_d", [NO, 128, 1024], F32)
    ob_d = dscr("ob_d", [NO, 128, 1024], F32)
    G_d = dscr("G_d", [128, D], F32)

    ARENA_BYTES = 204800
    es = ExitStack()
    arena_t = es.enter_context(nc.sbuf_tensor("arena", [128, ARENA_BYTES // 2], BF16))
    banks = [es.enter_context(nc.psum_tensor(f"bank{k}", [128, 512], F32)) for k in range(8)]
    sem_handles = [es.enter_context(nc.semaphore(f"s{k}")) for k in range(100)]
    sc = Sched(nc, sem_handles)
    ar = Arena(arena_t, ARENA_BYTES)

    ps = [b[:] for b in banks]
    psb = [b[:].bitcast(BF16) for b in banks]
    PB = [Res(f"bank{k}") for k in range(8)]

    def mm(out_, lhsT, rhs, start, stop, reads, writes, signal=True):
        sc.op("pe", lambda e: e.matmul(out_, lhsT, rhs, start=start, stop=stop,
                                        skip_group_check=True), reads, writes, signal)

    def tr(out_, in_, ident, reads, writes, signal=True):
        sc.op("pe", lambda e: e.transpose(out_, in_, ident), reads, writes, signal)

    def act(out_, in_, func, reads, writes, bias=None, scale=None, accum_out=None):
        kw = {}
        if bias is not None:
            kw["bias"] = bias
        if scale is not None:
            kw["scale"] = scale
        if accum_out is not None:
            kw["accum_out"] = accum_out
        sc.op("act", lambda e: e.activation(out_, in_, func, **kw), reads, writes)

    def ts(eng, out_, in0, s1, s2, op0, op1, reads, writes):
        if op1 is None:
            sc.op(eng, lambda e: e.tensor_scalar(out_, in0, s1, None, op0), reads, writes)
        else:
            sc.op(eng, lambda e: e.tensor_scalar(out_, in0, s1, s2, op0, op1), reads, writes)

    def tt(eng, out_, in0, in1, op, reads, writes):
        sc.op(eng, lambda e: e.tensor_tensor(out_, in0, in1, op), reads, writes)

    def stt(out_, in0, scalar, in1, op0, op1, reads, writes):
        sc.op("dve", lambda e: e.scalar_tensor_tensor(out_, in0, scalar, in1, op0, op1), reads, writes)

    def cp(eng, out_, in_, reads, writes):
        if eng == "act":
            sc.op("act", lambda e: e.copy(out_, in_), reads, writes)
        else:
            sc.op(eng, lambda e: e.tensor_copy(out_, in_), reads, writes)

    def rsum(out_, in_, reads, writes):
        sc.op("dve", lambda e: e.reduce_sum(out_, in_, axis=mybir.AxisListType.X), reads, writes)

    def memset(eng, ap, val, writes):
        sc.op(eng, lambda e: e.memset(ap, val), (), writes)

    def dma(queue, out_, in_, reads, writes, owner, serialize=True, bg=False):
        sc.dma(queue, lambda e: e.dma_start(out=out_, in_=in_), reads, writes, owner, serialize, bg)

    R_const = Res("const")
    ident = ar.alloc(BF16, 128)
    biasT = ar.alloc(BF16, 8, 512)
    Lt = ar.alloc(F32, 128)
    keepT = ar.alloc(F32, KW)
    addT = ar.alloc(F32, KW)
    ovl = ar.alloc(BF16, NCT, 128)
    p128 = ar.alloc(F32, 1)
    pos_all = ar.alloc(F32, NT)
    pos_own = ar.alloc(F32, NO)
    pos_cmp = ar.alloc(F32, NCT)
    gsub = ar.alloc(F32, 128)
    lamv = ar.alloc(F32, 4, 64)
    ones_f = ar.alloc(F32, 128)
    small = ar.alloc(F32, 16)

    R_constp = Res("constp")

    rope_lo = ar.mark()
    cosA = ar.alloc(F32, NT, 8)
    sinA = ar.alloc(F32, NT, 8)
    cosAo = ar.alloc(F32, NO, 8)
    sinAo = ar.alloc(F32, NO, 8)
    cosB = ar.alloc(F32, NT, 16)
    sinB = ar.alloc(F32, NT, 16)
    cosBo = ar.alloc(F32, NO, 16)
    sinBo = ar.alloc(F32, NO, 16)
    cosC = ar.alloc(F32, NCT, 16)
    sinC = ar.alloc(F32, NCT, 16)
    rope_hi = ar.mark()

    def cdma(queue, out_, in_):
        rr = R_const if queue == "sp" else R_constp
        dma(queue, out_, in_, (), (rr,), rr, serialize=False)

    cdma("pool", ident, c_ident)
    cdma("pool", biasT, c_bias)
    cdma("pool", ovl, c_ovl)
    cdma("sp", Lt, c_L)
    cdma("sp", keepT, c_keep)
    cdma("sp", addT, c_add)
    cdma("sp", p128, c_p128)
    cdma("sp", pos_all, c_pos_all)
    cdma("sp", pos_own, c_pos_own)
    cdma("sp", pos_cmp, c_pos_cmp)
    cdma("sp", gsub, dng.broadcast_to([128, 128]))
    for k in range(4):
        cdma("sp", lamv[:, k, :], lam4[k:k + 1, :].broadcast_to([128, 64]))
    sc.barrier()

    cast_q = []

    def wcast_cols(res, c0, c1, grp):
        for r in range(4):
            cast_q.append((grp, (lambda r=r, c0=c0, c1=c1, res=res: dma(
                "pool", win_bf[r * 512:(r + 1) * 512, c0:c1], w_in[r * 512:(r + 1) * 512, c0:c1],
                (), (res,), res, serialize=False, bg=True))))

    R_wcA = [Res(f"wcA{u}") for u in range(4)]
    R_wcB = Res("wcB")
    R_wcC = Res("wcC")
    R_wcT = Res("wcT")
    for u in range(4):
        for nm in ("ak", "av", "aq"):
            wcast_cols(R_wcA[u], OFF[nm] + 256 * u, OFF[nm] + 256 * u + 256, f"A{u}")
    wcast_cols(R_wcB, OFF["bq"], OFF["bz"], "B")
    wcast_cols(R_wcB, OFF["bgate"], OFF["mgate"], "B")
    for k in range(2):
        for r in range(4):
            cast_q.append(("B", (lambda k=k, r=r: dma(
                "pool", w1_bf[k, r * 1024:(r + 1) * 1024, :], w1[k, r * 1024:(r + 1) * 1024, :],
                (), (R_wcC,), R_wcC, serialize=False, bg=True))))
        cast_q.append(("B", (lambda k=k: dma("pool", w2_bf[k], w2[k], (), (R_wcC,), R_wcC,
                                               serialize=False, bg=True))))
    wcast_cols(R_wcT, OFF["az"], OFF["bq"], "T")
    wcast_cols(R_wcT, OFF["bz"], OFF["bgate"], "T")
    for q4 in range(4):
        wcast_cols(R_wcT, OFF["mgate"] + 1024 * q4, OFF["mgate"] + 1024 * q4 + 1024, "T")
    for r in range(4):
        cast_q.append(("T", (lambda r=r: dma("pool", wbr_bf[r * 512:(r + 1) * 512, :],
                                               w_br[r * 512:(r + 1) * 512, :], (), (R_wcT,), R_wcT,
                                               serialize=False, bg=True))))
        cast_q.append(("T", (lambda r=r: dma("pool", wout_bf[r * 512:(r + 1) * 512, :],
                                               w_out[r * 512:(r + 1) * 512, :], (), (R_wcT,), R_wcT,
                                               serialize=False, bg=True))))

    def drip(n):
        for _ in range(n):
            if cast_q:
                cast_q.pop(0)[1]()

    def flush_casts(grp):
        last = -1
        for k, (g_, _) in enumerate(cast_q):
            if g_ == grp:
                last = k
        drip(last + 1)

    flush_casts("A0")

    R_misc = Res("misc")
    memset("dve", ones_f, 1.0, (R_misc,))
    ts("dve", gsub, gsub, 0.8, None, ALU.mult, None, (R_const,), (R_misc,))
    junk64 = ar.alloc(F32, 64)
    for k in range(2):
        tt("dve", junk64, lamv[:, 2 * k, :], lamv[:, 2 * k + 1, :], ALU.mult, (R_const,), (R_misc,))
        rsum(small[:, k:k + 1], junk64, (R_misc,), (R_misc,))
    act(small[:, 2:4], small[:, 0:2], AF.Exp, (R_misc,), (R_misc,))
    tt("dve", small[:, 4:5], small[:, 2:3], small[:, 3:4], ALU.subtract, (R_misc,), (R_misc,))
    ts("dve", small[:, 5:6], small[:, 4:5], 0.2, None, ALU.add, None, (R_misc,), (R_misc,))
    ts("dve", small[:, 6:7], small[:, 5:6], -1.0, None, ALU.mult, None, (R_misc,), (R_misc,))
    neglam = small[:, 6:7]

    C1 = 6.28125
    C2 = 2 * PI - C1

    R_rope = Res("rope")

    def rope_gen(tmpl):
        rw = (R_rope,)
        tables = ((pos_all, NT, 8, 16, cosA, sinA), (pos_own, NO, 8, 16, cosAo, sinAo),
                  (pos_all, NT, 16, 32, cosB, sinB), (pos_own, NO, 16, 32, cosBo, sinBo),
                  (pos_cmp, NCT, 16, 32, cosC, sinC))
        for (pos, n, half, rd, cosT, sinT) in tables:
            ang, a, kf, r, msk = [t[:, 0:n * half].rearrange("p (n h) -> p n h", h=half) for t in tmpl]
            ki = a.bitcast(mybir.dt.int32)
            for j in range(half):
                inv = float(np.float32(1.0) / np.float32(THETA) ** (np.float32(j) * np.float32(2.0 / rd)))
                ts("dve", ang[:, :, j], pos, inv, None, ALU.mult, None, (R_const,) + rw, rw)
                yield
            for (dst, shift) in ((sinT, 0.0), (cosT, PI / 2)):
                ts("dve", r, ang, shift, None, ALU.add, None, rw, rw)
                yield
                ts("dve", kf, r, 1.0 / (2 * PI), None, ALU.mult, None, rw, rw)
                yield
                cp("dve", ki, kf, rw, rw)
                yield
                cp("dve", kf, ki, rw, rw)
                yield
                stt(r, kf, -C1, r, ALU.mult, ALU.add, rw, rw)
                yield
                stt(r, kf, -C2, r, ALU.mult, ALU.add, rw, rw)
                yield
                ts("dve", msk, r, PI, None, ALU.is_gt, None, rw, rw)
                yield
                stt(r, msk, -2 * PI, r, ALU.mult, ALU.add, rw, rw)
                yield
                ts("dve", msk, r, -PI, None, ALU.is_lt, None, rw, rw)
                yield
                stt(r, msk, 2 * PI, r, ALU.mult, ALU.add, rw, rw)
                yield
                ts("dve", r, r, PI, -PI, ALU.min, ALU.max, rw, rw)
                yield
                act(dst, r, AF.Sin, rw, rw)
                yield

    C1 = 6.28125
    C2 = 2 * PI - C1

    epsc = ar.alloc(F32, 1)
    memset("dve", epsc, EPS, (R_misc,))

    pm = ar.mark()
    A_bc = ar.alloc(F32, D)
    B_bc = ar.alloc(F32, D)
    pm2 = ar.mark()
    G_bc = ar.alloc(F32, D)
    R_G = Res("G")
    ng_bc = ar.alloc(F32, D)
    cts = ar.alloc(F32, KC)
    scv = ar.alloc(F32, KC)
    sc_rep = ar.alloc(F32, KC, 128)
    brow = ar.alloc(F32, 3 * D)
    wada = [ar.alloc(F32, KC, 512) for _ in range(2)]
    R_wada = [Res("wada0"), Res("wada1")]
    R_ada = Res("ada")
    dma("sp", cts, cT, (), (R_ada,), R_ada)
    dma("sp", ng_bc, norm_g.broadcast_to([128, D]), (), (R_ada,), R_ada)
    dma("sp", brow[0:1, :], b_ada, (), (R_ada,), R_ada)
    act(scv, cts, AF.Silu, (R_ada,), (R_ada,))
    cp("dve", sc_rep, scv.unsqueeze(2).broadcast_to([128, KC, 128]), (R_ada,), (R_ada,))
    w_ada_v = w_ada.rearrange("(kc p) c -> p kc c", p=128)
    for cb in range(12):
        sl = cb % 2
        dma("sp", wada[sl], w_ada_v[:, :, cb * 512:(cb + 1) * 512], (), (R_wada[sl],), R_wada[sl])
        bk = cb % 2
        for kc in range(KC):
            mm(ps[bk], sc_rep[:, kc, :], wada[sl][:, kc, :], kc == 0, False,
               (R_ada, R_wada[sl]), (PB[bk],), signal=False)
        mm(ps[bk], ones_f[0:1, :], brow[0:1, cb * 512:(cb + 1) * 512], False, True,
           (R_ada, R_misc), (PB[bk],))
        c0 = (cb % 4) * 512
        if cb < 4:
            cp("act", B_bc[:, c0:c0 + 512], ps[bk], (PB[bk],), (PB[bk], R_ada))
        elif cb < 8:
            stt(A_bc[:, c0:c0 + 512], ps[bk], 1.0, ng_bc[:, c0:c0 + 512], ALU.add, ALU.mult,
                (PB[bk], R_ada), (PB[bk], R_ada))
        else:
            cp("act", G_bc[:, c0:c0 + 512], ps[bk], (PB[bk],), (PB[bk], R_G))
    dma("sp", G_d, G_bc, (R_G,), (R_G,), R_G)

    sc.barrier()
    ar.release(pm2)
    NX = 3
    xbuf = [ar.alloc(F32, D) for _ in range(NX)]
    R_x = [Res(f"x{k}") for k in range(NX)]
    junkb = ar.alloc(BF16, D)
    R_junk = Res("junk")
    tmpf = [ar.alloc(F32, D) for _ in range(2)]
    R_tmp = [Res("tmp0"), Res("tmp1")]
    hb = [ar.alloc(BF16, D) for _ in range(2)]
    R_hb = [Res("hb0"), Res("hb1")]
    R_hb2 = [Res("hb0b"), Res("hb1b")]
    hTs = [ar.alloc(BF16, D) for _ in range(2)]
    R_hTs = [Res("hTs0"), Res("hTs1")]
    ssv = [ar.alloc(F32, 2) for _ in range(NX)]
    R_hTd = Res("hT_d")

    hitems = [(x_all, t, hT_d) for t in range(NT)] + [(x_own, t, hTo_d) for t in range(NO)]

    def h_load(n):
        xsrc, t, dst = hitems[n]
        k3 = n % NX
        dma("sp", xbuf[k3], xsrc[t * 128:(t + 1) * 128, :], (), (R_x[k3],), R_x[k3])

    def h_front(n):
        k3, k2 = n % NX, n % 2
        act(junkb, xbuf[k3], AF.Square, (R_x[k3],), (R_junk, R_x[k3]), accum_out=ssv[k3][:, 0:1])
        act(ssv[k3][:, 1:2], ssv[k3][:, 0:1], AF.Sqrt, (R_x[k3], R_misc), (R_x[k3],), bias=epsc, scale=1.0 / D)
        sc.op("dve", (lambda v=ssv[k3]: lambda e: e.reciprocal(v[:, 1:2], v[:, 1:2]))(), (R_x[k3],), (R_x[k3],))
        stt(tmpf[k2], xbuf[k3], ssv[k3][:, 1:2], A_bc, ALU.mult, ALU.mult,
            (R_x[k3], R_ada), (R_tmp[k2],))
        tt("pool", hb[k2][:, 0:1152], tmpf[k2][:, 0:1152], B_bc[:, 0:1152], ALU.add,
           (R_tmp[k2], R_ada), (R_hb[k2],))
        tt("dve", hb[k2][:, 1152:2048], tmpf[k2][:, 1152:2048], B_bc[:, 1152:2048], ALU.add,
           (R_tmp[k2], R_ada), (R_hb2[k2],))

    def h_back(n):
        xsrc, t, dst = hitems[n]
        k2 = n % 2
        b0, b1 = 2 + 2 * k2, 3 + 2 * k2
        for kc in range(KC):
            bk = b0 if kc < 8 else b1
            tr(psb[bk][:, (kc % 8) * 128:(kc % 8 + 1) * 128], hb[k2][:, kc * 128:(kc + 1) * 128],
               ident, (R_hb[k2], R_hb2[k2], R_const), (PB[bk],), signal=(kc % 8 == 7))
        cp("act", hTs[k2][:, 0:1024], psb[b0], (PB[b0],), (PB[b0], R_hTs[k2]))
        cp("act", hTs[k2][:, 1024:2048], psb[b1], (PB[b1],), (PB[b1], R_hTs[k2]))
        dma("act", dst[t], hTs[k2], (R_hTs[k2],), (R_hTd,), R_hTs[k2])

    rope_tmp = [ar.alloc(F32, NT * 16) for _ in range(5)]
    rgen = rope_gen(rope_tmp)
    h_load(0)
    h_load(1)
    h_front(0)
    for n in range(len(hitems)):
        if n + 2 < len(hitems):
            h_load(n + 2)
        if n + 1 < len(hitems):
            h_front(n + 1)
        for _ in range(2):
            next(rgen, None)
        h_back(n)
    for _ in rgen:
        pass
    ar.release(pm)
    sc.barrier()
    sc.recycle([R_wada[0], R_wada[1], R_ada] + R_x + R_hTs)

    if debug and debug.get("stop") == "prelude":
        return finish(nc, sc, es, out)

    win_v = win_bf.rearrange("(kc p) c -> p kc c", p=128)
    hT_v = hT_d.rearrange("t p (kc k) -> t p kc k", k=128)
    hTo_v = hTo_d.rearrange("t p (kc k) -> t p kc k", k=128)

    def rope_evac(psrc, dstb, nh, dh, half, cosT, sinT, reads, writes, tmp):
        pv = psrc.rearrange("p (h d) -> p h d", d=dh)
        dv = dstb.rearrange("p (h d) -> p h d", d=dh)
        x1 = pv[:, :, 0:half]
        x2 = pv[:, :, half:2 * half]
        cb_ = cosT.unsqueeze(1).broadcast_to([128, nh, half])
        sb_ = sinT.unsqueeze(1).broadcast_to([128, nh, half])
        t1 = tmp[:, 0, :].rearrange("p (h d) -> p h d", d=half)
        t2 = tmp[:, 1, :].rearrange("p (h d) -> p h d", d=half)
        t3 = tmp[:, 2, :].rearrange("p (h d) -> p h d", d=half)
        t4 = tmp[:, 3, :].rearrange("p (h d) -> p h d", d=half)
        cp("dve", dstb, psrc, reads, writes)
        tt("dve", t1, x1, cb_, ALU.mult, reads, writes)
        tt("dve", t2, x2, sb_, ALU.mult, reads, writes)
        tt("dve", t3, x2, cb_, ALU.mult, reads, writes)
        tt("dve", t4, x1, sb_, ALU.mult, reads, writes)
        tt("dve", dv[:, :, 0:half], t1, t2, ALU.subtract, reads, writes)
        tt("dve", dv[:, :, half:2 * half], t3, t4, ALU.add, reads, writes)

    R_oa = Res("oa_d")
    am = ar.mark()
    for u in range(4 if not (debug and debug.get("skipA")) else 0):
        um = ar.mark()
        flush_casts(f"A{u}")
        kT = ar.alloc(BF16, 2, S)
        V = ar.alloc(BF16, NT, 2, 130)
        qTp = ar.alloc(BF16, 2, NO, 2, 128)
        R_kT = [Res(f"kT{t}") for t in range(NT)]
        R_V = [Res(f"V{t}") for t in range(NT)]
        R_q = [Res(f"q{t}") for t in range(NO)]
        R_unit = Res("unit")
        memset("pool", V[:, :, :, 128:130], 1.0, R_V)
        memset("pool", qTp, 0.0, R_q)
        pm = ar.mark()
        W = ar.alloc(BF16, KC, 768)
        R_W = Res("W")
        for (c0, dst0) in ((OFF["ak"] + 256 * u, 0), (OFF["av"] + 256 * u, 256), (OFF["aq"] + 256 * u, 512)):
            dma("sp", W[:, :, dst0:dst0 + 256], win_v[:, :, c0:c0 + 256], (R_wcA[u],), (R_W,), R_W)
        NH = 3
        hTb = [ar.alloc(BF16, KC, 128) for _ in range(NH)]
        R_hTb = [Res(f"hTb{k}") for k in range(NH)]
        kb = [ar.alloc(BF16, 256) for _ in range(2)]
        R_kb = [Res("kb0"), Res("kb1")]
        rtmp = [ar.alloc(F32, 4, 32) for _ in range(2)]
        items = [("kv", t) for t in range(NT)] + [("q", i) for i in range(NO)]

        def ld(n):
            kind, idx = items[n]
            k3 = n % NH
            dma("sp", hTb[k3], (hT_v if kind == "kv" else hTo_v)[idx], (R_hTd,), (R_hTb[k3],), R_hTb[k3])

        def st_a(n):
            kind, idx = items[n]
            k3, bk = n % NH, n % 2
            if kind == "kv":
                for kc in range(KC):
                    mm(ps[bk], hTb[k3][:, kc, :], W[:, kc, 0:512], kc == 0, kc == KC - 1,
                       (R_hTb[k3], R_W), (PB[bk],), signal=(kc == KC - 1))
            else:
                for kc in range(KC):
                    mm(ps[bk][:, 0:256], hTb[k3][:, kc, :], W[:, kc, 512:768], kc == 0, kc == KC - 1,
                       (R_hTb[k3], R_W), (PB[bk],), signal=(kc == KC - 1))

        def st_b(n):
            kind, idx = items[n]
            bk, k2 = n % 2, n % 2
            if kind == "kv":
                t = idx
                cp("act", V[:, t, :, 0:128], ps[bk][:, 256:512].rearrange("p (h d) -> p h d", d=128),
                   (PB[bk],), (PB[bk], R_V[t]))
                rope_evac(ps[bk][:, 0:256], kb[k2], 4, 64, 8, cosA[:, t, :], sinA[:, t, :],
                          (PB[bk], R_misc), (PB[bk], R_kb[k2]), rtmp[k2])
            else:
                i = idx
                rope_evac(ps[bk][:, 0:256], kb[k2], 4, 64, 8, cosAo[:, i, :], sinAo[:, i, :],
                          (PB[bk], R_misc), (PB[bk], R_kb[k2]), rtmp[k2])

        def st_c(n):
            kind, idx = items[n]
            k2 = n % 2
            tb_ = 2 + k2
            for h in range(2):
                tr(psb[tb_][:, h * 128:(h + 1) * 128], kb[k2][:, h * 128:(h + 1) * 128], ident,
                   (R_kb[k2], R_const), (PB[tb_],), signal=(h == 1))
            if kind == "kv":
                t = idx
                cp("act", kT[:, :, t * 128:(t + 1) * 128],
                   psb[tb_][:, 0:256].rearrange("p (h k) -> p h k", k=128), (PB[tb_],), (PB[tb_], R_kT[t]))
            else:
                i = idx
                for h in range(2):
                    cp("act", qTp[0:64, h, i, 0, :], psb[tb_][0:64, h * 128:(h + 1) * 128],
                       (PB[tb_],), (PB[tb_], R_q[i]))
                    cp("act", qTp[64:128, h, i, 1, :], psb[tb_][64:128, h * 128:(h + 1) * 128],
                       (PB[tb_],), (PB[tb_], R_q[i]))

        ld(0)
        ld(1)
        st_a(0)
        for n in range(len(items)):
            if n + 2 < len(items):
                ld(n + 2)
            if n + 1 < len(items):
                st_a(n + 1)
            st_b(n)
            st_c(n)
        ar.release(pm)
        pm = ar.mark()
        NP = 4
        Pt = [ar.alloc(BF16, 512) for _ in range(NP)]
        R_P = [Res(f"P{k}") for k in range(NP)]
        fin = [ar.alloc(F32, 8) for _ in range(2)]
        o1 = [ar.alloc(F32, 128) for _ in range(2)]
        o2 = [ar.alloc(F32, 128) for _ in range(2)]
        ost = [ar.alloc(F32, 2, 128) for _ in range(2)]
        R_fin = [Res("fin0"), Res("fin1")]
        R_ost = [Res("ost0"), Res("ost1")]
        steps = [(i, kt) for i in range(NO) for kt in range(2 * i + 2)]
        pending = []
        fin2 = [[ar.alloc(F32, 8) for _ in range(2)] for _ in range(2)]
        o1b = [[ar.alloc(F32, 128) for _ in range(2)] for _ in range(2)]
        o2b = [[ar.alloc(F32, 128) for _ in range(2)] for _ in range(2)]
        R_fin2 = [[Res("f00"), Res("f01")], [Res("f10"), Res("f11")]]

        def acc_of(i):
            a2 = i % 2
            abk = (3 + 2 * a2, 4 + 2 * a2)
            return abk, [ps[abk[h]][:, 0:260].rearrange("p (m d) -> p m d", d=130) for h in range(2)]

        def emit_s(n):
            i, kt = steps[n]
            sb_ = n % 3
            last2 = kt >= 2 * i
            for h in range(2):
                mm(ps[sb_][:, h * 256:(h + 1) * 256], kT[:, h, kt * 128:(kt + 1) * 128],
                   qTp[:, h, i, :, :].rearrange("p m k -> p (m k)"), h == 0, (h == 1 and not last2),
                   (R_kT[kt], R_q[i]), (PB[sb_],), signal=(h == 1 and not last2))
            if last2:
                bt = 0 if kt == 2 * i else 1
                mm(ps[sb_], ident, biasT[:, bt, :], False, True, (R_const,), (PB[sb_],))

        emit_s(0)
        emit_s(1)
        for n, (i, kt) in enumerate(steps):
            nkt = 2 * i + 2
            os_ = i % 2
            if kt == 0:
                drip(2)
            if n + 2 < len(steps):
                emit_s(n + 2)
            abk, accs = acc_of(i)
            sb_ = n % 3
            pk = n % NP
            act(Pt[pk], ps[sb_], AF.Exp, (PB[sb_],), (PB[sb_], R_P[pk]), scale=0.125)
            for h in range(2):
                for m in range(2):
                    c0 = h * 256 + m * 128
                    mm(accs[h][:, m, 0:129], Pt[pk][:, c0:c0 + 128], V[:, kt, h, 0:129],
                       (kt == 0 and m == 0), kt == nkt - 1, (R_P[pk], R_V[kt]), (PB[abk[0]], PB[abk[1]]),
                       signal=(h == 1 and m == 1 and kt == nkt - 1))
            for (due, fn_) in list(pending):
                if due <= n:
                    pending.remove((due, fn_))
                    fn_()
            if kt != nkt - 1:
                continue
            for h in range(2):
                acc = accs[h]
                ab = abk[h]
                f = fin2[os_][h]
                oo1, oo2 = o1b[os_][h], o2b[os_][h]
                rd_, wr_ = (PB[ab], R_fin2[os_][h], R_misc), (PB[ab], R_fin2[os_][h])
                sc.op("dve", (lambda f=f, acc=acc: lambda e: e.reciprocal(f[:, 0:2], acc[:, :, 128]))(), rd_, wr_)
                tt("dve", f[:, 2:3], f[:, 1:2], neglam, ALU.mult, rd_, wr_)
                ts("dve", oo2, acc[:, 1, 0:128], f[:, 2:3], None, ALU.mult, None, rd_, wr_)
                stt(oo1, acc[:, 0, 0:128], f[:, 0:1], oo2, ALU.mult, ALU.add, rd_, wr_)
                tt("dve", oo2, oo1, oo1, ALU.mult, rd_[1:], wr_[1:])
                rsum(f[:, 3:4], oo2, rd_[1:], wr_[1:])
                ts("dve", f[:, 4:5], f[:, 3:4], 1.0 / 128, EPS, ALU.mult, ALU.add, rd_[1:], wr_[1:])

            def fin_part2(i=i, os_=os_):
                for h in range(2):
                    f = fin2[os_][h]
                    rr = (R_fin2[os_][h], R_misc)
                    act(f[:, 5:6], f[:, 4:5], AF.Ln, rr, (R_fin2[os_][h],))
                    act(f[:, 6:7], f[:, 5:6], AF.Exp, rr, (R_fin2[os_][h],), scale=-0.5)
                    stt(ost[os_][:, h, :], o1b[os_][h], f[:, 6:7], gsub, ALU.mult, ALU.mult,
                        rr + (R_ost[os_],), (R_fin2[os_][h], R_ost[os_]))
                dma("pool", oa_d[i][:, u * 256:(u + 1) * 256], ost[os_].rearrange("p h d -> p (h d)"),
                    (R_ost[os_],), (R_oa,), R_ost[os_])

            pending.append((n + 4, fin_part2))
        for (_, fn_) in pending:
            fn_()
        ar.release(pm)
        ar.release(um)
        sc.barrier()
        sc.recycle([R_W] + R_hTb + R_ost)

    ar.release(am)
    if debug and debug.get("stop") == "A":
        return finish(nc, sc, es, out)

    sB = 128.0 ** -0.5
    flush_casts("B")
    R_ob = Res("ob_d")
    bm = ar.mark()
    ones_ff = ar.alloc(F32, 128)
    memset("dve", ones_ff, 1.0, (R_misc,))
    hT_blk = hT_d.rearrange("t p (kc k) -> p t kc k", k=128)
    for g in range(2):
        um = ar.mark()
        kcT = ar.alloc(BF16, NCT * 128)
        vcx = ar.alloc(BF16, NCT, 128)
        R_kc = Res("kcT")
        R_vc = Res("vcx")
        pm = ar.mark()
        rawT = [ar.alloc(BF16, S) for _ in range(2)]
        R_raw = [Res("rawk"), Res("rawv")]
        Wc = ar.alloc(BF16, KC, 256)
        R_Wc = Res("Wc")
        dma("sp", Wc[:, :, 0:128], win_v[:, :, OFF["bkc"] + 128 * g:OFF["bkc"] + 128 * g + 128],
            (R_wcB,), (R_Wc,), R_Wc)
        dma("sp", Wc[:, :, 128:256], win_v[:, :, OFF["bvc"] + 128 * g:OFF["bvc"] + 128 * g + 128],
            (R_wcB,), (R_Wc,), R_Wc)
        w1s = [ar.alloc(BF16, 32, 256) for _ in range(2)]
        w2s = [ar.alloc(BF16, 2, 128) for _ in range(2)]
        peb = ar.alloc(BF16, 2, 32)
        R_cw = Res("cmpw")
        for k in range(2):
            dma("sp", w1s[k], w1_bf[k].rearrange("(t d) h -> d t h", d=128), (R_wcC,), (R_cw,), R_cw)
            dma("sp", w2s[k], w2_bf[k].rearrange("(j p) d -> p j d", p=128), (R_wcC,), (R_cw,), R_cw)
        R_cwp = Res("cmpwp")
        dma("pool", peb, peT, (), (R_cwp,), R_cwp)
        hblk = [ar.alloc(BF16, 4, KC, 128) for _ in range(2)]
        R_hblk = [Res("hblk0"), Res("hblk1")]
        nev = 0
        for tb in range(NT // 4):
            k2 = tb % 2
            dma("sp", hblk[k2], hT_blk[:, 4 * tb:4 * tb + 4], (R_hTd,), (R_hblk[k2],), R_hblk[k2])
            for which in range(2):
                bk = nev % 2
                nev += 1
                for kc in range(KC):
                    mm(ps[bk], Wc[:, kc, which * 128:(which + 1) * 128], hblk[k2][:, :, kc, :],
                       kc == 0, kc == KC - 1, (R_Wc, R_hblk[k2]), (PB[bk],), signal=(kc == KC - 1))
                cp("act" if which == 0 else "dve", rawT[which][:, tb * 512:(tb + 1) * 512], ps[bk],
                   (PB[bk],), (PB[bk], R_raw[which]))
        bias_h = ar.alloc(F32, 2)
        hidT = [ar.alloc(BF16, 512) for _ in range(2)]
        R_hid = Res("hid")
        kcb = ar.alloc(BF16, 128)
        rtc = ar.alloc(F32, 4, 16)
        R_kcb = Res("kcb")
        for which in range(2):
            for j in range(2):
                for t in range(32):
                    mm(ps[2][:, j:j + 1], w1s[which][:, t, j * 128:(j + 1) * 128], peb[:, which, t:t + 1],
                       t == 0, t == 31, (R_cw, R_cwp), (PB[2],), signal=(t == 31))
            cp("dve", bias_h, ps[2][:, 0:2], (PB[2],), (PB[2], R_hid))
            memset("dve", hidT[0], 0.0, (R_hid,))
            memset("dve", hidT[1], 0.0, (R_hid,))
            for j in range(2):
                bk = 3 + j
                for t in range(32):
                    mm(ps[bk][:, 0:NCMP], w1s[which][:, t, j * 128:(j + 1) * 128],
                       strided(rawT[which], t, 16, NCMP), t == 0, t == 31,
                       (R_cw, R_raw[which]), (PB[bk],), signal=(t == 31))
                act(hidT[j][:, 0:NCMP], ps[bk][:, 0:NCMP], AF.Silu, (PB[bk], R_hid), (PB[bk], R_hid),
                    bias=bias_h[:, j:j + 1])
            for ct in range(NCT):
                bk = 5 + ct % 2
                for j in range(2):
                    mm(ps[bk][:, 0:128], hidT[j][:, ct * 128:(ct + 1) * 128], w2s[which][:, j, :],
                       j == 0, j == 1, (R_hid, R_cw), (PB[bk],), signal=(j == 1))
                if which == 0:
                    rope_evac(ps[bk][:, 0:128], kcb, 1, 128, 16, cosC[:, ct, :], sinC[:, ct, :],
                              (PB[bk], R_misc), (PB[bk], R_kcb), rtc)
                    tr(psb[7][:, 0:128], kcb, ident, (R_kcb, R_const), (PB[7],))
                    cp("act", kcT[:, ct * 128:(ct + 1) * 128], psb[7][:, 0:128], (PB[7],), (PB[7], R_kc))
                else:
                    cp("act", vcx[:, ct, :], ps[bk][:, 0:128], (PB[bk],), (PB[bk], R_vc))
        ar.release(pm)
        sc.barrier()
        sc.recycle([R_Wc, R_cw, R_cwp] + R_hblk)
        qT = ar.alloc(BF16, 4, NO * 128)
        ksT = ar.alloc(BF16, S)
        kwT = ar.alloc(BF16, S)
        Vs = ar.alloc(BF16, NT, 130)
        Vw = ar.alloc(BF16, NT, 130)
        gts = ar.alloc(F32, NO, 12)
        Et = ar.alloc(BF16, S)
        R_K = [Res(f"K{t}") for t in range(NT)]
        R_Vt = [Res(f"Vt{t}") for t in range(NT)]
        R_q = [Res(f"q{t}") for t in range(NO)]
        R_E = Res("E")
        dma("pool", Et, c_E, (), (R_E,), R_E)
        memset("pool", Vs[:, :, 128:130], 1.0, R_Vt)
        memset("pool", Vw[:, :, 128:130], 1.0, R_Vt)
        pm = ar.mark()
        W2 = ar.alloc(BF16, KC, 1040)
        R_W = Res("W2")
        for (nm, d0, wdt) in (("bks", 0, 128), ("bvs", 128, 128), ("bkw", 256, 128), ("bvw", 384, 128)):
            c0 = OFF[nm] + 128 * g
            dma("sp", W2[:, :, d0:d0 + wdt], win_v[:, :, c0:c0 + wdt], (R_wcB,), (R_W,), R_W)
        c0 = OFF["bq"] + 512 * g
        dma("sp", W2[:, :, 512:1024], win_v[:, :, c0:c0 + 512], (R_wcB,), (R_W,), R_W)
        c0 = OFF["bgate"] + 12 * g
        dma("sp", W2[:, :, 1024:1036], win_v[:, :, c0:c0 + 12], (R_wcB,), (R_W,), R_W)
        NH = 2
        hTb = [ar.alloc(BF16, KC, 128) for _ in range(NH)]
        R_hTb = [Res(f"hTb{k}") for k in range(NH)]
        kb = [ar.alloc(BF16, 512) for _ in range(2)]
        R_kb = [Res("kb0"), Res("kb1")]
        rtmp = [ar.alloc(F32, 4, 64) for _ in range(2)]
        items = [("kv", t) for t in range(NT)] + [("q", i) for i in range(NO)]

        def ld(n):
            kind, idx = items[n]
            k3 = n % NH
            dma("sp", hTb[k3], (hT_v if kind == "kv" else hTo_v)[idx], (R_hTd,), (R_hTb[k3],), R_hTb[k3])

        def st_a(n):
            kind, idx = items[n]
            k3, bk = n % NH, n % 2
            c0 = 0 if kind == "kv" else 512
            for kc in range(KC):
                mm(ps[bk], hTb[k3][:, kc, :], W2[:, kc, c0:c0 + 512], kc == 0, kc == KC - 1,
                   (R_hTb[k3], R_W), (PB[bk],), signal=(kc == KC - 1))
            if kind == "q":
                gb_ = 4 + n % 2
                for kc in range(KC):
                    mm(ps[gb_][:, 0:12], hTb[k3][:, kc, :], W2[:, kc, 1024:1036], kc == 0, kc == KC - 1,
                       (R_hTb[k3], R_W), (PB[gb_],), signal=(kc == KC - 1))

        def st_b(n):
            kind, idx = items[n]
            bk, k2 = n % 2, n % 2
            if kind == "kv":
                t = idx
                rope_evac(ps[bk], kb[k2], 2, 256, 16, cosB[:, t, :], sinB[:, t, :],
                          (PB[bk], R_misc), (PB[bk], R_kb[k2]), rtmp[k2][:, :, 0:32])
                cp("pool", Vs[:, t, 0:128], kb[k2][:, 128:256], (R_kb[k2],), (R_Vt[t],))
                cp("pool", Vw[:, t, 0:128], kb[k2][:, 384:512], (R_kb[k2],), (R_Vt[t],))
            else:
                i = idx
                gb_ = 4 + n % 2
                act(gts[:, i, :], ps[gb_][:, 0:12], AF.Sigmoid, (PB[gb_],), (PB[gb_], R_q[i]))
                rope_evac(ps[bk], kb[k2], 4, 128, 16, cosBo[:, i, :], sinBo[:, i, :],
                          (PB[bk], R_misc), (PB[bk], R_kb[k2]), rtmp[k2])

        def st_c(n):
            kind, idx = items[n]
            k2 = n % 2
            tb_ = 2 + k2
            if kind == "kv":
                t = idx
                tr(psb[tb_][:, 0:128], kb[k2][:, 0:128], ident, (R_kb[k2], R_const), (PB[tb_],), signal=False)
                tr(psb[tb_][:, 128:256], kb[k2][:, 256:384], ident, (R_kb[k2], R_const), (PB[tb_],))
                cp("act", ksT[:, t * 128:(t + 1) * 128], psb[tb_][:, 0:128], (PB[tb_],), (PB[tb_], R_K[t]))
                cp("act", kwT[:, t * 128:(t + 1) * 128], psb[tb_][:, 128:256], (PB[tb_],), (PB[tb_], R_K[t]))
            else:
                i = idx
                for h in range(4):
                    tr(psb[tb_][:, h * 128:(h + 1) * 128], kb[k2][:, h * 128:(h + 1) * 128], ident,
                       (R_kb[k2], R_const), (PB[tb_],), signal=(h == 3))
                cp("act", qT[:, :, i * 128:(i + 1) * 128],
                   psb[tb_][:, 0:512].rearrange("p (h k) -> p h k", k=128), (PB[tb_],), (PB[tb_], R_q[i]))

        ld(0)
        st_a(0)
        for n in range(len(items)):
            if n + 1 < len(items):
                ld(n + 1)
                st_a(n + 1)
            st_b(n)
            st_c(n)
        ar.release(pm)
        sc.barrier()
        sc.recycle([R_W] + R_hTb)
        pm = ar.mark()
        NP = 4
        Pt = [ar.alloc(BF16, 512) for _ in range(NP)]
        R_P = [Res(f"P{k}") for k in range(NP)]
        e_t = [ar.alloc(F32, 512) for _ in range(NCT)]
        R_e = [Res(f"e{k}") for k in range(NCT)]
        ef = ar.alloc(F32, 512)
        cm = ar.alloc(F32, 128)
        zr = ar.alloc(F32, 512)
        R_z = Res("z")
        pcn = [ar.alloc(BF16, 512) for _ in range(2)]
        R_pcn = [Res("pcn0"), Res("pcn1")]
        obuf = [ar.alloc(F32, 4, 128) for _ in range(2)]
        R_obuf = [Res("obuf0"), Res("obuf1")]
        impa = ar.alloc(F32, 128)
        wk = ar.alloc(F32, 128)
        m8 = ar.alloc(F32, 16)
        thr = ar.alloc(F32, 1)
        negsel = ar.alloc(BF16, 128)
        nsT = ar.alloc(BF16, 4, 128)
        R_top = Res("top")
        R_ns = Res("nsT")
        fin = ar.alloc(F32, 8)
        R_fin = Res("finB")
        gts4 = gts.rearrange("p i (h c) -> p i h c", c=3)
        npt = 0
        npc = 0

        def branch(i, os_, kts, Kmat, Vmat, accb, gidx, bias_of, with_sel):
            nonlocal npt
            qi = qT[:, :, i * 128:(i + 1) * 128]
            accs = [ps[accb[0]][:, 0:260].rearrange("p (m d) -> p m d", d=130),
                    ps[accb[1]][:, 0:260].rearrange("p (m d) -> p m d", d=130)]
            npt0 = npt
            npt += len(kts)

            def emit_s(n_):
                kt = kts[n_]
                sb_ = (npt0 + n_) % 3
                bt = bias_of(kt)
                mm(ps[sb_], Kmat[:, kt * 128:(kt + 1) * 128], qi, True, False,
                   (R_K[kt], R_q[i]), (PB[sb_],), signal=False)
                if with_sel:
                    mm(ps[sb_], Et[:, kt * 128:(kt + 1) * 128], nsT.rearrange("p h k -> p (h k)"), False,
                       bt is None, (R_E, R_ns), (PB[sb_],), signal=(bt is None))
                if bt is not None:
                    mm(ps[sb_], ident, biasT[:, bt, :], False, True, (R_const,), (PB[sb_],))

            emit_s(0)
            if len(kts) > 1:
                emit_s(1)
            for n_, kt in enumerate(kts):
                if n_ + 2 < len(kts):
                    emit_s(n_ + 2)
                sb_ = (npt0 + n_) % 3
                pk = (npt0 + n_) % NP
                act(Pt[pk], ps[sb_], AF.Exp, (PB[sb_],), (PB[sb_], R_P[pk]), scale=sB)
                for h in range(4):
                    mm(accs[h // 2][:, h % 2, 0:129], Pt[pk][:, h * 128:(h + 1) * 128], Vmat[:, kt, 0:129],
                       (n_ == 0 and h % 2 == 0), n_ == len(kts) - 1, (R_P[pk], R_Vt[kt]),
                       (PB[accb[0]], PB[accb[1]]), signal=(h == 3 and n_ == len(kts) - 1))
            for b2 in range(2):
                rd_ = (PB[accb[b2]], R_fin, R_q[i], R_obuf[os_])
                wr_ = (PB[accb[b2]], R_fin, R_obuf[os_])
                acc = accs[b2]
                sc.op("dve", (lambda acc=acc: lambda e: e.reciprocal(fin[:, 0:2], acc[:, :, 128]))(), rd_, wr_)
                tt("dve", fin[:, 2:4], fin[:, 0:2], gts4[:, i, 2 * b2:2 * b2 + 2, gidx], ALU.mult, rd_, wr_)
                for hh in range(2):
                    h = 2 * b2 + hh
                    stt(obuf[os_][:, h, :], acc[:, hh, 0:128], fin[:, 2 + hh:3 + hh], obuf[os_][:, h, :],
                        ALU.mult, ALU.add, rd_, wr_)

        nsTs = [nsT, ar.alloc(BF16, 4, 128)]
        R_nss = [R_ns, Res("nsT1")]
        cstate = {}

        def cmp_a(i):
            nonlocal npt
            qi = qT[:, :, i * 128:(i + 1) * 128]
            nct = min(NCT, (16 * i + 14) // 128 + 1)
            cstate[i] = nct
            for ct in range(nct):
                sb_ = npt % 3
                npt += 1
                mm(ps[sb_], kcT[:, ct * 128:(ct + 1) * 128], qi, True, True, (R_kc, R_q[i]), (PB[sb_],))
                if 2048 * ct + 2063 <= 256 * i:
                    act(e_t[ct], ps[sb_], AF.Exp, (PB[sb_],), (PB[sb_], R_e[ct]), scale=sB)
                else:
                    act(ef, ps[sb_], AF.Exp, (PB[sb_],), (PB[sb_], R_e[ct]), scale=sB)
                    ts("dve", cm, Lt, p128[:, 0:1], float(2048 * ct - 256 * i), ALU.add, ALU.is_ge,
                       (R_const,), (R_e[ct],))
                    tt("dve", e_t[ct].rearrange("p (h k) -> p h k", k=128),
                       ef.rearrange("p (h k) -> p h k", k=128),
                       cm.unsqueeze(1).broadcast_to([128, 4, 128]), ALU.mult, (R_e[ct],), (R_e[ct],))
                mm(ps[7], ones_ff, e_t[ct], ct == 0, ct == nct - 1, (R_misc, R_e[ct]), (PB[7],),
                   signal=(ct == nct - 1))
            ts("dve", zr, ps[7], 1e-30, None, ALU.max, None, (PB[7],), (PB[7], R_z))
            sc.op("dve", lambda e: e.reciprocal(zr, zr), (R_z,), (R_z,))

        def cmp_b(i):
            nonlocal npc
            os_ = i % 2
            nct = cstate[i]
            for ct in range(nct):
                pc_ = npc % 2
                npc += 1
                tt("dve", pcn[pc_], e_t[ct], zr, ALU.mult, (R_e[ct], R_z), (R_pcn[pc_],))
                for h in range(4):
                    mm(ps[5][:, h * 128:(h + 1) * 128], pcn[pc_][:, h * 128:(h + 1) * 128], vcx[:, ct, :],
                       (ct == 0 and h == 0), ct == nct - 1, (R_pcn[pc_], R_vc), (PB[5],), signal=False)
                for h in range(4):
                    mm(ps[6][:, 0:128], pcn[pc_][:, h * 128:(h + 1) * 128], ovl[:, ct, :],
                       (ct == 0 and h == 0), (ct == nct - 1 and h == 3), (R_pcn[pc_], R_const), (PB[6], PB[5]),
                       signal=(h == 3))
            for h in range(4):
                ts("dve", obuf[os_][:, h, :], ps[5][:, h * 128:(h + 1) * 128], gts[:, i, 3 * h:3 * h + 1], None,
                   ALU.mult, None, (PB[5], R_q[i]), (PB[5], R_obuf[os_]))
            s0 = 4 * (NO - 1 - i)
            rt, wt = (PB[6], R_top, R_const), (PB[6], R_top)
            tt("dve", impa, ps[6][:, 0:128], keepT[:, s0:s0 + 128], ALU.mult, rt, wt)
            tt("dve", impa, impa, addT[:, s0:s0 + 128], ALU.add, rt, wt)
            memset("dve", impa[:, 0:1], 1.0e4, wt)
            sc.op("dve", lambda e: e.max(out=m8[:, 0:8], in_=impa), rt, wt)
            sc.op("dve", lambda e: e.match_replace(out=wk, in_to_replace=m8[:, 0:8], in_values=impa,
                                                   imm_value=-3.0e38), rt, wt)
            sc.op("dve", lambda e: e.max(out=m8[:, 8:16], in_=wk), rt, wt)
            ts("dve", thr, m8[:, 15:16], -5.0e29, None, ALU.max, None, rt, wt)
            ts("dve", negsel, impa, thr[:, 0:1], NEGB, ALU.is_lt, ALU.mult, rt, wt)

        def cmp_c(i):
            k = i % 2
            tr(psb[7][:, 0:128], negsel, ident, (R_top, R_const), (PB[7],))
            cp("act", nsTs[k], psb[7][:, 0:128].unsqueeze(1).broadcast_to([128, 4, 128]),
               (PB[7],), (PB[7], R_nss[k]))

        def tile_stream(i):
            nonlocal npt
            os_ = i % 2
            nsT_i = nsTs[i % 2]
            R_ns_i = R_nss[i % 2]
            qi = qT[:, :, i * 128:(i + 1) * 128]
            slc_k = list(range(2 * i + 2))
            win_k = [2 * i - 4 + r for r in range(6) if 2 * i - 4 + r >= 0]
            steps = [(0, kt) for kt in slc_k] + [(1, kt) for kt in win_k]
            cfg = [dict(K=ksT, V=Vs, accb=(3, 4), gidx=1, n=len(slc_k)),
                   dict(K=kwT, V=Vw, accb=(5, 6), gidx=2, n=len(win_k))]
            for c in cfg:
                c["accs"] = [ps[c["accb"][0]][:, 0:260].rearrange("p (m d) -> p m d", d=130),
                             ps[c["accb"][1]][:, 0:260].rearrange("p (m d) -> p m d", d=130)]
            npt0 = npt
            npt += len(steps)

            def bias_of(br, kt):
                if br == 0:
                    return (0 if kt == 2 * i else 1) if kt >= 2 * i else None
                return 2 + (kt - (2 * i - 4))

            def emit_s(n):
                br, kt = steps[n]
                c = cfg[br]
                sb_ = (npt0 + n) % 3
                bt = bias_of(br, kt)
                mm(ps[sb_], c["K"][:, kt * 128:(kt + 1) * 128], qi, True, False,
                   (R_K[kt], R_q[i]), (PB[sb_],), signal=False)
                if br == 0:
                    mm(ps[sb_], Et[:, kt * 128:(kt + 1) * 128], nsT_i.rearrange("p h k -> p (h k)"), False,
                       bt is None, (R_E, R_ns_i), (PB[sb_],), signal=(bt is None))
                if bt is not None:
                    mm(ps[sb_], ident, biasT[:, bt, :], False, True, (R_const,), (PB[sb_],))

            inject_at = max(0, len(slc_k) // 2)
            emit_s(0)
            emit_s(1)
            for n, (br, kt) in enumerate(steps):
                c = cfg[br]
                first = (n == 0) if br == 0 else (n == len(slc_k))
                last = (n == len(slc_k) - 1) if br == 0 else (n == len(steps) - 1)
                if n == inject_at and i + 1 < NO:
                    cmp_b(i + 1)
                if n + 2 < len(steps):
                    emit_s(n + 2)
                sb_ = (npt0 + n) % 3
                pk = (npt0 + n) % NP
                act(Pt[pk], ps[sb_], AF.Exp, (PB[sb_],), (PB[sb_], R_P[pk]), scale=sB)
                accb, accs = c["accb"], c["accs"]
                for h in range(4):
                    mm(accs[h // 2][:, h % 2, 0:129], Pt[pk][:, h * 128:(h + 1) * 128], c["V"][:, kt, 0:129],
                       (first and h % 2 == 0), last, (R_P[pk], R_Vt[kt]),
                       (PB[accb[0]], PB[accb[1]]), signal=(h == 3 and last))
                if not last:
                    continue
                for b2 in range(2):
                    rd_ = (PB[accb[b2]], R_fin, R_q[i], R_obuf[os_])
                    wr_ = (PB[accb[b2]], R_fin, R_obuf[os_])
                    acc = accs[b2]
                    sc.op("dve", (lambda acc=acc: lambda e: e.reciprocal(fin[:, 0:2], acc[:, :, 128]))(), rd_, wr_)
                    tt("dve", fin[:, 2:4], fin[:, 0:2], gts4[:, i, 2 * b2:2 * b2 + 2, c["gidx"]], ALU.mult, rd_, wr_)
                    for hh in range(2):
                        h = 2 * b2 + hh
                        stt(obuf[os_][:, h, :], acc[:, hh, 0:128], fin[:, 2 + hh:3 + hh], obuf[os_][:, h, :],
                            ALU.mult, ALU.add, rd_, wr_)

        cmp_a(0)
        cmp_b(0)
        cmp_c(0)
        for i in range(NO):
            os_ = i % 2
            if i + 1 < NO:
                cmp_a(i + 1)
            tile_stream(i)
            if i + 1 < NO:
                cmp_c(i + 1)
            dma("pool", ob_d[i][:, g * 512:(g + 1) * 512], obuf[os_].rearrange("p h d -> p (h d)"),
                (R_obuf[os_],), (R_ob,), R_obuf[os_])
        ar.release(pm)
        ar.release(um)
        sc.barrier()
        sc.recycle([R_E] + R_obuf)
    ar.release(bm)
    if debug and debug.get("stop") == "B":
        return finish(nc, sc, es, out)

    flush_casts("T")
    TBK = min(4, NO)
    NB = NO // TBK
    G_bc = ar.alloc(F32, D)
    fg_bc = ar.alloc(F32, D)
    R_tc = Res("tailconst")
    dma("sp", G_bc, G_d, (R_G,), (R_tc,), R_tc)
    dma("sp", fg_bc, fng.broadcast_to([128, D]), (), (R_tc,), R_tc)
    wbr_v = wbr_bf.rearrange("(kc p) c -> p kc c", p=128)
    wout_v = wout_bf.rearrange("(kc p) c -> p kc c", p=128)
    NWS = 6
    wsl = [ar.alloc(BF16, KC, 512) for _ in range(NWS)]
    R_ws = [Res(f"ws{k}") for k in range(NWS)]
    hTt, uT, xo = [], [], []
    for j in range(TBK):
        blk8 = ar.alloc(BF16, 2, KC, 128)
        hTt.append(blk8[:, 0])
        uT.append(blk8[:, 1])
        xo.append(blk8.rearrange("p a k c -> p (a k c)").bitcast(F32))
    ymT = [ar.alloc(BF16, KC, 128) for _ in range(TBK)]
    R_hTt = [Res(f"hTt{k}") for k in range(TBK)]
    R_uT = [Res(f"uT{k}") for k in range(TBK)]
    R_ymT = [Res(f"ymT{k}") for k in range(TBK)]
    R_xost = [Res(f"xost{k}") for k in range(TBK)]
    ar2 = Arena(arena_t, ARENA_BYTES)
    ar2.top = rope_lo
    NTS = 8
    n_in_rope = max(0, min(NTS, (rope_hi - rope_lo) // 2048))
    utmp = [ar.alloc(BF16, 512) for _ in range(2)]
    R_ut = [Res("ut0"), Res("ut1")]
    tsl = [ar2.alloc(F32, 512) for _ in range(n_in_rope)] + [ar.alloc(F32, 512) for _ in range(NTS - n_in_rope)]
    assert ar2.top <= rope_hi
    R_ts = [Res(f"ts{k}") for k in range(NTS)]
    ssf = ar.alloc(F32, 8)
    R_ss = Res("ssf")
    R_out = Res("out")
    nts = 0

    def tslot():
        nonlocal nts
        k = nts % NTS
        nts += 1
        return k

    mg0 = OFF["mgate"]
    wneeds = []
    for blk in range(NB):
        for cb in range(4):
            c0 = (OFF["az"] + 512 * cb) if cb < 2 else (OFF["bz"] + 512 * (cb - 2))
            wneeds.append([win_v[:, :, c0:c0 + 512]])
        for cb in range(4):
            wneeds.append([wbr_v[:, :, cb * 512:(cb + 1) * 512],
                           win_v[:, :, mg0 + cb * 512:mg0 + cb * 512 + 512],
                           win_v[:, :, mg0 + 2048 + cb * 512:mg0 + 2048 + cb * 512 + 512]])
        for cb in range(4):
            wneeds.append([wout_v[:, :, cb * 512:(cb + 1) * 512]])
    wslots = {}
    nws = 0

    def prefetch(idx):
        nonlocal nws
        if idx >= len(wneeds) or idx in wslots:
            return
        sl = []
        for src_ap in wneeds[idx]:
            k = nws % NWS
            nws += 1
            dma("sp", wsl[k], src_ap, (R_wcT,), (R_ws[k],), R_ws[k])
            sl.append(k)
        wslots[idx] = sl

    prefetch(0)
    widx = 0
    for blk in range(NB):
        for j in range(TBK):
            i = blk * TBK + j
            dma("sp", hTt[j], hTo_v[i], (R_hTd,), (R_hTt[j],), R_hTt[j])
        items = [(cb, j) for cb in range(4) for j in range(TBK)]
        info = {}

        def s1_a(n, blk=blk, items=items, info=info, widx=widx):
            cb, j = items[n]
            i = blk * TBK + j
            if j == 0:
                prefetch(widx + cb + 1)
            k = wslots[widx + cb][0]
            ko = tslot()
            info[n] = ko
            src = (oa_d if cb < 2 else ob_d)[i][:, (cb % 2) * 512:(cb % 2) * 512 + 512]
            dma("sp", tsl[ko], src, (R_oa, R_ob), (R_ts[ko],), R_ts[ko])
            bk = n % 2
            for kc in range(KC):
                mm(ps[bk], hTt[j][:, kc, :], wsl[k][:, kc, :], kc == 0, kc == KC - 1,
                   (R_hTt[j], R_ws[k]), (PB[bk],), signal=(kc == KC - 1))

        def s1_b(n, items=items, info=info):
            cb, j = items[n]
            bk, n2 = n % 2, n % 2
            ko = info[n]
            kz = tslot()
            act(tsl[kz], ps[bk], AF.Silu, (PB[bk],), (PB[bk], R_ts[kz]))
            tt("dve", utmp[n2], tsl[kz], tsl[ko], ALU.mult, (R_ts[kz], R_ts[ko]), (R_ut[n2],))

        def s1_c(n, items=items):
            cb, j = items[n]
            n2 = n % 2
            tb_ = 2 + n2
            for q in range(4):
                tr(psb[tb_][:, q * 128:(q + 1) * 128], utmp[n2][:, q * 128:(q + 1) * 128], ident,
                   (R_ut[n2], R_const), (PB[tb_],), signal=(q == 3))
            cp("act", uT[j][:, 4 * cb:4 * cb + 4, :],
               psb[tb_][:, 0:512].rearrange("p (q k) -> p q k", k=128), (PB[tb_],), (PB[tb_], R_uT[j]))

        s1_a(0)
        for n in range(len(items)):
            if n + 1 < len(items):
                s1_a(n + 1)
            s1_b(n)
            s1_c(n)
        widx += 4
        def s3_a(n, items=items, widx=widx):
            cb, j = items[n]
            if j == 0:
                prefetch(widx + cb + 1)
            kbr, kga, kgb = wslots[widx + cb]
            b4 = 4 * (n % 2)
            ba, bb, bc, bd = b4, b4 + 1, b4 + 2, b4 + 3
            for kc in range(8):
                mm(ps[ba], uT[j][:, kc, :], wsl[kbr][:, kc, :], kc == 0, kc == 7,
                   (R_uT[j], R_ws[kbr]), (PB[ba],), signal=(kc == 7))
            for kc in range(8, 16):
                mm(ps[bb], uT[j][:, kc, :], wsl[kbr][:, kc, :], kc == 8, kc == 15,
                   (R_uT[j], R_ws[kbr]), (PB[bb],), signal=(kc == 15))
            for kc in range(KC):
                mm(ps[bc], hTt[j][:, kc, :], wsl[kga][:, kc, :], kc == 0, kc == KC - 1,
                   (R_hTt[j], R_ws[kga]), (PB[bc],), signal=(kc == KC - 1))
            for kc in range(KC):
                mm(ps[bd], hTt[j][:, kc, :], wsl[kgb][:, kc, :], kc == 0, kc == KC - 1,
                   (R_hTt[j], R_ws[kgb]), (PB[bd],), signal=(kc == KC - 1))

        def s3_b(n, items=items):
            cb, j = items[n]
            b4 = 4 * (n % 2)
            n2 = n % 2
            ba, bb, bc, bd = b4, b4 + 1, b4 + 2, b4 + 3
            k1, k2_, k3_, k4_ = tslot(), tslot(), tslot(), tslot()
            act(tsl[k1], ps[bc], AF.Sigmoid, (PB[bc],), (PB[bc], R_ts[k1]))
            act(tsl[k2_], ps[bd], AF.Sigmoid, (PB[bd],), (PB[bd], R_ts[k2_]))
            tt("dve", tsl[k3_], tsl[k1], ps[ba], ALU.mult, (R_ts[k1], PB[ba]), (R_ts[k3_], PB[ba]))
            tt("dve", tsl[k4_], tsl[k2_], ps[bb], ALU.mult, (R_ts[k2_], PB[bb]), (R_ts[k4_], PB[bb]))
            tt("dve", utmp[n2], tsl[k3_], tsl[k4_], ALU.add, (R_ts[k3_], R_ts[k4_]), (R_ut[n2],))

        def s3_c(n, items=items):
            cb, j = items[n]
            n2 = n % 2
            ba = 4 * (n % 2)
            for q in range(4):
                tr(psb[ba][:, q * 128:(q + 1) * 128], utmp[n2][:, q * 128:(q + 1) * 128], ident,
                   (R_ut[n2], R_const), (PB[ba],), signal=(q == 3))
            cp("act", ymT[j][:, 4 * cb:4 * cb + 4, :],
               psb[ba][:, 0:512].rearrange("p (q k) -> p q k", k=128), (PB[ba],), (PB[ba], R_ymT[j]))

        s3_a(0)
        for n in range(len(items)):
            s3_b(n)
            if n + 1 < len(items):
                s3_a(n + 1)
            s3_c(n)
        widx += 4
        for j in range(TBK):
            i = blk * TBK + j
            dma("sp", xo[j], x_own[i * 128:(i + 1) * 128, :], (), (R_hTt[j], R_uT[j]), R_hTt[j])
        for cb in range(4):
            prefetch(widx + cb + 1)
            k = wslots[widx + cb][0]
            for j in range(TBK):
                bk = (cb * TBK + j) % 2
                for kc in range(KC):
                    mm(ps[bk], ymT[j][:, kc, :], wsl[k][:, kc, :], kc == 0, kc == KC - 1,
                       (R_ymT[j], R_ws[k]), (PB[bk],), signal=(kc == KC - 1))
                k1 = tslot()
                tt("dve", tsl[k1], ps[bk], G_bc[:, cb * 512:(cb + 1) * 512], ALU.mult,
                   (PB[bk], R_tc), (PB[bk], R_ts[k1]))
                tt("dve", xo[j][:, cb * 512:(cb + 1) * 512], tsl[k1], xo[j][:, cb * 512:(cb + 1) * 512],
                   ALU.add, (R_ts[k1], R_hTt[j], R_uT[j]), (R_hTt[j], R_uT[j]))
        widx += 4
        for j in range(TBK):
            i = blk * TBK + j
            rw_ = (R_hTt[j], R_uT[j])
            for q in range(4):
                k1 = tslot()
                act(tsl[k1], xo[j][:, q * 512:(q + 1) * 512], AF.Square, rw_, (R_ts[k1], R_ss),
                    accum_out=ssf[:, q:q + 1])
            rsum(ssf[:, 4:5], ssf[:, 0:4], (R_ss,), (R_ss,))
            act(ssf[:, 5:6], ssf[:, 4:5], AF.Sqrt, (R_ss, R_misc), (R_ss,), bias=epsc, scale=1.0 / D)
            sc.op("dve", lambda e: e.reciprocal(ssf[:, 6:7], ssf[:, 5:6]), (R_ss,), (R_ss,))
            stt(xo[j], xo[j], ssf[:, 6:7], fg_bc, ALU.mult, ALU.mult, rw_ + (R_ss, R_tc), rw_)
            dma("pool", out[i * 128:(i + 1) * 128, :], xo[j], rw_, (R_out,), R_xost[j])
    return finish(nc, sc, es, out)


def finish(nc, sc, es, out):
    sc.final_wait("pool")
    build_program.stats = {e: len(sc.q[e]) for e in sc.ENG}
    with nc.Block() as block:
        block.sync(sc.replay("sp"))
        block.tensor(sc.replay("pe"))
        block.scalar(sc.replay("act"))
        block.vector(sc.replay("dve"))
        block.gpsimd(sc.replay("pool"))
    es.close()
    return nc


def host_consts(S, p):
    NT = S // 128
    NO = NT // 2
    NCMP = (S - 32) // 16 + 1
    NCT = (NCMP + 127) // 128
    KW = 4 * NO - 4 + 128
    f = np.float32
    k = np.arange(128)
    cs = {}
    cs["c_ident"] = np.eye(128, dtype=f)
    cs["c_E"] = (k[:, None] == (np.arange(S)[None, :] // 64)).astype(f)
    c = (np.arange(NCT)[None, :, None] * 128 + k[:, None, None])
    n = k[None, None, :]
    cs["c_ovl"] = ((c >= 4 * n - 1) & (c <= 4 * n + 3) & (c < NCMP)).astype(f)
    cs["c_L"] = (k[None, :] - 16 * k[:, None] - 31).astype(f)
    m = np.arange(KW)[None, :]
    r = m - 4 * (NO - 1) - 2 * p
    hq = (k[:, None] >= 64).astype(np.int64)
    keep = (r < hq - 1).astype(f)
    add = np.where((r == hq - 1) | (r == hq), 1.0e4, np.where(r > hq, -1.0e30, 0.0)).astype(f)
    cs["c_keep"] = keep
    cs["c_add"] = add
    kk, qq = k[:, None], k[None, :]
    causal = np.where(kk <= qq, 0.0, NEGB).astype(f)
    lo = np.where(kk > qq, 0.0, NEGB).astype(f)
    allm = np.full((128, 128), NEGB, f)
    zero = np.zeros((128, 128), f)
    if p == 0:
        tabs = [causal, allm, lo, zero, zero, zero, causal, allm]
    else:
        tabs = [zero, causal, allm, lo, zero, zero, zero, causal]
    cs["c_bias"] = np.ascontiguousarray(
        np.stack([np.tile(t, (1, 4)) for t in tabs], axis=1)).astype(f)
    cs["c_pos_all"] = (128 * np.arange(NT)[None, :] + k[:, None]).astype(f)
    cs["c_pos_own"] = (128 * (2 * np.arange(NO)[None, :] + p) + k[:, None]).astype(f)
    cs["c_pos_cmp"] = (16 * (128 * np.arange(NCT)[None, :] + k[:, None]) + 31).astype(f)
    cs["c_p128"] = np.full((128, 1), 128.0 * p, f)
    return cs


def make_in_maps(inputs, S, batches):
    f = np.float32
    NT = S // 128
    g = {k_: np.asarray(v) for k_, v in inputs.items()}
    shared = {
        "w_ada": np.ascontiguousarray(g["w_ada"][0], f),
        "b_ada": np.ascontiguousarray(g["b_ada"][0][None, :], f),
        "norm_g": np.ascontiguousarray(g["norm_g"][0][None, :], f),
        "w_in": np.ascontiguousarray(g["w_in"][0], f),
        "lam4": np.ascontiguousarray(np.stack([g["lambda_q1"][0], g["lambda_k1"][0],
                                               g["lambda_q2"][0], g["lambda_k2"][0]]), f),
        "diff_norm_g": np.ascontiguousarray(g["diff_norm_g"][0][None, :], f),
        "peT": np.ascontiguousarray(np.stack([g["cmp_pe_k"][0].T, g["cmp_pe_v"][0].T], axis=1), f),
        "cmp_w1": np.ascontiguousarray(np.stack([g["cmp_w1_k"][0], g["cmp_w1_v"][0]]), f),
        "cmp_w2": np.ascontiguousarray(np.stack([g["cmp_w2_k"][0], g["cmp_w2_v"][0]]), f),
        "w_branch": np.ascontiguousarray(g["w_branch"][0], f),
        "w_out": np.ascontiguousarray(g["w_out"][0], f),
        "final_norm_g": np.ascontiguousarray(g["final_norm_g"][None, :], f),
    }
    consts = [host_consts(S, 0), host_consts(S, 1)]
    maps = []
    for b in batches:
        xb = np.ascontiguousarray(g["x"][b], f)
        for p in range(2):
            m = dict(shared)
            m.update(consts[p])
            m["x_all"] = xb
            m["x_own"] = np.ascontiguousarray(xb.reshape(NT, 128, D)[p::2].reshape(-1, D))
            m["cT"] = np.ascontiguousarray(g["c"][b].reshape(KC, 128).T, f)
            maps.append(m)
    return maps


def kernel(**inputs):
    S = 8192
    B = 4
    NT = S // 128
    nc = build_program(S)
    maps = make_in_maps(inputs, S, list(range(B)))
    res = run_bass_kernel_spmd(nc, maps, core_ids=list(range(8)))
    outp = np.empty((B, S, D), np.float32)
    for b in range(B):
        v = outp[b].reshape(NT, 128, D)
        for p in range(2):
            v[p::2] = np.asarray(res.results[2 * b + p]["out"], np.float32).reshape(NT // 2, 128, D)
    return outp
```

```python
import math
from contextlib import ExitStack

import numpy as np
import concourse.bass as bass
import concourse.mybir as mybir
from concourse.bass_utils import run_bass_kernel_spmd

F32 = mybir.dt.float32
BF16 = mybir.dt.bfloat16
AF = mybir.ActivationFunctionType
ALU = mybir.AluOpType

D = 2048
KC = 16
IN_W = 11800
EPS = 1e-6
THETA = 500000.0
NEGB = -30000.0
PI = math.pi

OFF = dict(aq=0, ak=1024, av=2048, az=3072, bq=4096, bkc=5120, bvc=5376, bks=5632, bvs=5888,
           bkw=6144, bvw=6400, bz=6656, bgate=7680, mgate=7704)


class Sem:
    __slots__ = ("h", "issued", "bg", "kind")

    def __init__(self, h):
        self.h = h
        self.issued = 0
        self.bg = False
        self.kind = None


class Res:
    __slots__ = ("name", "w", "r", "dsem")
    registry = []

    def __init__(self, name):
        self.name = name
        self.w = None
        self.r = []
        self.dsem = None
        Res.registry.append(self)


class Sched:
    ENG = ("pe", "act", "dve", "pool", "sp")

    def __init__(self, nc, sem_handles):
        self.nc = nc
        self.free = [Sem(h) for h in sem_handles]
        self.esem = {e: self.free.pop() for e in self.ENG}
        self.pend = {e: False for e in self.ENG}
        self.q = {e: [] for e in self.ENG}
        self.waited = {e: {} for e in self.ENG}
        self.dsems = []
        self.free_kind = {"sw": [], "hw": []}
        self.nops = {e: 0 for e in self.ENG}

    def new_sem(self, kind):
        pool = self.free_kind[kind]
        if pool:
            s = pool.pop()
        else:
            s = self.free.pop()
            s.kind = kind
        if s not in self.dsems:
            self.dsems.append(s)
        return s

    def recycle(self, res_list):
        for r in res_list:
            if r.dsem is not None:
                self.free_kind[r.dsem.kind].append(r.dsem)
                r.dsem = None

    def _plan_waits(self, eng, deps):
        best = {}
        for sem, val in deps:
            if sem is self.esem[eng] and eng == "pe":
                continue
            if best.get(sem, 0) < val:
                best[sem] = val
        waits = []
        wd = self.waited[eng]
        for sem, val in best.items():
            if wd.get(sem, 0) >= val:
                continue
            assert val <= sem.issued, f"wait on unsignaled token eng={eng}"
            wd[sem] = val
            waits.append((sem.h, val))
        return waits

    def _deps(self, reads, writes):
        deps = []
        for r in reads:
            if r.w is not None:
                deps.append(r.w)
        for w in writes:
            if w.w is not None:
                deps.append(w.w)
            deps.extend(w.r)
        return deps

    def op(self, eng, fn, reads=(), writes=(), signal=True):
        waits = self._plan_waits(eng, self._deps(reads, writes))
        sem = self.esem[eng]
        tok = (sem, sem.issued + 1)
        if signal:
            sem.issued += 1
            self.pend[eng] = False
        else:
            assert eng == "pe"
            self.pend[eng] = True
        for r in reads:
            r.r.append(tok)
        for w in writes:
            w.w = tok
            w.r = []
        self.q[eng].append((waits, fn, sem.h if signal else None, 1))
        self.nops[eng] += 1

    def dma(self, queue, fn, reads=(), writes=(), owner=None, serialize=True, bg=False):
        kind = "sw" if queue == "pool" else "hw"
        if owner.dsem is None:
            owner.dsem = self.new_sem(kind)
            owner.dsem.bg = bg
        assert owner.dsem.kind == kind, f"semaphore of {owner.name} used from both DMA queue kinds"
        sem = owner.dsem
        deps = self._deps(reads, writes)
        if serialize and sem.issued > 0:
            deps.append((sem, sem.issued))
        waits = self._plan_waits(queue, deps)
        sem.issued += 16
        tok = (sem, sem.issued)
        for r in reads:
            r.r.append(tok)
        for w in writes:
            w.w = tok
            w.r = []
        self.q[queue].append((waits, fn, sem.h, 16))

    def barrier(self):
        for e in self.ENG:
            assert not self.pend[e], f"barrier with unsignaled ops on {e}"
        toks = [(self.esem[e], self.esem[e].issued) for e in self.ENG if self.esem[e].issued > 0]
        toks += [(s, s.issued) for s in self.dsems if s.issued > 0 and not s.bg]
        for e in self.ENG:
            waits = self._plan_waits(e, [t for t in toks if t[0] is not self.esem[e]])
            if waits:
                self.q[e].append((waits, None, None, 0))

    def final_wait(self, queue):
        toks = [(s, s.issued) for s in self.dsems if s.issued > 0]
        toks += [(self.esem[e], self.esem[e].issued) for e in self.ENG
                 if e != queue and self.esem[e].issued > 0]
        waits = self._plan_waits(queue, toks)
        self.q[queue].append((waits, None, None, 0))

    def replay(self, eng):
        items = self.q[eng]

        def f(e):
            for waits, fn, sem, inc in items:
                for (h, v) in waits:
                    e.wait_ge(h, v)
                if fn is None:
                    continue
                ins = fn(e)
                if sem is not None:
                    ins.then_inc(sem, inc)
        return f


class Arena:
    def __init__(self, tensor, nbytes):
        self.t = tensor
        self.n = nbytes
        self.top = 0

    def mark(self):
        return self.top

    def release(self, m):
        self.top = m

    def alloc(self, dtype, *shape):
        esz = 4 if dtype == F32 else 2
        n = 1
        for s in shape:
            n *= s
        nb = (n * esz + 31) // 32 * 32
        off = self.top
        self.top += nb
        assert self.top <= self.n, f"SBUF arena overflow {self.top} > {self.n}"
        ap = self.t[:, off // 2: off // 2 + (n * esz) // 2]
        if dtype == F32:
            ap = ap.bitcast(F32)
        if len(shape) == 2:
            ap = ap.rearrange("p (a b) -> p a b", b=shape[1])
        elif len(shape) == 3:
            ap = ap.rearrange("p (a b c) -> p a b c", b=shape[1], c=shape[2])
        elif len(shape) == 4:
            ap = ap.rearrange("p (a b c d) -> p a b c d", b=shape[1], c=shape[2], d=shape[3])
        return ap


def strided(ap2d, start, step, count):
    pst = ap2d.ap[0]
    est = ap2d.ap[-1][0]
    return bass.AP(ap2d.tensor, ap2d.offset + start * est, (tuple(pst), (step * est, count)))


def build_program(S, debug=None):
    NT = S // 128
    NO = NT // 2
    NCMP = (S - 32) // 16 + 1
    NCT = (NCMP + 127) // 128
    KW = 4 * NO - 4 + 128

    nc = bass.Bass("TRN2", target_bir_lowering=False)

    def din(name, shape, dt=F32):
        return nc.dram_tensor(name, list(shape), dt, kind="ExternalInput").ap()

    def dscr(name, shape, dt):
        kind = "ExternalOutput" if (debug and name in debug) else "Internal"
        return nc.dram_tensor(name, list(shape), dt, kind=kind).ap()

    x_all = din("x_all", [S, D])
    x_own = din("x_own", [S // 2, D])
    cT = din("cT", [128, KC])
    w_ada = din("w_ada", [D, 3 * D])
    b_ada = din("b_ada", [1, 3 * D])
    norm_g = din("norm_g", [1, D])
    w_in = din("w_in", [D, IN_W])
    lam4 = din("lam4", [4, 64])
    dng = din("diff_norm_g", [1, 128])
    peT = din("peT", [128, 2, 32])
    w1 = din("cmp_w1", [2, 4096, 256])
    w2 = din("cmp_w2", [2, 256, 128])
    w_br = din("w_branch", [D, D])
    w_out = din("w_out", [D, D])
    fng = din("final_norm_g", [1, D])
    c_ident = din("c_ident", [128, 128])
    c_E = din("c_E", [128, S])
    c_ovl = din("c_ovl", [128, NCT, 128])
    c_L = din("c_L", [128, 128])
    c_keep = din("c_keep", [128, KW])
    c_add = din("c_add", [128, KW])
    c_bias = din("c_bias", [128, 8, 512])
    c_pos_all = din("c_pos_all", [128, NT])
    c_pos_own = din("c_pos_own", [128, NO])
    c_pos_cmp = din("c_pos_cmp", [128, NCT])
    c_p128 = din("c_p128", [128, 1])

    out = nc.dram_tensor("out", [S // 2, D], F32, kind="ExternalOutput").ap()

    hT_d = dscr("hT_d", [NT, 128, D], BF16)
    hTo_d = dscr("hTo_d", [NO, 128, D], BF16)
    win_bf = dscr("win_bf", [D, IN_W], BF16)
    wbr_bf = dscr("wbr_bf", [D, D], BF16)
    wout_bf = dscr("wout_bf", [D, D], BF16)
    w1_bf = dscr("w1_bf", [2, 4096, 256], BF16)
    w2_bf = dscr("w2_bf", [2, 256, 128], BF16)
    oa_d = dscr("oa_d", [NO, 128, 1024], F32)
    ob_d = dscr("ob_d", [NO, 128, 1024], F32)
    G_d = dscr("G_d", [128, D], F32)

    ARENA_BYTES = 204800
    es = ExitStack()
    arena_t = es.enter_context(nc.sbuf_tensor("arena", [128, ARENA_BYTES // 2], BF16))
    banks = [es.enter_context(nc.psum_tensor(f"bank{k}", [128, 512], F32)) for k in range(8)]
    sem_handles = [es.enter_context(nc.semaphore(f"s{k}")) for k in range(100)]
    sc = Sched(nc, sem_handles)
    ar = Arena(arena_t, ARENA_BYTES)

    ps = [b[:] for b in banks]
    psb = [b[:].bitcast(BF16) for b in banks]
    PB = [Res(f"bank{k}") for k in range(8)]

    def mm(out_, lhsT, rhs, start, stop, reads, writes, signal=True):
        sc.op("pe", lambda e: e.matmul(out_, lhsT, rhs, start=start, stop=stop,
                                        skip_group_check=True), reads, writes, signal)

    def tr(out_, in_, ident, reads, writes, signal=True):
        sc.op("pe", lambda e: e.transpose(out_, in_, ident), reads, writes, signal)

    def act(out_, in_, func, reads, writes, bias=None, scale=None, accum_out=None):
        kw = {}
        if bias is not None:
            kw["bias"] = bias
        if scale is not None:
            kw["scale"] = scale
        if accum_out is not None:
            kw["accum_out"] = accum_out
        sc.op("act", lambda e: e.activation(out_, in_, func, **kw), reads, writes)

    def ts(eng, out_, in0, s1, s2, op0, op1, reads, writes):
        if op1 is None:
            sc.op(eng, lambda e: e.tensor_scalar(out_, in0, s1, None, op0), reads, writes)
        else:
            sc.op(eng, lambda e: e.tensor_scalar(out_, in0, s1, s2, op0, op1), reads, writes)

    def tt(eng, out_, in0, in1, op, reads, writes):
        sc.op(eng, lambda e: e.tensor_tensor(out_, in0, in1, op), reads, writes)

    def stt(out_, in0, scalar, in1, op0, op1, reads, writes):
        sc.op("dve", lambda e: e.scalar_tensor_tensor(out_, in0, scalar, in1, op0, op1), reads, writes)

    def cp(eng, out_, in_, reads, writes):
        if eng == "act":
            sc.op("act", lambda e: e.copy(out_, in_), reads, writes)
        else:
            sc.op(eng, lambda e: e.tensor_copy(out_, in_), reads, writes)

    def rsum(out_, in_, reads, writes):
        sc.op("dve", lambda e: e.reduce_sum(out_, in_, axis=mybir.AxisListType.X), reads, writes)

    def memset(eng, ap, val, writes):
        sc.op(eng, lambda e: e.memset(ap, val), (), writes)

    def dma(queue, out_, in_, reads, writes, owner, serialize=True, bg=False):
        sc.dma(queue, lambda e: e.dma_start(out=out_, in_=in_), reads, writes, owner, serialize, bg)

    R_const = Res("const")
    ident = ar.alloc(BF16, 128)
    biasT = ar.alloc(BF16, 8, 512)
    Lt = ar.alloc(F32, 128)
    keepT = ar.alloc(F32, KW)
    addT = ar.alloc(F32, KW)
    ovl = ar.alloc(BF16, NCT, 128)
    p128 = ar.alloc(F32, 1)
    pos_all = ar.alloc(F32, NT)
    pos_own = ar.alloc(F32, NO)
    pos_cmp = ar.alloc(F32, NCT)
    gsub = ar.alloc(F32, 128)
    lamv = ar.alloc(F32, 4, 64)
    ones_f = ar.alloc(F32, 128)
    small = ar.alloc(F32, 16)

    R_constp = Res("constp")

    rope_lo = ar.mark()
    cosA = ar.alloc(F32, NT, 8)
    sinA = ar.alloc(F32, NT, 8)
    cosAo = ar.alloc(F32, NO, 8)
    sinAo = ar.alloc(F32, NO, 8)
    cosB = ar.alloc(F32, NT, 16)
    sinB = ar.alloc(F32, NT, 16)
    cosBo = ar.alloc(F32, NO, 16)
    sinBo = ar.alloc(F32, NO, 16)
    cosC = ar.alloc(F32, NCT, 16)
    sinC = ar.alloc(F32, NCT, 16)
    rope_hi = ar.mark()

    def cdma(queue, out_, in_):
        rr = R_const if queue == "sp" else R_constp
        dma(queue, out_, in_, (), (rr,), rr, serialize=False)

    cdma("pool", ident, c_ident)
    cdma("pool", biasT, c_bias)
    cdma("pool", ovl, c_ovl)
    cdma("sp", Lt, c_L)
    cdma("sp", keepT, c_keep)
    cdma("sp", addT, c_add)
    cdma("sp", p128, c_p128)
    cdma("sp", pos_all, c_pos_all)
    cdma("sp", pos_own, c_pos_own)
    cdma("sp", pos_cmp, c_pos_cmp)
    cdma("sp", gsub, dng.broadcast_to([128, 128]))
    for k in range(4):
        cdma("sp", lamv[:, k, :], lam4[k:k + 1, :].broadcast_to([128, 64]))
    sc.barrier()

    cast_q = []

    def wcast_cols(res, c0, c1, grp):
        for r in range(4):
            cast_q.append((grp, (lambda r=r, c0=c0, c1=c1, res=res: dma(
                "pool", win_bf[r * 512:(r + 1) * 512, c0:c1], w_in[r * 512:(r + 1) * 512, c0:c1],
                (), (res,), res, serialize=False, bg=True))))

    R_wcA = [Res(f"wcA{u}") for u in range(4)]
    R_wcB = Res("wcB")
    R_wcC = Res("wcC")
    R_wcT = Res("wcT")
    for u in range(4):
        for nm in ("ak", "av", "aq"):
            wcast_cols(R_wcA[u], OFF[nm] + 256 * u, OFF[nm] + 256 * u + 256, f"A{u}")
    wcast_cols(R_wcB, OFF["bq"], OFF["bz"], "B")
    wcast_cols(R_wcB, OFF["bgate"], OFF["mgate"], "B")
    for k in range(2):
        for r in range(4):
            cast_q.append(("B", (lambda k=k, r=r: dma(
                "pool", w1_bf[k, r * 1024:(r + 1) * 1024, :], w1[k, r * 1024:(r + 1) * 1024, :],
                (), (R_wcC,), R_wcC, serialize=False, bg=True))))
        cast_q.append(("B", (lambda k=k: dma("pool", w2_bf[k], w2[k], (), (R_wcC,), R_wcC,
                                               serialize=False, bg=True))))
    wcast_cols(R_wcT, OFF["az"], OFF["bq"], "T")
    wcast_cols(R_wcT, OFF["bz"], OFF["bgate"], "T")
    for q4 in range(4):
        wcast_cols(R_wcT, OFF["mgate"] + 1024 * q4, OFF["mgate"] + 1024 * q4 + 1024, "T")
    for r in range(4):
        cast_q.append(("T", (lambda r=r: dma("pool", wbr_bf[r * 512:(r + 1) * 512, :],
                                               w_br[r * 512:(r + 1) * 512, :], (), (R_wcT,), R_wcT,
                                               serialize=False, bg=True))))
        cast_q.append(("T", (lambda r=r: dma("pool", wout_bf[r * 512:(r + 1) * 512, :],
                                               w_out[r * 512:(r + 1) * 512, :], (), (R_wcT,), R_wcT,
                                               serialize=False, bg=True))))

    def drip(n):
        for _ in range(n):
            if cast_q:
                cast_q.pop(0)[1]()

    def flush_casts(grp):
        last = -1
        for k, (g_, _) in enumerate(cast_q):
            if g_ == grp:
                last = k
        drip(last + 1)

    flush_casts("A0")

    R_misc = Res("misc")
    memset("dve", ones_f, 1.0, (R_misc,))
    ts("dve", gsub, gsub, 0.8, None, ALU.mult, None, (R_const,), (R_misc,))
    junk64 = ar.alloc(F32, 64)
    for k in range(2):
        tt("dve", junk64, lamv[:, 2 * k, :], lamv[:, 2 * k + 1, :], ALU.mult, (R_const,), (R_misc,))
        rsum(small[:, k:k + 1], junk64, (R_misc,), (R_misc,))
    act(small[:, 2:4], small[:, 0:2], AF.Exp, (R_misc,), (R_misc,))
    tt("dve", small[:, 4:5], small[:, 2:3], small[:, 3:4], ALU.subtract, (R_misc,), (R_misc,))
    ts("dve", small[:, 5:6], small[:, 4:5], 0.2, None, ALU.add, None, (R_misc,), (R_misc,))
    ts("dve", small[:, 6:7], small[:, 5:6], -1.0, None, ALU.mult, None, (R_misc,), (R_misc,))
    neglam = small[:, 6:7]

    C1 = 6.28125
    C2 = 2 * PI - C1

    R_rope = Res("rope")

    def rope_gen(tmpl):
        rw = (R_rope,)
        tables = ((pos_all, NT, 8, 16, cosA, sinA), (pos_own, NO, 8, 16, cosAo, sinAo),
                  (pos_all, NT, 16, 32, cosB, sinB), (pos_own, NO, 16, 32, cosBo, sinBo),
                  (pos_cmp, NCT, 16, 32, cosC, sinC))
        for (pos, n, half, rd, cosT, sinT) in tables:
            ang, a, kf, r, msk = [t[:, 0:n * half].rearrange("p (n h) -> p n h", h=half) for t in tmpl]
            ki = a.bitcast(mybir.dt.int32)
            for j in range(half):
                inv = float(np.float32(1.0) / np.float32(THETA) ** (np.float32(j) * np.float32(2.0 / rd)))
                ts("dve", ang[:, :, j], pos, inv, None, ALU.mult, None, (R_const,) + rw, rw)
                yield
            for (dst, shift) in ((sinT, 0.0), (cosT, PI / 2)):
                ts("dve", r, ang, shift, None, ALU.add, None, rw, rw)
                yield
                ts("dve", kf, r, 1.0 / (2 * PI), None, ALU.mult, None, rw, rw)
                yield
                cp("dve", ki, kf, rw, rw)
                yield
                cp("dve", kf, ki, rw, rw)
                yield
                stt(r, kf, -C1, r, ALU.mult, ALU.add, rw, rw)
                yield
                stt(r, kf, -C2, r, ALU.mult, ALU.add, rw, rw)
                yield
                ts("dve", msk, r, PI, None, ALU.is_gt, None, rw, rw)
                yield
                stt(r, msk, -2 * PI, r, ALU.mult, ALU.add, rw, rw)
                yield
                ts("dve", msk, r, -PI, None, ALU.is_lt, None, rw, rw)
                yield
                stt(r, msk, 2 * PI, r, ALU.mult, ALU.add, rw, rw)
                yield
                ts("dve", r, r, PI, -PI, ALU.min, ALU.max, rw, rw)
                yield
                act(dst, r, AF.Sin, rw, rw)
                yield

    C1 = 6.28125
    C2 = 2 * PI - C1

    epsc = ar.alloc(F32, 1)
    memset("dve", epsc, EPS, (R_misc,))

    pm = ar.mark()
    A_bc = ar.alloc(F32, D)
    B_bc = ar.alloc(F32, D)
    pm2 = ar.mark()
    G_bc = ar.alloc(F32, D)
    R_G = Res("G")
    ng_bc = ar.alloc(F32, D)
    cts = ar.alloc(F32, KC)
    scv = ar.alloc(F32, KC)
    sc_rep = ar.alloc(F32, KC, 128)
    brow = ar.alloc(F32, 3 * D)
    wada = [ar.alloc(F32, KC, 512) for _ in range(2)]
    R_wada = [Res("wada0"), Res("wada1")]
    R_ada = Res("ada")
    dma("sp", cts, cT, (), (R_ada,), R_ada)
    dma("sp", ng_bc, norm_g.broadcast_to([128, D]), (), (R_ada,), R_ada)
    dma("sp", brow[0:1, :], b_ada, (), (R_ada,), R_ada)
    act(scv, cts, AF.Silu, (R_ada,), (R_ada,))
    cp("dve", sc_rep, scv.unsqueeze(2).broadcast_to([128, KC, 128]), (R_ada,), (R_ada,))
    w_ada_v = w_ada.rearrange("(kc p) c -> p kc c", p=128)
    for cb in range(12):
        sl = cb % 2
        dma("sp", wada[sl], w_ada_v[:, :, cb * 512:(cb + 1) * 512], (), (R_wada[sl],), R_wada[sl])
        bk = cb % 2
        for kc in range(KC):
            mm(ps[bk], sc_rep[:, kc, :], wada[sl][:, kc, :], kc == 0, False,
               (R_ada, R_wada[sl]), (PB[bk],), signal=False)
        mm(ps[bk], ones_f[0:1, :], brow[0:1, cb * 512:(cb + 1) * 512], False, True,
           (R_ada, R_misc), (PB[bk],))
        c0 = (cb % 4) * 512
        if cb < 4:
            cp("act", B_bc[:, c0:c0 + 512], ps[bk], (PB[bk],), (PB[bk], R_ada))
        elif cb < 8:
            stt(A_bc[:, c0:c0 + 512], ps[bk], 1.0, ng_bc[:, c0:c0 + 512], ALU.add, ALU.mult,
                (PB[bk], R_ada), (PB[bk], R_ada))
        else:
            cp("act", G_bc[:, c0:c0 + 512], ps[bk], (PB[bk],), (PB[bk], R_G))
    dma("sp", G_d, G_bc, (R_G,), (R_G,), R_G)

    sc.barrier()
    ar.release(pm2)
    NX = 3
    xbuf = [ar.alloc(F32, D) for _ in range(NX)]
    R_x = [Res(f"x{k}") for k in range(NX)]
    junkb = ar.alloc(BF16, D)
    R_junk = Res("junk")
    tmpf = [ar.alloc(F32, D) for _ in range(2)]
    R_tmp = [Res("tmp0"), Res("tmp1")]
    hb = [ar.alloc(BF16, D) for _ in range(2)]
    R_hb = [Res("hb0"), Res("hb1")]
    R_hb2 = [Res("hb0b"), Res("hb1b")]
    hTs = [ar.alloc(BF16, D) for _ in range(2)]
    R_hTs = [Res("hTs0"), Res("hTs1")]
    ssv = [ar.alloc(F32, 2) for _ in range(NX)]
    R_hTd = Res("hT_d")

    hitems = [(x_all, t, hT_d) for t in range(NT)] + [(x_own, t, hTo_d) for t in range(NO)]

    def h_load(n):
        xsrc, t, dst = hitems[n]
        k3 = n % NX
        dma("sp", xbuf[k3], xsrc[t * 128:(t + 1) * 128, :], (), (R_x[k3],), R_x[k3])

    def h_front(n):
        k3, k2 = n % NX, n % 2
        act(junkb, xbuf[k3], AF.Square, (R_x[k3],), (R_junk, R_x[k3]), accum_out=ssv[k3][:, 0:1])
        act(ssv[k3][:, 1:2], ssv[k3][:, 0:1], AF.Sqrt, (R_x[k3], R_misc), (R_x[k3],), bias=epsc, scale=1.0 / D)
        sc.op("dve", (lambda v=ssv[k3]: lambda e: e.reciprocal(v[:, 1:2], v[:, 1:2]))(), (R_x[k3],), (R_x[k3],))
        stt(tmpf[k2], xbuf[k3], ssv[k3][:, 1:2], A_bc, ALU.mult, ALU.mult,
            (R_x[k3], R_ada), (R_tmp[k2],))
        tt("pool", hb[k2][:, 0:1152], tmpf[k2][:, 0:1152], B_bc[:, 0:1152], ALU.add,
           (R_tmp[k2], R_ada), (R_hb[k2],))
        tt("dve", hb[k2][:, 1152:2048], tmpf[k2][:, 1152:2048], B_bc[:, 1152:2048], ALU.add,
           (R_tmp[k2], R_ada), (R_hb2[k2],))

    def h_back(n):
        xsrc, t, dst = hitems[n]
        k2 = n % 2
        b0, b1 = 2 + 2 * k2, 3 + 2 * k2
        for kc in range(KC):
            bk = b0 if kc < 8 else b1
            tr(psb[bk][:, (kc % 8) * 128:(kc % 8 + 1) * 128], hb[k2][:, kc * 128:(kc + 1) * 128],
               ident, (R_hb[k2], R_hb2[k2], R_const), (PB[bk],), signal=(kc % 8 == 7))
        cp("act", hTs[k2][:, 0:1024], psb[b0], (PB[b0],), (PB[b0], R_hTs[k2]))
        cp("act", hTs[k2][:, 1024:2048], psb[b1], (PB[b1],), (PB[b1], R_hTs[k2]))
        dma("act", dst[t], hTs[k2], (R_hTs[k2],), (R_hTd,), R_hTs[k2])

    rope_tmp = [ar.alloc(F32, NT * 16) for _ in range(5)]
    rgen = rope_gen(rope_tmp)
    h_load(0)
    h_load(1)
    h_front(0)
    for n in range(len(hitems)):
        if n + 2 < len(hitems):
            h_load(n + 2)
        if n + 1 < len(hitems):
            h_front(n + 1)
        for _ in range(3):
            next(rgen, None)
        h_back(n)
    for _ in rgen:
        pass
    ar.release(pm)
    sc.barrier()
    sc.recycle([R_wada[0], R_wada[1], R_ada] + R_x + R_hTs)

    if debug and debug.get("stop") == "prelude":
        return finish(nc, sc, es, out)

    win_v = win_bf.rearrange("(kc p) c -> p kc c", p=128)
    hT_v = hT_d.rearrange("t p (kc k) -> t p kc k", k=128)
    hTo_v = hTo_d.rearrange("t p (kc k) -> t p kc k", k=128)

    def rope_evac(psrc, dstb, nh, dh, half, cosT, sinT, reads, writes, tmp):
        pv = psrc.rearrange("p (h d) -> p h d", d=dh)
        dv = dstb.rearrange("p (h d) -> p h d", d=dh)
        x1 = pv[:, :, 0:half]
        x2 = pv[:, :, half:2 * half]
        cb_ = cosT.unsqueeze(1).broadcast_to([128, nh, half])
        sb_ = sinT.unsqueeze(1).broadcast_to([128, nh, half])
        t1 = tmp[:, 0, :].rearrange("p (h d) -> p h d", d=half)
        t2 = tmp[:, 1, :].rearrange("p (h d) -> p h d", d=half)
        t3 = tmp[:, 2, :].rearrange("p (h d) -> p h d", d=half)
        t4 = tmp[:, 3, :].rearrange("p (h d) -> p h d", d=half)
        cp("dve", dstb, psrc, reads, writes)
        tt("dve", t1, x1, cb_, ALU.mult, reads, writes)
        tt("dve", t2, x2, sb_, ALU.mult, reads, writes)
        tt("dve", t3, x2, cb_, ALU.mult, reads, writes)
        tt("dve", t4, x1, sb_, ALU.mult, reads, writes)
        tt("dve", dv[:, :, 0:half], t1, t2, ALU.subtract, reads, writes)
        tt("dve", dv[:, :, half:2 * half], t3, t4, ALU.add, reads, writes)

    R_oa = Res("oa_d")
    am = ar.mark()
    for u in range(4 if not (debug and debug.get("skipA")) else 0):
        um = ar.mark()
        flush_casts(f"A{u}")
        kT = ar.alloc(BF16, 2, S)
        V = ar.alloc(BF16, NT, 2, 130)
        qTp = ar.alloc(BF16, 2, NO, 2, 128)
        R_kT = [Res(f"kT{t}") for t in range(NT)]
        R_V = [Res(f"V{t}") for t in range(NT)]
        R_q = [Res(f"q{t}") for t in range(NO)]
        R_unit = Res("unit")
        memset("pool", V[:, :, :, 128:130], 1.0, R_V)
        memset("pool", qTp, 0.0, R_q)
        pm = ar.mark()
        W = ar.alloc(BF16, KC, 768)
        R_W = Res("W")
        for (c0, dst0) in ((OFF["ak"] + 256 * u, 0), (OFF["av"] + 256 * u, 256), (OFF["aq"] + 256 * u, 512)):
            dma("sp", W[:, :, dst0:dst0 + 256], win_v[:, :, c0:c0 + 256], (R_wcA[u],), (R_W,), R_W)
        NH = 3
        hTb = [ar.alloc(BF16, KC, 128) for _ in range(NH)]
        R_hTb = [Res(f"hTb{k}") for k in range(NH)]
        kb = [ar.alloc(BF16, 256) for _ in range(2)]
        R_kb = [Res("kb0"), Res("kb1")]
        rtmp = [ar.alloc(F32, 4, 32) for _ in range(2)]
        items = [("kv", t) for t in range(NT)] + [("q", i) for i in range(NO)]

        def ld(n):
            kind, idx = items[n]
            k3 = n % NH
            dma("sp", hTb[k3], (hT_v if kind == "kv" else hTo_v)[idx], (R_hTd,), (R_hTb[k3],), R_hTb[k3])

        def st_a(n):
            kind, idx = items[n]
            k3, bk = n % NH, n % 2
            if kind == "kv":
                for kc in range(KC):
                    mm(ps[bk], hTb[k3][:, kc, :], W[:, kc, 0:512], kc == 0, kc == KC - 1,
                       (R_hTb[k3], R_W), (PB[bk],), signal=(kc == KC - 1))
            else:
                for kc in range(KC):
                    mm(ps[bk][:, 0:256], hTb[k3][:, kc, :], W[:, kc, 512:768], kc == 0, kc == KC - 1,
                       (R_hTb[k3], R_W), (PB[bk],), signal=(kc == KC - 1))

        def st_b(n):
            kind, idx = items[n]
            bk, k2 = n % 2, n % 2
            if kind == "kv":
                t = idx
                cp("act", V[:, t, :, 0:128], ps[bk][:, 256:512].rearrange("p (h d) -> p h d", d=128),
                   (PB[bk],), (PB[bk], R_V[t]))
                rope_evac(ps[bk][:, 0:256], kb[k2], 4, 64, 8, cosA[:, t, :], sinA[:, t, :],
                          (PB[bk], R_misc), (PB[bk], R_kb[k2]), rtmp[k2])
            else:
                i = idx
                rope_evac(ps[bk][:, 0:256], kb[k2], 4, 64, 8, cosAo[:, i, :], sinAo[:, i, :],
                          (PB[bk], R_misc), (PB[bk], R_kb[k2]), rtmp[k2])

        def st_c(n):
            kind, idx = items[n]
            k2 = n % 2
            tb_ = 2 + k2
            for h in range(2):
                tr(psb[tb_][:, h * 128:(h + 1) * 128], kb[k2][:, h * 128:(h + 1) * 128], ident,
                   (R_kb[k2], R_const), (PB[tb_],), signal=(h == 1))
            if kind == "kv":
                t = idx
                cp("act", kT[:, :, t * 128:(t + 1) * 128],
                   psb[tb_][:, 0:256].rearrange("p (h k) -> p h k", k=128), (PB[tb_],), (PB[tb_], R_kT[t]))
            else:
                i = idx
                for h in range(2):
                    cp("act", qTp[0:64, h, i, 0, :], psb[tb_][0:64, h * 128:(h + 1) * 128],
                       (PB[tb_],), (PB[tb_], R_q[i]))
                    cp("act", qTp[64:128, h, i, 1, :], psb[tb_][64:128, h * 128:(h + 1) * 128],
                       (PB[tb_],), (PB[tb_], R_q[i]))

        ld(0)
        ld(1)
        st_a(0)
        for n in range(len(items)):
            if n + 2 < len(items):
                ld(n + 2)
            if n + 1 < len(items):
                st_a(n + 1)
            st_b(n)
            st_c(n)
        ar.release(pm)
        pm = ar.mark()
        NP = 4
        Pt = [ar.alloc(BF16, 512) for _ in range(NP)]
        R_P = [Res(f"P{k}") for k in range(NP)]
        fin = [ar.alloc(F32, 8) for _ in range(2)]
        o1 = [ar.alloc(F32, 128) for _ in range(2)]
        o2 = [ar.alloc(F32, 128) for _ in range(2)]
        ost = [ar.alloc(F32, 2, 128) for _ in range(2)]
        R_fin = [Res("fin0"), Res("fin1")]
        R_ost = [Res("ost0"), Res("ost1")]
        steps = [(i, kt) for i in range(NO) for kt in range(2 * i + 2)]
        pending = []
        fin2 = [[ar.alloc(F32, 8) for _ in range(2)] for _ in range(2)]
        o1b = [[ar.alloc(F32, 128) for _ in range(2)] for _ in range(2)]
        o2b = [[ar.alloc(F32, 128) for _ in range(2)] for _ in range(2)]
        R_fin2 = [[Res("f00"), Res("f01")], [Res("f10"), Res("f11")]]

        def acc_of(i):
            a2 = i % 2
            abk = (3 + 2 * a2, 4 + 2 * a2)
            return abk, [ps[abk[h]][:, 0:260].rearrange("p (m d) -> p m d", d=130) for h in range(2)]

        def emit_s(n):
            i, kt = steps[n]
            sb_ = n % 3
            last2 = kt >= 2 * i
            for h in range(2):
                mm(ps[sb_][:, h * 256:(h + 1) * 256], kT[:, h, kt * 128:(kt + 1) * 128],
                   qTp[:, h, i, :, :].rearrange("p m k -> p (m k)"), h == 0, (h == 1 and not last2),
                   (R_kT[kt], R_q[i]), (PB[sb_],), signal=(h == 1 and not last2))
            if last2:
                bt = 0 if kt == 2 * i else 1
                mm(ps[sb_], ident, biasT[:, bt, :], False, True, (R_const,), (PB[sb_],))

        emit_s(0)
        emit_s(1)
        for n, (i, kt) in enumerate(steps):
            nkt = 2 * i + 2
            os_ = i % 2
            if kt == 0:
                drip(2)
            if n + 2 < len(steps):
                emit_s(n + 2)
            abk, accs = acc_of(i)
            sb_ = n % 3
            pk = n % NP
            act(Pt[pk], ps[sb_], AF.Exp, (PB[sb_],), (PB[sb_], R_P[pk]), scale=0.125)
            for h in range(2):
                for m in range(2):
                    c0 = h * 256 + m * 128
                    mm(accs[h][:, m, 0:129], Pt[pk][:, c0:c0 + 128], V[:, kt, h, 0:129],
                       (kt == 0 and m == 0), kt == nkt - 1, (R_P[pk], R_V[kt]), (PB[abk[0]], PB[abk[1]]),
                       signal=(h == 1 and m == 1 and kt == nkt - 1))
            for (due, fn_) in list(pending):
                if due <= n:
                    pending.remove((due, fn_))
                    fn_()
            if kt != nkt - 1:
                continue
            for h in range(2):
                acc = accs[h]
                ab = abk[h]
                f = fin2[os_][h]
                oo1, oo2 = o1b[os_][h], o2b[os_][h]
                rd_, wr_ = (PB[ab], R_fin2[os_][h], R_misc), (PB[ab], R_fin2[os_][h])
                sc.op("dve", (lambda f=f, acc=acc: lambda e: e.reciprocal(f[:, 0:2], acc[:, :, 128]))(), rd_, wr_)
                tt("dve", f[:, 2:3], f[:, 1:2], neglam, ALU.mult, rd_, wr_)
                ts("dve", oo2, acc[:, 1, 0:128], f[:, 2:3], None, ALU.mult, None, rd_, wr_)
                stt(oo1, acc[:, 0, 0:128], f[:, 0:1], oo2, ALU.mult, ALU.add, rd_, wr_)
                tt("dve", oo2, oo1, oo1, ALU.mult, rd_[1:], wr_[1:])
                rsum(f[:, 3:4], oo2, rd_[1:], wr_[1:])
                ts("dve", f[:, 4:5], f[:, 3:4], 1.0 / 128, EPS, ALU.mult, ALU.add, rd_[1:], wr_[1:])

            def fin_part2(i=i, os_=os_):
                for h in range(2):
                    f = fin2[os_][h]
                    rr = (R_fin2[os_][h], R_misc)
                    act(f[:, 5:6], f[:, 4:5], AF.Ln, rr, (R_fin2[os_][h],))
                    act(f[:, 6:7], f[:, 5:6], AF.Exp, rr, (R_fin2[os_][h],), scale=-0.5)
                    stt(ost[os_][:, h, :], o1b[os_][h], f[:, 6:7], gsub, ALU.mult, ALU.mult,
                        rr + (R_ost[os_],), (R_fin2[os_][h], R_ost[os_]))
                dma("pool", oa_d[i][:, u * 256:(u + 1) * 256], ost[os_].rearrange("p h d -> p (h d)"),
                    (R_ost[os_],), (R_oa,), R_ost[os_])

            pending.append((n + 4, fin_part2))
        for (_, fn_) in pending:
            fn_()
        ar.release(pm)
        ar.release(um)
        sc.barrier()
        sc.recycle([R_W] + R_hTb + R_ost)

    ar.release(am)
    if debug and debug.get("stop") == "A":
        return finish(nc, sc, es, out)

    sB = 128.0 ** -0.5
    flush_casts("B")
    R_ob = Res("ob_d")
    bm = ar.mark()
    ones_ff = ar.alloc(F32, 128)
    memset("dve", ones_ff, 1.0, (R_misc,))
    hT_blk = hT_d.rearrange("t p (kc k) -> p t kc k", k=128)
    for g in range(2):
        um = ar.mark()
        kcT = ar.alloc(BF16, NCT * 128)
        vcx = ar.alloc(BF16, NCT, 128)
        R_kc = Res("kcT")
        R_vc = Res("vcx")
        pm = ar.mark()
        rawT = [ar.alloc(BF16, S) for _ in range(2)]
        R_raw = [Res("rawk"), Res("rawv")]
        Wc = ar.alloc(BF16, KC, 256)
        R_Wc = Res("Wc")
        dma("sp", Wc[:, :, 0:128], win_v[:, :, OFF["bkc"] + 128 * g:OFF["bkc"] + 128 * g + 128],
            (R_wcB,), (R_Wc,), R_Wc)
        dma("sp", Wc[:, :, 128:256], win_v[:, :, OFF["bvc"] + 128 * g:OFF["bvc"] + 128 * g + 128],
            (R_wcB,), (R_Wc,), R_Wc)
        w1s = [ar.alloc(BF16, 32, 256) for _ in range(2)]
        w2s = [ar.alloc(BF16, 2, 128) for _ in range(2)]
        peb = ar.alloc(BF16, 2, 32)
        R_cw = Res("cmpw")
        for k in range(2):
            dma("sp", w1s[k], w1_bf[k].rearrange("(t d) h -> d t h", d=128), (R_wcC,), (R_cw,), R_cw)
            dma("sp", w2s[k], w2_bf[k].rearrange("(j p) d -> p j d", p=128), (R_wcC,), (R_cw,), R_cw)
        R_cwp = Res("cmpwp")
        dma("pool", peb, peT, (), (R_cwp,), R_cwp)
        hblk = [ar.alloc(BF16, 4, KC, 128) for _ in range(2)]
        R_hblk = [Res("hblk0"), Res("hblk1")]
        nev = 0
        for tb in range(NT // 4):
            k2 = tb % 2
            dma("sp", hblk[k2], hT_blk[:, 4 * tb:4 * tb + 4], (R_hTd,), (R_hblk[k2],), R_hblk[k2])
            for which in range(2):
                bk = nev % 2
                nev += 1
                for kc in range(KC):
                    mm(ps[bk], Wc[:, kc, which * 128:(which + 1) * 128], hblk[k2][:, :, kc, :],
                       kc == 0, kc == KC - 1, (R_Wc, R_hblk[k2]), (PB[bk],), signal=(kc == KC - 1))
                cp("act" if which == 0 else "dve", rawT[which][:, tb * 512:(tb + 1) * 512], ps[bk],
                   (PB[bk],), (PB[bk], R_raw[which]))
        bias_h = ar.alloc(F32, 2)
        hidT = [ar.alloc(BF16, 512) for _ in range(2)]
        R_hid = Res("hid")
        kcb = ar.alloc(BF16, 128)
        rtc = ar.alloc(F32, 4, 16)
        R_kcb = Res("kcb")
        for which in range(2):
            for j in range(2):
                for t in range(32):
                    mm(ps[2][:, j:j + 1], w1s[which][:, t, j * 128:(j + 1) * 128], peb[:, which, t:t + 1],
                       t == 0, t == 31, (R_cw, R_cwp), (PB[2],), signal=(t == 31))
            cp("dve", bias_h, ps[2][:, 0:2], (PB[2],), (PB[2], R_hid))
            memset("dve", hidT[0], 0.0, (R_hid,))
            memset("dve", hidT[1], 0.0, (R_hid,))
            for j in range(2):
                bk = 3 + j
                for t in range(32):
                    mm(ps[bk][:, 0:NCMP], w1s[which][:, t, j * 128:(j + 1) * 128],
                       strided(rawT[which], t, 16, NCMP), t == 0, t == 31,
                       (R_cw, R_raw[which]), (PB[bk],), signal=(t == 31))
                act(hidT[j][:, 0:NCMP], ps[bk][:, 0:NCMP], AF.Silu, (PB[bk], R_hid), (PB[bk], R_hid),
                    bias=bias_h[:, j:j + 1])
            for ct in range(NCT):
                bk = 5 + ct % 2
                for j in range(2):
                    mm(ps[bk][:, 0:128], hidT[j][:, ct * 128:(ct + 1) * 128], w2s[which][:, j, :],
                       j == 0, j == 1, (R_hid, R_cw), (PB[bk],), signal=(j == 1))
                if which == 0:
                    rope_evac(ps[bk][:, 0:128], kcb, 1, 128, 16, cosC[:, ct, :], sinC[:, ct, :],
                              (PB[bk], R_misc), (PB[bk], R_kcb), rtc)
                    tr(psb[7][:, 0:128], kcb, ident, (R_kcb, R_const), (PB[7],))
                    cp("act", kcT[:, ct * 128:(ct + 1) * 128], psb[7][:, 0:128], (PB[7],), (PB[7], R_kc))
                else:
                    cp("act", vcx[:, ct, :], ps[bk][:, 0:128], (PB[bk],), (PB[bk], R_vc))
        ar.release(pm)
        sc.barrier()
        sc.recycle([R_Wc, R_cw, R_cwp] + R_hblk)
        qT = ar.alloc(BF16, 4, NO * 128)
        ksT = ar.alloc(BF16, S)
        kwT = ar.alloc(BF16, S)
        Vs = ar.alloc(BF16, NT, 130)
        Vw = ar.alloc(BF16, NT, 130)
        gts = ar.alloc(F32, NO, 12)
        Et = ar.alloc(BF16, S)
        R_K = [Res(f"K{t}") for t in range(NT)]
        R_Vt = [Res(f"Vt{t}") for t in range(NT)]
        R_q = [Res(f"q{t}") for t in range(NO)]
        R_E = Res("E")
        dma("pool", Et, c_E, (), (R_E,), R_E)
        memset("pool", Vs[:, :, 128:130], 1.0, R_Vt)
        memset("pool", Vw[:, :, 128:130], 1.0, R_Vt)
        pm = ar.mark()
        W2 = ar.alloc(BF16, KC, 1040)
        R_W = Res("W2")
        for (nm, d0, wdt) in (("bks", 0, 128), ("bvs", 128, 128), ("bkw", 256, 128), ("bvw", 384, 128)):
            c0 = OFF[nm] + 128 * g
            dma("sp", W2[:, :, d0:d0 + wdt], win_v[:, :, c0:c0 + wdt], (R_wcB,), (R_W,), R_W)
        c0 = OFF["bq"] + 512 * g
        dma("sp", W2[:, :, 512:1024], win_v[:, :, c0:c0 + 512], (R_wcB,), (R_W,), R_W)
        c0 = OFF["bgate"] + 12 * g
        dma("sp", W2[:, :, 1024:1036], win_v[:, :, c0:c0 + 12], (R_wcB,), (R_W,), R_W)
        NH = 3
        hTb = [ar.alloc(BF16, KC, 128) for _ in range(NH)]
        R_hTb = [Res(f"hTb{k}") for k in range(NH)]
        kb = [ar.alloc(BF16, 512) for _ in range(2)]
        R_kb = [Res("kb0"), Res("kb1")]
        rtmp = [ar.alloc(F32, 4, 64) for _ in range(2)]
        items = [("kv", t) for t in range(NT)] + [("q", i) for i in range(NO)]

        def ld(n):
            kind, idx = items[n]
            k3 = n % NH
            dma("sp", hTb[k3], (hT_v if kind == "kv" else hTo_v)[idx], (R_hTd,), (R_hTb[k3],), R_hTb[k3])

        def st_a(n):
            kind, idx = items[n]
            k3, bk = n % NH, n % 2
            c0 = 0 if kind == "kv" else 512
            for kc in range(KC):
                mm(ps[bk], hTb[k3][:, kc, :], W2[:, kc, c0:c0 + 512], kc == 0, kc == KC - 1,
                   (R_hTb[k3], R_W), (PB[bk],), signal=(kc == KC - 1))
            if kind == "q":
                gb_ = 4 + n % 2
                for kc in range(KC):
                    mm(ps[gb_][:, 0:12], hTb[k3][:, kc, :], W2[:, kc, 1024:1036], kc == 0, kc == KC - 1,
                       (R_hTb[k3], R_W), (PB[gb_],), signal=(kc == KC - 1))

        def st_b(n):
            kind, idx = items[n]
            bk, k2 = n % 2, n % 2
            if kind == "kv":
                t = idx
                rope_evac(ps[bk], kb[k2], 2, 256, 16, cosB[:, t, :], sinB[:, t, :],
                          (PB[bk], R_misc), (PB[bk], R_kb[k2]), rtmp[k2][:, :, 0:32])
                cp("pool", Vs[:, t, 0:128], kb[k2][:, 128:256], (R_kb[k2],), (R_Vt[t],))
                cp("pool", Vw[:, t, 0:128], kb[k2][:, 384:512], (R_kb[k2],), (R_Vt[t],))
            else:
                i = idx
                gb_ = 4 + n % 2
                act(gts[:, i, :], ps[gb_][:, 0:12], AF.Sigmoid, (PB[gb_],), (PB[gb_], R_q[i]))
                rope_evac(ps[bk], kb[k2], 4, 128, 16, cosBo[:, i, :], sinBo[:, i, :],
                          (PB[bk], R_misc), (PB[bk], R_kb[k2]), rtmp[k2])

        def st_c(n):
            kind, idx = items[n]
            k2 = n % 2
            tb_ = 2 + k2
            if kind == "kv":
                t = idx
                tr(psb[tb_][:, 0:128], kb[k2][:, 0:128], ident, (R_kb[k2], R_const), (PB[tb_],), signal=False)
                tr(psb[tb_][:, 128:256], kb[k2][:, 256:384], ident, (R_kb[k2], R_const), (PB[tb_],))
                cp("act", ksT[:, t * 128:(t + 1) * 128], psb[tb_][:, 0:128], (PB[tb_],), (PB[tb_], R_K[t]))
                cp("act", kwT[:, t * 128:(t + 1) * 128], psb[tb_][:, 128:256], (PB[tb_],), (PB[tb_], R_K[t]))
            else:
                i = idx
                for h in range(4):
                    tr(psb[tb_][:, h * 128:(h + 1) * 128], kb[k2][:, h * 128:(h + 1) * 128], ident,
                       (R_kb[k2], R_const), (PB[tb_],), signal=(h == 3))
                cp("act", qT[:, :, i * 128:(i + 1) * 128],
                   psb[tb_][:, 0:512].rearrange("p (h k) -> p h k", k=128), (PB[tb_],), (PB[tb_], R_q[i]))

        ld(0)
        ld(1)
        st_a(0)
        for n in range(len(items)):
            if n + 2 < len(items):
                ld(n + 2)
            if n + 1 < len(items):
                st_a(n + 1)
            st_b(n)
            st_c(n)
        ar.release(pm)
        sc.barrier()
        sc.recycle([R_W] + R_hTb)
        pm = ar.mark()
        NP = 4
        Pt = [ar.alloc(BF16, 512) for _ in range(NP)]
        R_P = [Res(f"P{k}") for k in range(NP)]
        e_t = [ar.alloc(F32, 512) for _ in range(NCT)]
        R_e = [Res(f"e{k}") for k in range(NCT)]
        ef = ar.alloc(F32, 512)
        cm = ar.alloc(F32, 128)
        zr = ar.alloc(F32, 512)
        R_z = Res("z")
        pcn = [ar.alloc(BF16, 512) for _ in range(2)]
        R_pcn = [Res("pcn0"), Res("pcn1")]
        obuf = [ar.alloc(F32, 4, 128) for _ in range(2)]
        R_obuf = [Res("obuf0"), Res("obuf1")]
        impa = ar.alloc(F32, 128)
        wk = ar.alloc(F32, 128)
        m8 = ar.alloc(F32, 16)
        thr = ar.alloc(F32, 1)
        negsel = ar.alloc(BF16, 128)
        nsT = ar.alloc(BF16, 4, 128)
        R_top = Res("top")
        R_ns = Res("nsT")
        fin = ar.alloc(F32, 8)
        R_fin = Res("finB")
        gts4 = gts.rearrange("p i (h c) -> p i h c", c=3)
        npt = 0
        npc = 0

        def branch(i, os_, kts, Kmat, Vmat, accb, gidx, bias_of, with_sel):
            nonlocal npt
            qi = qT[:, :, i * 128:(i + 1) * 128]
            accs = [ps[accb[0]][:, 0:260].rearrange("p (m d) -> p m d", d=130),
                    ps[accb[1]][:, 0:260].rearrange("p (m d) -> p m d", d=130)]
            npt0 = npt
            npt += len(kts)

            def emit_s(n_):
                kt = kts[n_]
                sb_ = (npt0 + n_) % 3
                bt = bias_of(kt)
                mm(ps[sb_], Kmat[:, kt * 128:(kt + 1) * 128], qi, True, False,
                   (R_K[kt], R_q[i]), (PB[sb_],), signal=False)
                if with_sel:
                    mm(ps[sb_], Et[:, kt * 128:(kt + 1) * 128], nsT.rearrange("p h k -> p (h k)"), False,
                       bt is None, (R_E, R_ns), (PB[sb_],), signal=(bt is None))
                if bt is not None:
                    mm(ps[sb_], ident, biasT[:, bt, :], False, True, (R_const,), (PB[sb_],))

            emit_s(0)
            if len(kts) > 1:
                emit_s(1)
            for n_, kt in enumerate(kts):
                if n_ + 2 < len(kts):
                    emit_s(n_ + 2)
                sb_ = (npt0 + n_) % 3
                pk = (npt0 + n_) % NP
                act(Pt[pk], ps[sb_], AF.Exp, (PB[sb_],), (PB[sb_], R_P[pk]), scale=sB)
                for h in range(4):
                    mm(accs[h // 2][:, h % 2, 0:129], Pt[pk][:, h * 128:(h + 1) * 128], Vmat[:, kt, 0:129],
                       (n_ == 0 and h % 2 == 0), n_ == len(kts) - 1, (R_P[pk], R_Vt[kt]),
                       (PB[accb[0]], PB[accb[1]]), signal=(h == 3 and n_ == len(kts) - 1))
            for b2 in range(2):
                rd_ = (PB[accb[b2]], R_fin, R_q[i], R_obuf[os_])
                wr_ = (PB[accb[b2]], R_fin, R_obuf[os_])
                acc = accs[b2]
                sc.op("dve", (lambda acc=acc: lambda e: e.reciprocal(fin[:, 0:2], acc[:, :, 128]))(), rd_, wr_)
                tt("dve", fin[:, 2:4], fin[:, 0:2], gts4[:, i, 2 * b2:2 * b2 + 2, gidx], ALU.mult, rd_, wr_)
                for hh in range(2):
                    h = 2 * b2 + hh
                    stt(obuf[os_][:, h, :], acc[:, hh, 0:128], fin[:, 2 + hh:3 + hh], obuf[os_][:, h, :],
                        ALU.mult, ALU.add, rd_, wr_)

        nsTs = [nsT, ar.alloc(BF16, 4, 128)]
        R_nss = [R_ns, Res("nsT1")]
        cstate = {}

        def cmp_a(i):
            nonlocal npt
            qi = qT[:, :, i * 128:(i + 1) * 128]
            nct = min(NCT, (16 * i + 14) // 128 + 1)
            cstate[i] = nct
            for ct in range(nct):
                sb_ = npt % 3
                npt += 1
                mm(ps[sb_], kcT[:, ct * 128:(ct + 1) * 128], qi, True, True, (R_kc, R_q[i]), (PB[sb_],))
                if 2048 * ct + 2063 <= 256 * i:
                    act(e_t[ct], ps[sb_], AF.Exp, (PB[sb_],), (PB[sb_], R_e[ct]), scale=sB)
                else:
                    act(ef, ps[sb_], AF.Exp, (PB[sb_],), (PB[sb_], R_e[ct]), scale=sB)
                    ts("dve", cm, Lt, p128[:, 0:1], float(2048 * ct - 256 * i), ALU.add, ALU.is_ge,
                       (R_const,), (R_e[ct],))
                    tt("dve", e_t[ct].rearrange("p (h k) -> p h k", k=128),
                       ef.rearrange("p (h k) -> p h k", k=128),
                       cm.unsqueeze(1).broadcast_to([128, 4, 128]), ALU.mult, (R_e[ct],), (R_e[ct],))
                mm(ps[7], ones_ff, e_t[ct], ct == 0, ct == nct - 1, (R_misc, R_e[ct]), (PB[7],),
                   signal=(ct == nct - 1))
            ts("dve", zr, ps[7], 1e-30, None, ALU.max, None, (PB[7],), (PB[7], R_z))
            sc.op("dve", lambda e: e.reciprocal(zr, zr), (R_z,), (R_z,))

        def cmp_b(i):
            nonlocal npc
            os_ = i % 2
            nct = cstate[i]
            for ct in range(nct):
                pc_ = npc % 2
                npc += 1
                tt("dve", pcn[pc_], e_t[ct], zr, ALU.mult, (R_e[ct], R_z), (R_pcn[pc_],))
                for h in range(4):
                    mm(ps[5][:, h * 128:(h + 1) * 128], pcn[pc_][:, h * 128:(h + 1) * 128], vcx[:, ct, :],
                       (ct == 0 and h == 0), ct == nct - 1, (R_pcn[pc_], R_vc), (PB[5],), signal=False)
                for h in range(4):
                    mm(ps[6][:, 0:128], pcn[pc_][:, h * 128:(h + 1) * 128], ovl[:, ct, :],
                       (ct == 0 and h == 0), (ct == nct - 1 and h == 3), (R_pcn[pc_], R_const), (PB[6], PB[5]),
                       signal=(h == 3))
            for h in range(4):
                ts("dve", obuf[os_][:, h, :], ps[5][:, h * 128:(h + 1) * 128], gts[:, i, 3 * h:3 * h + 1], None,
                   ALU.mult, None, (PB[5], R_q[i]), (PB[5], R_obuf[os_]))
            s0 = 4 * (NO - 1 - i)
            rt, wt = (PB[6], R_top, R_const), (PB[6], R_top)
            tt("dve", impa, ps[6][:, 0:128], keepT[:, s0:s0 + 128], ALU.mult, rt, wt)
            tt("dve", impa, impa, addT[:, s0:s0 + 128], ALU.add, rt, wt)
            memset("dve", impa[:, 0:1], 1.0e4, wt)
            sc.op("dve", lambda e: e.max(out=m8[:, 0:8], in_=impa), rt, wt)
            sc.op("dve", lambda e: e.match_replace(out=wk, in_to_replace=m8[:, 0:8], in_values=impa,
                                                   imm_value=-3.0e38), rt, wt)
            sc.op("dve", lambda e: e.max(out=m8[:, 8:16], in_=wk), rt, wt)
            ts("dve", thr, m8[:, 15:16], -5.0e29, None, ALU.max, None, rt, wt)
            ts("dve", negsel, impa, thr[:, 0:1], NEGB, ALU.is_lt, ALU.mult, rt, wt)

        def cmp_c(i):
            k = i % 2
            tr(psb[7][:, 0:128], negsel, ident, (R_top, R_const), (PB[7],))
            cp("act", nsTs[k], psb[7][:, 0:128].unsqueeze(1).broadcast_to([128, 4, 128]),
               (PB[7],), (PB[7], R_nss[k]))

        def tile_stream(i):
            nonlocal npt
            os_ = i % 2
            nsT_i = nsTs[i % 2]
            R_ns_i = R_nss[i % 2]
            qi = qT[:, :, i * 128:(i + 1) * 128]
            slc_k = list(range(2 * i + 2))
            win_k = [2 * i - 4 + r for r in range(6) if 2 * i - 4 + r >= 0]
            steps = [(0, kt) for kt in slc_k] + [(1, kt) for kt in win_k]
            cfg = [dict(K=ksT, V=Vs, accb=(3, 4), gidx=1, n=len(slc_k)),
                   dict(K=kwT, V=Vw, accb=(5, 6), gidx=2, n=len(win_k))]
            for c in cfg:
                c["accs"] = [ps[c["accb"][0]][:, 0:260].rearrange("p (m d) -> p m d", d=130),
                             ps[c["accb"][1]][:, 0:260].rearrange("p (m d) -> p m d", d=130)]
            npt0 = npt
            npt += len(steps)

            def bias_of(br, kt):
                if br == 0:
                    return (0 if kt == 2 * i else 1) if kt >= 2 * i else None
                return 2 + (kt - (2 * i - 4))

            def emit_s(n):
                br, kt = steps[n]
                c = cfg[br]
                sb_ = (npt0 + n) % 3
                bt = bias_of(br, kt)
                mm(ps[sb_], c["K"][:, kt * 128:(kt + 1) * 128], qi, True, False,
                   (R_K[kt], R_q[i]), (PB[sb_],), signal=False)
                if br == 0:
                    mm(ps[sb_], Et[:, kt * 128:(kt + 1) * 128], nsT_i.rearrange("p h k -> p (h k)"), False,
                       bt is None, (R_E, R_ns_i), (PB[sb_],), signal=(bt is None))
                if bt is not None:
                    mm(ps[sb_], ident, biasT[:, bt, :], False, True, (R_const,), (PB[sb_],))

            inject_at = max(0, len(slc_k) // 2)
            emit_s(0)
            emit_s(1)
            for n, (br, kt) in enumerate(steps):
                c = cfg[br]
                first = (n == 0) if br == 0 else (n == len(slc_k))
                last = (n == len(slc_k) - 1) if br == 0 else (n == len(steps) - 1)
                if n == inject_at and i + 1 < NO:
                    cmp_b(i + 1)
                if n + 2 < len(steps):
                    emit_s(n + 2)
                sb_ = (npt0 + n) % 3
                pk = (npt0 + n) % NP
                act(Pt[pk], ps[sb_], AF.Exp, (PB[sb_],), (PB[sb_], R_P[pk]), scale=sB)
                accb, accs = c["accb"], c["accs"]
                for h in range(4):
                    mm(accs[h // 2][:, h % 2, 0:129], Pt[pk][:, h * 128:(h + 1) * 128], c["V"][:, kt, 0:129],
                       (first and h % 2 == 0), last, (R_P[pk], R_Vt[kt]),
                       (PB[accb[0]], PB[accb[1]]), signal=(h == 3 and last))
                if not last:
                    continue
                for b2 in range(2):
                    rd_ = (PB[accb[b2]], R_fin, R_q[i], R_obuf[os_])
                    wr_ = (PB[accb[b2]], R_fin, R_obuf[os_])
                    acc = accs[b2]
                    sc.op("dve", (lambda acc=acc: lambda e: e.reciprocal(fin[:, 0:2], acc[:, :, 128]))(), rd_, wr_)
                    tt("dve", fin[:, 2:4], fin[:, 0:2], gts4[:, i, 2 * b2:2 * b2 + 2, c["gidx"]], ALU.mult, rd_, wr_)
                    for hh in range(2):
                        h = 2 * b2 + hh
                        stt(obuf[os_][:, h, :], acc[:, hh, 0:128], fin[:, 2 + hh:3 + hh], obuf[os_][:, h, :],
                            ALU.mult, ALU.add, rd_, wr_)

        cmp_a(0)
        cmp_b(0)
        cmp_c(0)
        for i in range(NO):
            os_ = i % 2
            if i + 1 < NO:
                cmp_a(i + 1)
            tile_stream(i)
            if i + 1 < NO:
                cmp_c(i + 1)
            dma("pool", ob_d[i][:, g * 512:(g + 1) * 512], obuf[os_].rearrange("p h d -> p (h d)"),
                (R_obuf[os_],), (R_ob,), R_obuf[os_])
        ar.release(pm)
        ar.release(um)
        sc.barrier()
        sc.recycle([R_E] + R_obuf)
    ar.release(bm)
    if debug and debug.get("stop") == "B":
        return finish(nc, sc, es, out)

    flush_casts("T")
    TBK = min(4, NO)
    NB = NO // TBK
    G_bc = ar.alloc(F32, D)
    fg_bc = ar.alloc(F32, D)
    R_tc = Res("tailconst")
    dma("sp", G_bc, G_d, (R_G,), (R_tc,), R_tc)
    dma("sp", fg_bc, fng.broadcast_to([128, D]), (), (R_tc,), R_tc)
    wbr_v = wbr_bf.rearrange("(kc p) c -> p kc c", p=128)
    wout_v = wout_bf.rearrange("(kc p) c -> p kc c", p=128)
    NWS = 6
    wsl = [ar.alloc(BF16, KC, 512) for _ in range(NWS)]
    R_ws = [Res(f"ws{k}") for k in range(NWS)]
    hTt, uT, xo = [], [], []
    for j in range(TBK):
        blk8 = ar.alloc(BF16, 2, KC, 128)
        hTt.append(blk8[:, 0])
        uT.append(blk8[:, 1])
        xo.append(blk8.rearrange("p a k c -> p (a k c)").bitcast(F32))
    ymT = [ar.alloc(BF16, KC, 128) for _ in range(TBK)]
    R_hTt = [Res(f"hTt{k}") for k in range(TBK)]
    R_uT = [Res(f"uT{k}") for k in range(TBK)]
    R_ymT = [Res(f"ymT{k}") for k in range(TBK)]
    R_xost = [Res(f"xost{k}") for k in range(TBK)]
    ar2 = Arena(arena_t, ARENA_BYTES)
    ar2.top = rope_lo
    NTS = 8
    n_in_rope = max(0, min(NTS, (rope_hi - rope_lo) // 2048))
    utmp = [ar.alloc(BF16, 512) for _ in range(2)]
    R_ut = [Res("ut0"), Res("ut1")]
    tsl = [ar2.alloc(F32, 512) for _ in range(n_in_rope)] + [ar.alloc(F32, 512) for _ in range(NTS - n_in_rope)]
    assert ar2.top <= rope_hi
    R_ts = [Res(f"ts{k}") for k in range(NTS)]
    ssf = ar.alloc(F32, 8)
    R_ss = Res("ssf")
    R_out = Res("out")
    nts = 0

    def tslot():
        nonlocal nts
        k = nts % NTS
        nts += 1
        return k

    mg0 = OFF["mgate"]
    wneeds = []
    for blk in range(NB):
        for cb in range(4):
            c0 = (OFF["az"] + 512 * cb) if cb < 2 else (OFF["bz"] + 512 * (cb - 2))
            wneeds.append([win_v[:, :, c0:c0 + 512]])
        for cb in range(4):
            wneeds.append([wbr_v[:, :, cb * 512:(cb + 1) * 512],
                           win_v[:, :, mg0 + cb * 512:mg0 + cb * 512 + 512],
                           win_v[:, :, mg0 + 2048 + cb * 512:mg0 + 2048 + cb * 512 + 512]])
        for cb in range(4):
            wneeds.append([wout_v[:, :, cb * 512:(cb + 1) * 512]])
    wslots = {}
    nws = 0

    def prefetch(idx):
        nonlocal nws
        if idx >= len(wneeds) or idx in wslots:
            return
        sl = []
        for src_ap in wneeds[idx]:
            k = nws % NWS
            nws += 1
            dma("sp", wsl[k], src_ap, (R_wcT,), (R_ws[k],), R_ws[k])
            sl.append(k)
        wslots[idx] = sl

    prefetch(0)
    widx = 0
    for blk in range(NB):
        for j in range(TBK):
            i = blk * TBK + j
            dma("sp", hTt[j], hTo_v[i], (R_hTd,), (R_hTt[j],), R_hTt[j])
        items = [(cb, j) for cb in range(4) for j in range(TBK)]
        info = {}

        def s1_a(n, blk=blk, items=items, info=info, widx=widx):
            cb, j = items[n]
            i = blk * TBK + j
            if j == 0:
                prefetch(widx + cb + 1)
            k = wslots[widx + cb][0]
            ko = tslot()
            info[n] = ko
            src = (oa_d if cb < 2 else ob_d)[i][:, (cb % 2) * 512:(cb % 2) * 512 + 512]
            dma("sp", tsl[ko], src, (R_oa, R_ob), (R_ts[ko],), R_ts[ko])
            bk = n % 2
            for kc in range(KC):
                mm(ps[bk], hTt[j][:, kc, :], wsl[k][:, kc, :], kc == 0, kc == KC - 1,
                   (R_hTt[j], R_ws[k]), (PB[bk],), signal=(kc == KC - 1))

        def s1_b(n, items=items, info=info):
            cb, j = items[n]
            bk, n2 = n % 2, n % 2
            ko = info[n]
            kz = tslot()
            act(tsl[kz], ps[bk], AF.Silu, (PB[bk],), (PB[bk], R_ts[kz]))
            tt("dve", utmp[n2], tsl[kz], tsl[ko], ALU.mult, (R_ts[kz], R_ts[ko]), (R_ut[n2],))

        def s1_c(n, items=items):
            cb, j = items[n]
            n2 = n % 2
            tb_ = 2 + n2
            for q in range(4):
                tr(psb[tb_][:, q * 128:(q + 1) * 128], utmp[n2][:, q * 128:(q + 1) * 128], ident,
                   (R_ut[n2], R_const), (PB[tb_],), signal=(q == 3))
            cp("act", uT[j][:, 4 * cb:4 * cb + 4, :],
               psb[tb_][:, 0:512].rearrange("p (q k) -> p q k", k=128), (PB[tb_],), (PB[tb_], R_uT[j]))

        s1_a(0)
        for n in range(len(items)):
            if n + 1 < len(items):
                s1_a(n + 1)
            s1_b(n)
            s1_c(n)
        widx += 4
        def s3_a(n, items=items, widx=widx):
            cb, j = items[n]
            if j == 0:
                prefetch(widx + cb + 1)
            kbr, kga, kgb = wslots[widx + cb]
            b4 = 4 * (n % 2)
            ba, bb, bc, bd = b4, b4 + 1, b4 + 2, b4 + 3
            for kc in range(8):
                mm(ps[ba], uT[j][:, kc, :], wsl[kbr][:, kc, :], kc == 0, kc == 7,
                   (R_uT[j], R_ws[kbr]), (PB[ba],), signal=(kc == 7))
            for kc in range(8, 16):
                mm(ps[bb], uT[j][:, kc, :], wsl[kbr][:, kc, :], kc == 8, kc == 15,
                   (R_uT[j], R_ws[kbr]), (PB[bb],), signal=(kc == 15))
            for kc in range(KC):
                mm(ps[bc], hTt[j][:, kc, :], wsl[kga][:, kc, :], kc == 0, kc == KC - 1,
                   (R_hTt[j], R_ws[kga]), (PB[bc],), signal=(kc == KC - 1))
            for kc in range(KC):
                mm(ps[bd], hTt[j][:, kc, :], wsl[kgb][:, kc, :], kc == 0, kc == KC - 1,
                   (R_hTt[j], R_ws[kgb]), (PB[bd],), signal=(kc == KC - 1))

        def s3_b(n, items=items):
            cb, j = items[n]
            b4 = 4 * (n % 2)
            n2 = n % 2
            ba, bb, bc, bd = b4, b4 + 1, b4 + 2, b4 + 3
            k1, k2_, k3_, k4_ = tslot(), tslot(), tslot(), tslot()
            act(tsl[k1], ps[bc], AF.Sigmoid, (PB[bc],), (PB[bc], R_ts[k1]))
            act(tsl[k2_], ps[bd], AF.Sigmoid, (PB[bd],), (PB[bd], R_ts[k2_]))
            tt("dve", tsl[k3_], tsl[k1], ps[ba], ALU.mult, (R_ts[k1], PB[ba]), (R_ts[k3_], PB[ba]))
            tt("dve", tsl[k4_], tsl[k2_], ps[bb], ALU.mult, (R_ts[k2_], PB[bb]), (R_ts[k4_], PB[bb]))
            tt("dve", utmp[n2], tsl[k3_], tsl[k4_], ALU.add, (R_ts[k3_], R_ts[k4_]), (R_ut[n2],))

        def s3_c(n, items=items):
            cb, j = items[n]
            n2 = n % 2
            ba = 4 * (n % 2)
            for q in range(4):
                tr(psb[ba][:, q * 128:(q + 1) * 128], utmp[n2][:, q * 128:(q + 1) * 128], ident,
                   (R_ut[n2], R_const), (PB[ba],), signal=(q == 3))
            cp("act", ymT[j][:, 4 * cb:4 * cb + 4, :],
               psb[ba][:, 0:512].rearrange("p (q k) -> p q k", k=128), (PB[ba],), (PB[ba], R_ymT[j]))

        s3_a(0)
        for n in range(len(items)):
            s3_b(n)
            if n + 1 < len(items):
                s3_a(n + 1)
            s3_c(n)
        widx += 4
        for j in range(TBK):
            i = blk * TBK + j
            dma("sp", xo[j], x_own[i * 128:(i + 1) * 128, :], (), (R_hTt[j], R_uT[j]), R_hTt[j])
        for cb in range(4):
            prefetch(widx + cb + 1)
            k = wslots[widx + cb][0]
            for j in range(TBK):
                bk = (cb * TBK + j) % 2
                for kc in range(KC):
                    mm(ps[bk], ymT[j][:, kc, :], wsl[k][:, kc, :], kc == 0, kc == KC - 1,
                       (R_ymT[j], R_ws[k]), (PB[bk],), signal=(kc == KC - 1))
                k1 = tslot()
                tt("dve", tsl[k1], ps[bk], G_bc[:, cb * 512:(cb + 1) * 512], ALU.mult,
                   (PB[bk], R_tc), (PB[bk], R_ts[k1]))
                tt("dve", xo[j][:, cb * 512:(cb + 1) * 512], tsl[k1], xo[j][:, cb * 512:(cb + 1) * 512],
                   ALU.add, (R_ts[k1], R_hTt[j], R_uT[j]), (R_hTt[j], R_uT[j]))
        widx += 4
        for j in range(TBK):
            i = blk * TBK + j
            rw_ = (R_hTt[j], R_uT[j])
            for q in range(4):
                k1 = tslot()
                act(tsl[k1], xo[j][:, q * 512:(q + 1) * 512], AF.Square, rw_, (R_ts[k1], R_ss),
                    accum_out=ssf[:, q:q + 1])
            rsum(ssf[:, 4:5], ssf[:, 0:4], (R_ss,), (R_ss,))
            act(ssf[:, 5:6], ssf[:, 4:5], AF.Sqrt, (R_ss, R_misc), (R_ss,), bias=epsc, scale=1.0 / D)
            sc.op("dve", lambda e: e.reciprocal(ssf[:, 6:7], ssf[:, 5:6]), (R_ss,), (R_ss,))
            stt(xo[j], xo[j], ssf[:, 6:7], fg_bc, ALU.mult, ALU.mult, rw_ + (R_ss, R_tc), rw_)
            dma("pool", out[i * 128:(i + 1) * 128, :], xo[j], rw_, (R_out,), R_xost[j])
    return finish(nc, sc, es, out)


def finish(nc, sc, es, out):
    sc.final_wait("pool")
    build_program.stats = {e: len(sc.q[e]) for e in sc.ENG}
    with nc.Block() as block:
        block.sync(sc.replay("sp"))
        block.tensor(sc.replay("pe"))
        block.scalar(sc.replay("act"))
        block.vector(sc.replay("dve"))
        block.gpsimd(sc.replay("pool"))
    es.close()
    return nc


def host_consts(S, p):
    NT = S // 128
    NO = NT // 2
    NCMP = (S - 32) // 16 + 1
    NCT = (NCMP + 127) // 128
    KW = 4 * NO - 4 + 128
    f = np.float32
    k = np.arange(128)
    cs = {}
    cs["c_ident"] = np.eye(128, dtype=f)
    cs["c_E"] = (k[:, None] == (np.arange(S)[None, :] // 64)).astype(f)
    c = (np.arange(NCT)[None, :, None] * 128 + k[:, None, None])
    n = k[None, None, :]
    cs["c_ovl"] = ((c >= 4 * n - 1) & (c <= 4 * n + 3) & (c < NCMP)).astype(f)
    cs["c_L"] = (k[None, :] - 16 * k[:, None] - 31).astype(f)
    m = np.arange(KW)[None, :]
    r = m - 4 * (NO - 1) - 2 * p
    hq = (k[:, None] >= 64).astype(np.int64)
    keep = (r < hq - 1).astype(f)
    add = np.where((r == hq - 1) | (r == hq), 1.0e4, np.where(r > hq, -1.0e30, 0.0)).astype(f)
    cs["c_keep"] = keep
    cs["c_add"] = add
    kk, qq = k[:, None], k[None, :]
    causal = np.where(kk <= qq, 0.0, NEGB).astype(f)
    lo = np.where(kk > qq, 0.0, NEGB).astype(f)
    allm = np.full((128, 128), NEGB, f)
    zero = np.zeros((128, 128), f)
    if p == 0:
        tabs = [causal, allm, lo, zero, zero, zero, causal, allm]
    else:
        tabs = [zero, causal, allm, lo, zero, zero, zero, causal]
    cs["c_bias"] = np.ascontiguousarray(
        np.stack([np.tile(t, (1, 4)) for t in tabs], axis=1)).astype(f)
    cs["c_pos_all"] = (128 * np.arange(NT)[None, :] + k[:, None]).astype(f)
    cs["c_pos_own"] = (128 * (2 * np.arange(NO)[None, :] + p) + k[:, None]).astype(f)
    cs["c_pos_cmp"] = (16 * (128 * np.arange(NCT)[None, :] + k[:, None]) + 31).astype(f)
    cs["c_p128"] = np.full((128, 1), 128.0 * p, f)
    return cs


def make_in_maps(inputs, S, batches):
    f = np.float32
    NT = S // 128
    g = {k_: np.asarray(v) for k_, v in inputs.items()}
    shared = {
        "w_ada": np.ascontiguousarray(g["w_ada"][0], f),
        "b_ada": np.ascontiguousarray(g["b_ada"][0][None, :], f),
        "norm_g": np.ascontiguousarray(g["norm_g"][0][None, :], f),
        "w_in": np.ascontiguousarray(g["w_in"][0], f),
        "lam4": np.ascontiguousarray(np.stack([g["lambda_q1"][0], g["lambda_k1"][0],
                                               g["lambda_q2"][0], g["lambda_k2"][0]]), f),
        "diff_norm_g": np.ascontiguousarray(g["diff_norm_g"][0][None, :], f),
        "peT": np.ascontiguousarray(np.stack([g["cmp_pe_k"][0].T, g["cmp_pe_v"][0].T], axis=1), f),
        "cmp_w1": np.ascontiguousarray(np.stack([g["cmp_w1_k"][0], g["cmp_w1_v"][0]]), f),
        "cmp_w2": np.ascontiguousarray(np.stack([g["cmp_w2_k"][0], g["cmp_w2_v"][0]]), f),
        "w_branch": np.ascontiguousarray(g["w_branch"][0], f),
        "w_out": np.ascontiguousarray(g["w_out"][0], f),
        "final_norm_g": np.ascontiguousarray(g["final_norm_g"][None, :], f),
    }
    consts = [host_consts(S, 0), host_consts(S, 1)]
    maps = []
    for b in batches:
        xb = np.ascontiguousarray(g["x"][b], f)
        for p in range(2):
            m = dict(shared)
            m.update(consts[p])
            m["x_all"] = xb
            m["x_own"] = np.ascontiguousarray(xb.reshape(NT, 128, D)[p::2].reshape(-1, D))
            m["cT"] = np.ascontiguousarray(g["c"][b].reshape(KC, 128).T, f)
            maps.append(m)
    return maps


def kernel(**inputs):
    S = 8192
    B = 4
    NT = S // 128
    nc = build_program(S)
    maps = make_in_maps(inputs, S, list(range(B)))
    res = run_bass_kernel_spmd(nc, maps, core_ids=list(range(8)))
    outp = np.empty((B, S, D), np.float32)
    for b in range(B):
        v = outp[b].reshape(NT, 128, D)
        for p in range(2):
            v[p::2] = np.asarray(res.results[2 * b + p]["out"], np.float32).reshape(NT // 2, 128, D)
    return outp
```
